# Optimizing a Trainium2 kernel written in Bass

```python
import math
import jax, jax.numpy as jnp
from jax import lax
import numpy as np

D_MODEL = 2048
BATCH = 2
SEQ = 8192
DEPTH = 4

N_A = DEPTH // 2
N_B = DEPTH - N_A
NORM_EPS = 1e-6

SSD_EXPAND = 2
D_INNER = SSD_EXPAND * D_MODEL
SSD_HEAD_DIM = 64
SSD_HEADS = D_INNER // SSD_HEAD_DIM
SSD_GROUPS = 8
SSD_STATE = 128
SSD_CONV = 4
SSD_CHUNK = 128
SSD_BC_DIM = SSD_GROUPS * SSD_STATE
SSD_CONV_DIM = D_INNER + 2 * SSD_BC_DIM
SSD_IN_DIM = D_INNER + SSD_CONV_DIM + SSD_HEADS
DT_MIN = 1e-3
DT_MAX = 1e-1

DIL_PATTERNS = ((128, 1), (512, 4), (2048, 16))
N_DIL = len(DIL_PATTERNS)
DIL_HEADS = 16
DIL_HEAD_DIM = D_MODEL // DIL_HEADS
DIL_WIDTH = DIL_HEADS * DIL_HEAD_DIM
DIL_BLOCK = 128
ROPE_THETA = 10000.0

MEM_LEN = 256
MEM_HEADS = 4
MEM_HEAD_DIM = 128
MEM_WIDTH = MEM_HEADS * MEM_HEAD_DIM

D_FF = -(-(8 * D_MODEL) // (3 * 256)) * 256

kernel_name = 'yoco_ssd_dilated_hybrid'


def rmsnorm(x, g):
    xf = x.astype(jnp.float32)
    y = xf * lax.rsqrt(jnp.mean(xf * xf, axis=-1, keepdims=True) + NORM_EPS)
    return (y * g.astype(jnp.float32)).astype(x.dtype)


def rope(t, positions):
    half = t.shape[-1] // 2
    inv_freq = ROPE_THETA ** (-jnp.arange(half, dtype=jnp.float32) / half)
    ang = positions.astype(jnp.float32)[:, :, None] * inv_freq
    cos = jnp.cos(ang)[:, :, None, :]
    sin = jnp.sin(ang)[:, :, None, :]
    t1 = t[..., :half].astype(jnp.float32)
    t2 = t[..., half:].astype(jnp.float32)
    return jnp.concatenate([t1 * cos - t2 * sin, t2 * cos + t1 * sin], axis=-1).astype(t.dtype)


def causal_depthwise_conv(u, w, bias):
    out = lax.conv_general_dilated(
        u, w[:, None, :].astype(u.dtype), window_strides=(1,),
        padding=[(w.shape[0] - 1, 0)], dimension_numbers=('NWC', 'WIO', 'NWC'),
        feature_group_count=u.shape[-1])
    return out + bias.astype(u.dtype)


def ssd_chunked(xh, dt, a_neg, bm, cm):
    b, s = xh.shape[0], xh.shape[1]
    c = s // SSD_CHUNK
    L = SSD_CHUNK
    G = SSD_GROUPS
    J = SSD_HEADS // G
    dtype = xh.dtype
    x = xh.reshape(b, c, L, G, J, SSD_HEAD_DIM)
    dtc = dt.reshape(b, c, L, G, J)
    bc = bm.reshape(b, c, L, G, SSD_STATE)
    cc = cm.reshape(b, c, L, G, SSD_STATE)
    a_cum = jnp.cumsum(dtc * a_neg.reshape(G, J), axis=2)
    causal = jnp.tril(jnp.ones((L, L), dtype=bool))[None, None, :, :, None, None]
    seg = a_cum[:, :, :, None] - a_cum[:, :, None, :]
    decay = jnp.exp(jnp.where(causal, seg, -jnp.inf))
    cb = jnp.einsum('bctgn,bcsgn->bctsg', cc, bc).astype(jnp.float32)
    m = cb[..., None] * decay * dtc[:, :, None]
    y_diag = jnp.einsum('bctsgj,bcsgjp->bctgjp', m.astype(dtype), x)
    w_end = jnp.exp(a_cum[:, :, -1:] - a_cum) * dtc
    states = jnp.einsum('bclgn,bclgj,bclgjp->bcgjpn', bc, w_end.astype(dtype), x)
    chunk_decay = jnp.exp(a_cum[:, :, -1])

    def step(h, inp):
        st, dec = inp
        return h * dec[..., None, None] + st, h

    h0 = jnp.zeros((b, G, J, SSD_HEAD_DIM, SSD_STATE), jnp.float32)
    _, h_in = lax.scan(step, h0, (jnp.moveaxis(states.astype(jnp.float32), 1, 0),
                                  jnp.moveaxis(chunk_decay, 1, 0)))
    h_in = jnp.moveaxis(h_in, 0, 1)
    y_off = jnp.einsum('bclgn,bcgjpn,bclgj->bclgjp', cc, h_in.astype(dtype),
                       jnp.exp(a_cum).astype(dtype))
    return (y_diag + y_off).reshape(b, s, SSD_HEADS, SSD_HEAD_DIM)


def ssd_mixer(u, w_in, conv_w, conv_b, dt_bias, a_log, d_skip, norm_w, w_out):
    b, s, _ = u.shape
    proj = u @ w_in
    z = proj[..., :D_INNER]
    xbc = proj[..., D_INNER:D_INNER + SSD_CONV_DIM]
    dt_raw = proj[..., D_INNER + SSD_CONV_DIM:]
    xbc = jax.nn.silu(causal_depthwise_conv(xbc, conv_w, conv_b))
    xs = xbc[..., :D_INNER].reshape(b, s, SSD_HEADS, SSD_HEAD_DIM)
    bm = xbc[..., D_INNER:D_INNER + SSD_BC_DIM].reshape(b, s, SSD_GROUPS, SSD_STATE)
    cm = xbc[..., D_INNER + SSD_BC_DIM:].reshape(b, s, SSD_GROUPS, SSD_STATE)
    dt = jax.nn.softplus(dt_raw.astype(jnp.float32) + dt_bias.astype(jnp.float32))
    a_neg = -jnp.exp(a_log.astype(jnp.float32))
    y = ssd_chunked(xs, dt, a_neg, bm, cm) + xs * d_skip[:, None].astype(xs.dtype)
    y = y.reshape(b, s, D_INNER) * jax.nn.silu(z)
    yg = y.reshape(b, s, SSD_GROUPS, D_INNER // SSD_GROUPS).astype(jnp.float32)
    yg = yg * lax.rsqrt(jnp.mean(yg * yg, axis=-1, keepdims=True) + NORM_EPS)
    y = (yg.reshape(b, s, D_INNER) * norm_w.astype(jnp.float32)).astype(u.dtype)
    return y @ w_out


def banded_window_attn(q, k, v, steps):
    bb, n, h, dh = q.shape
    nb = -(-n // DIL_BLOCK)
    pad = nb * DIL_BLOCK - n
    padw = ((0, 0), (0, pad), (0, 0), (0, 0))
    q = jnp.pad(q, padw)
    k = jnp.pad(k, padw)
    v = jnp.pad(v, padw)
    qb = q.reshape(bb, nb, DIL_BLOCK, h, dh)

    def with_prev(t):
        tb = t.reshape(bb, nb, DIL_BLOCK, h, dh)
        prev = jnp.concatenate([jnp.zeros_like(tb[:, :1]), tb[:, :-1]], axis=1)
        return jnp.concatenate([prev, tb], axis=2)

    kw = with_prev(k)
    vw = with_prev(v)
    sc = jnp.einsum('bnqhd,bnkhd->bnhqk', qb, kw).astype(jnp.float32) * (dh ** -0.5)
    qi = jnp.arange(DIL_BLOCK)[:, None] + DIL_BLOCK
    ki = jnp.arange(2 * DIL_BLOCK)[None, :]
    dist = qi - ki
    key_idx = jnp.arange(nb)[:, None, None] * DIL_BLOCK + ki[None] - DIL_BLOCK
    valid = (dist >= 0) & (dist <= steps) & (key_idx >= 0)
    sc = jnp.where(valid[None, :, None], sc, -jnp.inf)
    mx = jnp.max(sc, axis=-1, keepdims=True)
    p = jnp.exp(sc - mx)
    den = jnp.sum(p, axis=-1)
    o = jnp.einsum('bnhqk,bnkhd->bnqhd', p.astype(v.dtype), vw)
    o = o / jnp.swapaxes(den, 2, 3)[..., None].astype(o.dtype)
    lse = jnp.swapaxes(mx[..., 0] + jnp.log(den), 2, 3)
    o = o.reshape(bb, nb * DIL_BLOCK, h, dh)[:, :n]
    lse = lse.reshape(bb, nb * DIL_BLOCK, h)[:, :n]
    return o, lse


def dilated_group_attn(q, k, v, window, dilation):
    b, s, h, dh = q.shape
    n = s // dilation

    def to_sub(t):
        return t.reshape(b, n, dilation, h, dh).transpose(0, 2, 1, 3, 4).reshape(b * dilation, n, h, dh)

    o, lse = banded_window_attn(to_sub(q), to_sub(k), to_sub(v), window // dilation)
    o = o.reshape(b, dilation, n, h, dh).transpose(0, 2, 1, 3, 4).reshape(b, s, h, dh)
    lse = lse.reshape(b, dilation, n, h).transpose(0, 2, 1, 3).reshape(b, s, h)
    return o, lse


def dilated_mixer(u, k_all, v_all, positions, w_q, w_o):
    b, s, _ = u.shape
    q_all = rope((u @ w_q).reshape(b, s, N_DIL * DIL_HEADS, DIL_HEAD_DIM), positions)
    outs = []
    lses = []
    for g, (window, dilation) in enumerate(DIL_PATTERNS):
        sl = slice(g * DIL_HEADS, (g + 1) * DIL_HEADS)
        o, lse = dilated_group_attn(q_all[:, :, sl], k_all[:, :, sl], v_all[:, :, sl], window, dilation)
        outs.append(o)
        lses.append(lse)
    wts = jax.nn.softmax(jnp.stack(lses, axis=0), axis=0)
    o = jnp.einsum('gbsh,gbshd->bshd', wts.astype(q_all.dtype), jnp.stack(outs, axis=0))
    return o.reshape(b, s, DIL_WIDTH) @ w_o


def memory_cross_attn(u, mem_n, w_q, w_kv, w_o):
    b, s, _ = u.shape
    q = (u @ w_q).reshape(b, s, MEM_HEADS, MEM_HEAD_DIM)
    kv = mem_n @ w_kv
    k = kv[..., :MEM_WIDTH].reshape(b, MEM_LEN, MEM_HEADS, MEM_HEAD_DIM)
    v = kv[..., MEM_WIDTH:].reshape(b, MEM_LEN, MEM_HEADS, MEM_HEAD_DIM)
    sc = jnp.einsum('bshd,bmhd->bhsm', q, k).astype(jnp.float32) * (MEM_HEAD_DIM ** -0.5)
    p = jax.nn.softmax(sc, axis=-1).astype(v.dtype)
    o = jnp.einsum('bhsm,bmhd->bshd', p, v).reshape(b, s, MEM_WIDTH)
    return o @ w_o


def swiglu(u, w_in, w_out):
    gu = u @ w_in
    return (jax.nn.silu(gu[..., :D_FF]) * gu[..., D_FF:]) @ w_out


def setup_inputs(seed: int = 0) -> dict:
    key = jax.random.key(seed)
    ks = jax.random.split(key, 26)
    f32 = jnp.float32

    def dense(k, shape, fan_in):
        return jax.random.normal(k, shape, f32) * (fan_in ** -0.5)

    def gain(k, shape):
        return 1.0 + 0.02 * jax.random.normal(k, shape, f32)

    x = jax.random.normal(ks[0], (BATCH, SEQ, D_MODEL), f32)
    mem = jax.random.normal(ks[1], (BATCH, MEM_LEN, D_MODEL), f32)
    start = jax.random.randint(ks[2], (BATCH, 1), 0, 1024, dtype=jnp.int32)
    positions = start + jnp.arange(SEQ, dtype=jnp.int32)[None, :]
    norm_mix = gain(ks[3], (DEPTH, D_MODEL))
    norm_mem = gain(ks[4], (DEPTH, D_MODEL))
    norm_ffn = gain(ks[5], (DEPTH, D_MODEL))
    norm_final = gain(ks[6], (D_MODEL,))
    ssd_w_in = dense(ks[7], (N_A, D_MODEL, SSD_IN_DIM), D_MODEL)
    ssd_conv_w = dense(ks[8], (N_A, SSD_CONV, SSD_CONV_DIM), SSD_CONV)
    ssd_conv_b = 0.02 * jax.random.normal(ks[9], (N_A, SSD_CONV_DIM), f32)
    u = jax.random.uniform(ks[10], (N_A, SSD_HEADS), f32)
    dt0 = jnp.exp(u * (math.log(DT_MAX) - math.log(DT_MIN)) + math.log(DT_MIN))
    ssd_dt_bias = dt0 + jnp.log(-jnp.expm1(-dt0))
    ssd_a_log = jnp.log(jax.random.uniform(ks[11], (N_A, SSD_HEADS), f32, 1.0, 16.0))
    ssd_d = gain(ks[12], (N_A, SSD_HEADS))
    ssd_norm = gain(ks[13], (N_A, D_INNER))
    ssd_w_out = dense(ks[14], (N_A, D_INNER, D_MODEL), D_INNER)
    kv_norm = gain(ks[15], (D_MODEL,))
    w_kv_shared = dense(ks[16], (D_MODEL, 2 * N_DIL * DIL_WIDTH), D_MODEL)
    dil_w_q = dense(ks[17], (N_B, D_MODEL, N_DIL * DIL_WIDTH), D_MODEL)
    dil_w_o = dense(ks[18], (N_B, DIL_WIDTH, D_MODEL), DIL_WIDTH)
    mem_src_norm = gain(ks[19], (D_MODEL,))
    mem_w_q = dense(ks[20], (DEPTH, D_MODEL, MEM_WIDTH), D_MODEL)
    mem_w_kv = dense(ks[21], (DEPTH, D_MODEL, 2 * MEM_WIDTH), D_MODEL)
    mem_w_o = dense(ks[22], (DEPTH, MEM_WIDTH, D_MODEL), MEM_WIDTH)
    ffn_w_in = dense(ks[23], (DEPTH, D_MODEL, 2 * D_FF), D_MODEL)
    ffn_w_out = dense(ks[24], (DEPTH, D_FF, D_MODEL), D_FF)
    return {'x': x, 'mem': mem, 'positions': positions,
            'norm_mix': norm_mix, 'norm_mem': norm_mem, 'norm_ffn': norm_ffn, 'norm_final': norm_final,
            'ssd_w_in': ssd_w_in, 'ssd_conv_w': ssd_conv_w, 'ssd_conv_b': ssd_conv_b,
            'ssd_dt_bias': ssd_dt_bias, 'ssd_a_log': ssd_a_log, 'ssd_d': ssd_d,
            'ssd_norm': ssd_norm, 'ssd_w_out': ssd_w_out,
            'kv_norm': kv_norm, 'w_kv_shared': w_kv_shared, 'dil_w_q': dil_w_q, 'dil_w_o': dil_w_o,
            'mem_src_norm': mem_src_norm, 'mem_w_q': mem_w_q, 'mem_w_kv': mem_w_kv, 'mem_w_o': mem_w_o,
            'ffn_w_in': ffn_w_in, 'ffn_w_out': ffn_w_out}


def reference(x, mem, positions, norm_mix, norm_mem, norm_ffn, norm_final,
              ssd_w_in, ssd_conv_w, ssd_conv_b, ssd_dt_bias, ssd_a_log, ssd_d, ssd_norm, ssd_w_out,
              kv_norm, w_kv_shared, dil_w_q, dil_w_o,
              mem_src_norm, mem_w_q, mem_w_kv, mem_w_o, ffn_w_in, ffn_w_out):
    b, s, _ = x.shape
    mem_n = rmsnorm(mem, mem_src_norm)
    h = x
    k_sh = None
    v_sh = None
    for i in range(DEPTH):
        if i < N_A:
            h = h + ssd_mixer(rmsnorm(h, norm_mix[i]), ssd_w_in[i], ssd_conv_w[i], ssd_conv_b[i],
                              ssd_dt_bias[i], ssd_a_log[i], ssd_d[i], ssd_norm[i], ssd_w_out[i])
        else:
            if i == N_A:
                kv = rmsnorm(h, kv_norm) @ w_kv_shared
                k_sh = rope(kv[..., :N_DIL * DIL_WIDTH].reshape(b, s, N_DIL * DIL_HEADS, DIL_HEAD_DIM), positions)
                v_sh = kv[..., N_DIL * DIL_WIDTH:].reshape(b, s, N_DIL * DIL_HEADS, DIL_HEAD_DIM)
            j = i - N_A
            h = h + dilated_mixer(rmsnorm(h, norm_mix[i]), k_sh, v_sh, positions, dil_w_q[j], dil_w_o[j])
        h = h + memory_cross_attn(rmsnorm(h, norm_mem[i]), mem_n, mem_w_q[i], mem_w_kv[i], mem_w_o[i])
        h = h + swiglu(rmsnorm(h, norm_ffn[i]), ffn_w_in[i], ffn_w_out[i])
    return rmsnorm(h, norm_final)
```

```python
import math
from contextlib import ExitStack
import numpy as np
import concourse.bass as bass
import concourse.mybir as mybir
from concourse.bass_utils import run_bass_kernel_spmd

F32 = mybir.dt.float32
BF16 = mybir.dt.bfloat16
I32 = mybir.dt.int32
AF = mybir.ActivationFunctionType
ALU = mybir.AluOpType

ENG = ("pe", "act", "dve", "pool", "sp")

D = 2048
KC = 16
DI = 4096
NH = 64
NG = 8
DFF = 5632
EPS = 1e-6
DILS = (1, 4, 16)
TB = 512
PK = 8


class Buf:
    __slots__ = ("name", "w", "r")

    def __init__(self, name=""):
        self.name = name
        self.w = {}
        self.r = []


class Sched:
    NDMA = 8

    def __init__(self, nc, stack):
        self.nc = nc
        self.ops = {e: [] for e in ENG}
        self.cnt = {e: 0 for e in ENG}
        self.sem = {e: stack.enter_context(nc.semaphore("s_" + e)) for e in ENG}
        self.dsem = {e: [stack.enter_context(nc.semaphore("d_%s%d" % (e, i))) for i in range(self.NDMA)]
                     for e in ("sp", "pool", "act")}
        self.dcnt = {e: [0] * self.NDMA for e in self.dsem}
        self.drr = {e: 0 for e in self.dsem}
        self.seen = {e: {} for e in ENG}
        self.semobj = {}
        for e in ENG:
            self.semobj[("e", e)] = self.sem[e]
        for e in self.dsem:
            for i, s in enumerate(self.dsem[e]):
                self.semobj[("d", e, i)] = s

    def _deps(self, eng, reads, writes, acc=()):
        deps = {}

        def add(tok):
            k, v = tok
            if deps.get(k, 0) < v:
                deps[k] = v
        for b in reads:
            for t in b.w.items():
                add(t)
        for b in writes:
            for t in b.w.items():
                add(t)
            for t in b.r:
                add(t)
        for b in acc:
            for t in b.r:
                add(t)
        waits = []
        seen = self.seen[eng]
        for k, v in deps.items():
            if eng == "pe" and k == ("e", "pe"):
                continue
            if seen.get(k, 0) >= v:
                continue
            seen[k] = v
            waits.append((k, v))
        return waits

    def _mark(self, tok, reads, writes, acc=()):
        for b in reads:
            b.r = [t for t in b.r if t[0] != tok[0]]
            b.r.append(tok)
        for b in writes:
            b.w = {tok[0]: tok[1]}
            b.r = []
        for b in acc:
            b.w[tok[0]] = tok[1]
            b.r = []

    def op(self, eng, fn, reads=(), writes=(), acc=()):
        waits = self._deps(eng, reads, writes, acc)
        self.cnt[eng] += 1
        tok = (("e", eng), self.cnt[eng])
        self.ops[eng].append((waits, fn, (("e", eng), 1)))
        self._mark(tok, reads, writes, acc)
        return tok

    def dma(self, q, fn, reads=(), writes=(), acc=()):
        i = self.drr[q]
        self.drr[q] = (i + 1) % self.NDMA
        key = ("d", q, i)
        waits = self._deps(q, reads, writes, acc)
        prev = self.dcnt[q][i]
        if prev and self.seen[q].get(key, 0) < prev:
            self.seen[q][key] = prev
            waits.append((key, prev))
        self.dcnt[q][i] = prev + 16
        tok = (key, prev + 16)
        self.ops[q].append((waits, fn, (key, 16)))
        self._mark(tok, reads, writes, acc)
        return tok

    def barrier(self):
        toks = [(("e", e), self.cnt[e]) for e in ENG if self.cnt[e]]
        for q in self.dsem:
            for i in range(self.NDMA):
                if self.dcnt[q][i]:
                    toks.append((("d", q, i), self.dcnt[q][i]))
        for e in ENG:
            self.wait_all(e, toks)

    def wait_all(self, eng, toks):
        waits = []
        for k, v in toks:
            if eng == "pe" and k == ("e", "pe"):
                continue
            if self.seen[eng].get(k, 0) < v:
                self.seen[eng][k] = v
                waits.append((k, v))
        if waits:
            self.ops[eng].append((waits, None, None))

    def emit(self, block):
        engs = {"pe": block.tensor, "act": block.scalar, "dve": block.vector, "pool": block.gpsimd,
                "sp": block.sync}
        for e in ENG:
            ops = self.ops[e]
            if not ops:
                continue

            def body(engine, ops=ops):
                for waits, fn, inc in ops:
                    for k, v in waits:
                        engine.wait_ge(self.semobj[k], v)
                    if fn is not None:
                        ins = fn(engine)
                        ins.then_inc(self.semobj[inc[0]], inc[1])
            engs[e](body)


def bc(ap, shape):
    return ap.broadcast_to(list(shape))


class Prog:
    ARENA = 24600

    def __init__(self, T, plan):
        self.T = T
        self.NB = T // TB
        self.plan = plan
        nc = self.nc = bass.Bass("TRN2", target_bir_lowering=False)
        st = self.st = ExitStack()
        din = lambda n, s, dt=F32: nc.dram_tensor(n, list(s), dt, kind="ExternalInput").ap()
        dint = lambda n, s, dt=F32: nc.dram_tensor(n, list(s), dt, kind="Internal").ap()
        self.x = din("x", [T, D])
        self.mem = din("mem", [256, D])
        self.pos = din("pos", [T], I32)
        self.gains = din("gains", [128, 15, KC])
        self.convw = din("convw", [128, 2, 48, 4])
        self.convb = din("convb", [128, 2, 48])
        self.rowp = din("rowp", [2, 3, 64])
        self.ssdnw = din("ssdnw", [128, 2, 32])
        self.cst = din("cst", [128, 6, 128])
        self.invf = din("invf", [128, 2])
        self.ssd_w_in = din("ssd_w_in", [2, D, 10304])
        self.ssd_w_out = din("ssd_w_out", [2, DI, D])
        self.w_kv = din("w_kv_shared", [D, 12288])
        self.dil_w_q = din("dil_w_q", [2, D, 6144])
        self.dil_w_o = din("dil_w_o", [2, D, D])
        self.mem_w_q = din("mem_w_q", [4, D, 512])
        self.mem_w_kv = din("mem_w_kv", [4, D, 1024])
        self.mem_w_o = din("mem_w_o", [4, 512, D])
        self.ffn_w_in = din("ffn_w_in", [4, D, 2 * DFF])
        self.ffn_w_out = din("ffn_w_out", [4, DFF, D])
        self.y = nc.dram_tensor("y", [T, D], F32, kind="ExternalOutput").ap()
        self.hs = dint("hs", [KC, 128, T])
        self.cosd = dint("cosd", [128, T])
        self.sind = dint("sind", [128, T])
        self.memTd = dint("memTd", [128, KC, 256], BF16)
        self.KTs = dint("KTs", [48, 128, T], BF16)
        self.VTs = dint("VTs", [48, 128, T], BF16)
        self.QTs = dint("QTs", [48, 128, T], BF16)
        self.OTs = dint("OTs", [KC, 128, T], BF16)
        self.b_hs = [Buf() for _ in range(self.NB)]
        self.b_tab = Buf()
        self.b_memTd = Buf()
        self.b_KV = Buf()
        self.b_Q = Buf()
        self.b_O = Buf()
        self.b_y = Buf()

        S = self.S = Sched(nc, st)
        sb = self.sb = lambda n, s, dt: st.enter_context(nc.sbuf_tensor(n, list(s), dt))
        self.ps = [st.enter_context(nc.psum_tensor("ps%d" % i, [128, 512], F32)) for i in range(8)]
        self.bps = [Buf("ps%d" % i) for i in range(8)]
        self.pi = 0
        self.cstf = sb("cstf", [128, 6, 128], F32)
        self.cstb = sb("cstb", [128, 6, 128], BF16)
        self.onesb = sb("onesb", [128, 128], BF16)
        self.gn = sb("gn", [128, 15, KC], F32)
        self.cw = sb("cw", [128, 2, 48, 4], F32)
        self.cb = sb("cb", [128, 2, 48], F32)
        self.rp = sb("rp", [128, 2, 3, 64], F32)
        self.aneg = sb("aneg", [128, 2, 64], F32)
        self.snw = sb("snw", [128, 2, 32], F32)
        self.ivf = sb("ivf", [128, 2], F32)
        self.epsb = sb("epsb", [128, 1], F32)
        self.b_c = Buf("consts")
        self.hT = sb("hT", [128, KC, TB], F32)
        self.b_hT = Buf("hT")
        self.xn = sb("xn", [128, KC, TB], BF16)
        self.b_xn = Buf("xn")
        self.rstd = sb("rstd", [128, TB], F32)
        self.b_rstd = Buf("rstd")
        self.panels = [sb("pan%d" % i, [128, PK, 512], BF16) for i in range(2)]
        self.bpan = [Buf("pan%d" % i) for i in range(2)]
        self.pani = 0
        self.KmT = sb("KmT", [128, 4, 256], BF16)
        self.Vm = sb("Vm", [128, 2, 512], BF16)
        self.b_kvm = Buf()
        self.Hst = sb("Hst", [128, DI], F32)
        self.Hb = sb("Hb", [128, DI], BF16)
        self.b_H = [Buf() for _ in range(NG)]
        self.b_Hb = [Buf() for _ in range(NG)]
        self.halo = sb("halo", [128, 48, 3], F32)
        self.b_halo = Buf()
        self.arena = sb("arena", [128, self.ARENA], F32)
        self.b_ar = Buf("arena")

        self.build()
        with nc.Block() as block:
            S.emit(block)
        st.close()

    def bank(self):
        i = self.pi % 8
        self.pi += 1
        return i

    def carve(self, off_bytes, shape, dt, base="arena"):
        esz = 4 if dt in (F32, I32) else 2
        n = 1
        for s in shape[1:]:
            n *= s
        nw = (n * esz + 3) // 4
        assert off_bytes % 4 == 0
        if base == "arena":
            assert off_bytes + n * esz <= self.ARENA * 4, (off_bytes, shape)
            flat = self.arena[:, off_bytes // 4: off_bytes // 4 + nw]
        else:
            assert off_bytes + n * esz <= KC * TB * 2, (off_bytes, shape)
            flat = self.xn[:].rearrange("p a b -> p (a b)").bitcast(F32)[:, off_bytes // 4: off_bytes // 4 + nw]
        if dt != F32:
            flat = flat.bitcast(dt)
        flat = flat[:, 0:n]
        if len(shape) == 2:
            return flat
        if len(shape) == 3:
            return flat.rearrange("p (a b) -> p a b", b=shape[2])
        if len(shape) == 4:
            return flat.rearrange("p (a b c) -> p a b c", b=shape[2], c=shape[3])
        raise ValueError

    def mm(self, out, lhsT, rhs, start, stop, R, W):
        return self.S.op("pe", lambda e: e.matmul(out, lhsT=lhsT, rhs=rhs, start=start, stop=stop), reads=R, writes=W)

    def tr(self, out, in_, ident, R, W):
        return self.S.op("pe", lambda e: e.transpose(out, in_, ident), reads=R, writes=W)

    def act(self, out, in_, func, R, W, acc=(), **kw):
        return self.S.op("act", lambda e: e.activation(out=out, in_=in_, func=func, **kw), reads=R, writes=W, acc=acc)

    def amul(self, out, in_, mul, R, W, acc=()):
        return self.S.op("act", lambda e: e.mul(out=out, in_=in_, mul=mul), reads=R, writes=W, acc=acc)

    def tt(self, out, in0, in1, op, R, W, eng="dve", acc=()):
        return self.S.op(eng, lambda e: e.tensor_tensor(out=out, in0=in0, in1=in1, op=op), reads=R, writes=W, acc=acc)

    def ts(self, out, in0, s1, s2, op0, op1, R, W, eng="dve", acc=()):
        if s2 is None:
            return self.S.op(eng, lambda e: e.tensor_scalar(out=out, in0=in0, scalar1=s1, scalar2=None, op0=op0),
                             reads=R, writes=W, acc=acc)
        return self.S.op(eng, lambda e: e.tensor_scalar(out=out, in0=in0, scalar1=s1, scalar2=s2, op0=op0, op1=op1),
                         reads=R, writes=W, acc=acc)

    def stt(self, out, in0, scalar, in1, op0, op1, R, W, eng="dve", acc=()):
        return self.S.op(eng, lambda e: e.scalar_tensor_tensor(out=out, in0=in0, scalar=scalar, in1=in1, op0=op0, op1=op1),
                         reads=R, writes=W, acc=acc)

    def cp(self, out, in_, R, W, eng="dve", acc=()):
        return self.S.op(eng, lambda e: e.tensor_copy(out=out, in_=in_), reads=R, writes=W, acc=acc)

    def ld(self, out, in_, R, W, q="sp", acc=(), **kw):
        return self.S.dma(q, lambda e: e.dma_start(out=out, in_=in_, **kw), reads=R, writes=W, acc=acc)

    def panel(self, w2d, k0, nk, c0, ncols):
        i = self.pani % len(self.panels)
        self.pani += 1
        pan, b = self.panels[i], self.bpan[i]
        step = 4
        for kk in range(0, nk, step):
            n = min(step, nk - kk)
            src = w2d[(k0 + kk) * 128:(k0 + kk + n) * 128, c0:c0 + ncols].rearrange("(k p) n -> p k n", p=128)
            self.ld(pan[:, kk:kk + n, 0:ncols], src, [], [], q="pool", acc=[b])
        return pan, b

    def load_h(self, blk):
        self.ld(self.hT[:], self.hs[:, :, blk * TB:(blk + 1) * TB].rearrange("k p t -> p k t"),
                [self.b_hs[blk]], [self.b_hT])

    def store_h(self, blk):
        self.ld(self.hs[:, :, blk * TB:(blk + 1) * TB].rearrange("k p t -> p k t"), self.hT[:],
                [self.b_hT], [self.b_hs[blk]])

    def rms(self, gi, out=None, ob=None):
        out = self.xn if out is None else out
        ob = self.b_xn if ob is None else ob
        sq = self.carve(0, [128, KC, TB], BF16)
        bsq = Buf()
        for k in range(KC):
            self.act(sq[:, k, :], self.hT[:, k, :], AF.Square, [self.b_hT], [], acc=[bsq, self.b_ar])
        bk = self.bank()
        for k in range(KC):
            self.mm(self.ps[bk][:], self.onesb[:], sq[:, k, :], k == 0, k == KC - 1, [bsq, self.b_c], [self.bps[bk]])
        self.act(self.rstd[:], self.ps[bk][:], AF.Ln, [self.bps[bk], self.b_c], [self.b_rstd], scale=1.0 / D, bias=self.epsb[:])
        self.act(self.rstd[:], self.rstd[:], AF.Exp, [self.b_rstd], [self.b_rstd], scale=-0.5)
        for k in range(KC):
            self.stt(out[:, k, :], self.hT[:, k, :], self.gn[:, gi, k:k + 1], self.rstd[:], ALU.mult, ALU.mult,
                     [self.b_hT, self.b_rstd, self.b_c], [], acc=[ob])

    def linear_fm(self, w2d, c0, ncols, nk, rhs_of_k, R, consume, xcols=TB):
        for cb in range(0, ncols, 512):
            nc_ = min(512, ncols - cb)
            nm = nc_ // 128
            banks = [self.bank() for _ in range(nm)]
            for k0 in range(0, nk, PK):
                n = min(PK, nk - k0)
                pan, pb = self.panel(w2d, k0, n, c0 + cb, nc_)
                for m in range(nm):
                    for k in range(n):
                        self.mm(self.ps[banks[m]][:, 0:xcols], pan[:, k, m * 128:(m + 1) * 128], rhs_of_k(k0 + k),
                                (k0 + k) == 0, (k0 + k) == nk - 1, [pb] + R, [self.bps[banks[m]]])
            for m in range(nm):
                consume((cb // 128) + m, self.ps[banks[m]][:, 0:xcols], self.bps[banks[m]])

    def linear_tm(self, w2d, c0, ncols, nk, lhs_of, nch, R, consume):
        banks = [self.bank() for _ in range(nch)]
        for k0 in range(0, nk, PK):
            n = min(PK, nk - k0)
            pan, pb = self.panel(w2d, k0, n, c0, ncols)
            for c in range(nch):
                for k in range(n):
                    self.mm(self.ps[banks[c]][:, 0:ncols], lhs_of(k0 + k, c), pan[:, k, 0:ncols],
                            (k0 + k) == 0, (k0 + k) == nk - 1, [pb] + R, [self.bps[banks[c]]])
        for c in range(nch):
            consume(c, self.ps[banks[c]][:, 0:ncols], self.bps[banks[c]])

    def add_to_h(self, m, ps, pb):
        self.tt(self.hT[:, m, :], ps, self.hT[:, m, :], ALU.add, [pb, self.b_hT], [], acc=[self.b_hT])

    def setup(self):
        S = self.S
        self.ld(self.cstf[:], self.cst, [], [], acc=[self.b_c])
        self.ld(self.gn[:], self.gains, [], [], acc=[self.b_c])
        self.ld(self.cw[:], self.convw, [], [], acc=[self.b_c])
        self.ld(self.cb[:], self.convb, [], [], acc=[self.b_c])
        self.ld(self.snw[:], self.ssdnw, [], [], acc=[self.b_c])
        self.ld(self.ivf[:], self.invf, [], [], acc=[self.b_c])
        self.ld(self.rp[:].rearrange("p a b c -> p (a b c)"),
                self.rowp.rearrange("a b c -> (a b c)").partition_broadcast(128), [], [], acc=[self.b_c])
        S.barrier()
        self.cp(self.cstb[:], self.cstf[:], [self.b_c], [], acc=[self.b_c])
        S.op("dve", lambda e: e.memset(self.onesb[:], 1.0), [], [], acc=[self.b_c])
        S.op("dve", lambda e: e.memset(self.epsb[:], EPS), [], [], acc=[self.b_c])
        self.act(self.aneg[:], self.rp[:, :, 1, :], AF.Exp, [self.b_c], [], acc=[self.b_c])
        S.barrier()
        self.ts(self.aneg[:], self.aneg[:], -1.0, None, ALU.mult, None, [self.b_c], [], acc=[self.b_c])
        self.ident = self.cstf[:, 0, :]
        self.identb = self.cstb[:, 0, :]
        self.tri = self.cstf[:, 1, :]
        self.U = self.cstf[:, 2, :]
        self.sel127 = self.cstf[:, 3, :]
        self.swapb = self.cstb[:, 4, :]
        S.barrier()

    def prepass(self):
        NB = self.NB
        xt = self.carve(0, [128, 4, D], F32)
        A = [self.b_ar]
        for blk in range(NB):
            self.ld(xt, self.x[blk * TB:(blk + 1) * TB, :].rearrange("(c p) d -> p c d", p=128), [], A)
            for c in range(4):
                for k in range(KC):
                    bk = self.bank()
                    self.tr(self.ps[bk][:, 0:128], xt[:, c, k * 128:(k + 1) * 128], self.ident, A + [self.b_c], [self.bps[bk]])
                    if (c * KC + k) % 2 == 0:
                        self.act(self.hT[:, k, c * 128:(c + 1) * 128], self.ps[bk][:, 0:128], AF.Copy, [self.bps[bk]], [], acc=[self.b_hT])
                    else:
                        self.cp(self.hT[:, k, c * 128:(c + 1) * 128], self.ps[bk][:, 0:128], [self.bps[bk]], [], acc=[self.b_hT])
            self.store_h(blk)
            pi_ = self.carve(40960, [128, TB], I32)
            ang = self.carve(43008, [128, TB], F32)
            kf = self.carve(45056, [128, TB], F32)
            r = self.carve(47104, [128, TB], F32)
            sc = self.carve(49152, [128, 2, TB], F32)
            B = [Buf("rope")]
            self.ld(pi_, self.pos[blk * TB:(blk + 1) * TB].partition_broadcast(128), [], B)
            self.cp(ang, pi_, B, B)
            self.ts(ang, ang, self.ivf[:, 0:1], None, ALU.mult, None, B + [self.b_c], B)
            MAGIC = 12582912.0
            HI = 6.28125
            LO = float(2 * np.pi - 6.28125)
            for j, shift in enumerate((0.5 * np.pi, 0.0)):
                self.ts(kf, ang, float(shift), float(1 / (2 * np.pi)), ALU.add, ALU.mult, B, B)
                self.ts(kf, kf, MAGIC, None, ALU.add, None, B, B)
                self.ts(kf, kf, MAGIC, None, ALU.subtract, None, B, B)
                self.ts(r, ang, float(shift), None, ALU.add, None, B, B)
                self.stt(r, kf, -HI, r, ALU.mult, ALU.add, B, B)
                self.stt(r, kf, -LO, r, ALU.mult, ALU.add, B, B)
                self.ts(r, r, float(-np.pi), float(np.pi), ALU.max, ALU.min, B, B)
                self.act(sc[:, j, :], r, AF.Sin, B, B)
            self.ts(sc[:, 1, :], sc[:, 1, :], self.ivf[:, 1:2], None, ALU.mult, None, B + [self.b_c], B)
            self.ld(self.cosd[:, blk * TB:(blk + 1) * TB], sc[:, 0, :], B, [], acc=[self.b_tab])
            self.ld(self.sind[:, blk * TB:(blk + 1) * TB], sc[:, 1, :], B, [], acc=[self.b_tab])
            self.S.barrier()
        mt = self.carve(0, [128, 2, D], F32)
        junk = self.carve(16384, [128, D], F32)
        ssq = self.carve(24576, [128, 4], F32)
        memT = self.carve(32768, [128, KC, 256], BF16)
        self.ld(mt, self.mem.rearrange("(c p) d -> p c d", p=128), [], A)
        for c in range(2):
            self.S.op("dve", lambda e, c=c: e.memset(ssq[:, c:c + 1], 0.0), A, A)
            self.act(junk, mt[:, c, :], AF.Square, A, A, accum_out=ssq[:, c:c + 1])
        self.act(ssq[:, 2:4], ssq[:, 0:2], AF.Ln, A + [self.b_c], A, scale=1.0 / D, bias=self.epsb[:])
        self.act(ssq[:, 2:4], ssq[:, 2:4], AF.Exp, A, A, scale=-0.5)
        bm = Buf()
        for c in range(2):
            self.ts(mt[:, c, :], mt[:, c, :], ssq[:, 2 + c:3 + c], None, ALU.mult, None, A, A)
            for k in range(KC):
                bk = self.bank()
                self.tr(self.ps[bk][:, 0:128], mt[:, c, k * 128:(k + 1) * 128], self.ident, A + [self.b_c], [self.bps[bk]])
                self.ts(memT[:, k, c * 128:(c + 1) * 128], self.ps[bk][:, 0:128], self.gn[:, 13, k:k + 1], None,
                        ALU.mult, None, [self.bps[bk], self.b_c], [], acc=[bm])
        self.ld(self.memTd, memT, [bm], [self.b_memTd])
        self.S.barrier()

    def mem_kv(self, li):
        w = self.mem_w_kv[li]
        memT = self.carve(32768, [128, KC, 256], BF16)
        bm = Buf()
        self.ld(memT, self.memTd, [self.b_memTd], [bm])

        def consume(m, ps, pb):
            self.cp(self.KmT[:, m, :], ps, [pb], [], acc=[self.b_kvm])
        self.linear_fm(w, 0, 512, KC, lambda k: memT[:, k, :], [bm], consume, xcols=256)

        def cv(c, ps, pb):
            self.cp(self.Vm[:, c, :], ps, [pb], [], acc=[self.b_kvm])
        self.linear_tm(w, 512, 512, KC, lambda k, c: memT[:, k, c * 128:(c + 1) * 128], 2, [bm], cv)
        self.S.barrier()

    def mem_attn(self, li):
        self.rms(4 + li)
        self.S.barrier()
        qT = self.carve(0, [128, 4, TB], BF16)
        oT = self.carve(4096, [128, 4, TB], BF16)
        pT = [self.carve(8192 + i * 2048, [128, 2, TB], BF16) for i in range(2)]
        rden = [self.carve(12288 + i * 2048, [128, TB], F32) for i in range(2)]
        bq, bo_ = Buf(), Buf()
        bpp = [Buf(), Buf()]
        brr = [Buf(), Buf()]
        sc = 128 ** -0.5

        def cq(m, ps, pb):
            self.amul(qT[:, m, :], ps, sc, [pb], [], acc=[bq])
        self.linear_fm(self.mem_w_q[li], 0, 512, KC, lambda k: self.xn[:, k, :], [self.b_xn], cq)
        for hd in range(4):
            p_ = pT[hd % 2]
            bp = bpp[hd % 2]
            first = True
            for mc in range(2):
                bk = self.bank()
                self.mm(self.ps[bk][:], self.KmT[:, hd, mc * 128:(mc + 1) * 128], qT[:, hd, :], True, True,
                        [bq, self.b_kvm], [self.bps[bk]])
                self.act(p_[:, mc, :], self.ps[bk][:], AF.Exp, [self.bps[bk]], [bp] if mc == 0 else [], acc=[] if mc == 0 else [bp])
            bo, bd = self.bank(), self.bank()
            for mc in range(2):
                self.mm(self.ps[bo][:], self.Vm[:, mc, hd * 128:(hd + 1) * 128], p_[:, mc, :], mc == 0, mc == 1,
                        [bp, self.b_kvm], [self.bps[bo]])
            for mc in range(2):
                self.mm(self.ps[bd][:], self.onesb[:], p_[:, mc, :], mc == 0, mc == 1, [bp, self.b_c], [self.bps[bd]])
            rd = rden[hd % 2]
            brd = brr[hd % 2]
            self.S.op("dve", lambda e, bd=bd, rd=rd: e.reciprocal(out=rd, in_=self.ps[bd][:]), [self.bps[bd]], [brd])
            self.tt(oT[:, hd, :], self.ps[bo][:], rd, ALU.mult, [self.bps[bo], brd], [], acc=[bo_])
        self.linear_fm(self.mem_w_o[li], 0, D, 4, lambda k: oT[:, k, :], [bo_], self.add_to_h)
        self.S.barrier()

    def ffn(self, li):
        self.rms(8 + li)
        self.S.barrier()
        hid = self.carve(0, [128, 44, TB], BF16)
        sg = self.carve(45056, [128, 8, TB], BF16)
        bhid = Buf()
        bsg8 = [Buf() for _ in range(8)]
        w = self.ffn_w_in[li]
        for j in range(11):
            par = (j % 2) * 4
            bsg = bsg8[par:par + 4]

            def cg(m, ps, pb, bsg=bsg, par=par):
                self.act(sg[:, par + m % 4, :], ps, AF.Silu, [pb], [bsg[m % 4]])
            self.linear_fm(w, j * 512, 512, KC, lambda k: self.xn[:, k, :], [self.b_xn], cg)

            def cu(m, ps, pb, bsg=bsg, par=par, j=j):
                self.tt(hid[:, j * 4 + m % 4, :], ps, sg[:, par + m % 4, :], ALU.mult, [pb, bsg[m % 4]], [], acc=[bhid])
            self.linear_fm(w, DFF + j * 512, 512, KC, lambda k: self.xn[:, k, :], [self.b_xn], cu)
        self.linear_fm(self.ffn_w_out[li], 0, D, 44, lambda k: hid[:, k, :], [bhid], self.add_to_h)
        self.S.barrier()

    def ssd_block(self, li, blk):
        S = self.S
        C_ = [self.b_c]
        w = self.ssd_w_in[li]
        self.rms(li)
        S.barrier()
        xT = self.carve(0, [128, 32, TB], BF16)
        BT = self.carve(32768, [128, NG, TB], BF16)
        CT = self.carve(40960, [128, NG, TB], BF16)
        zs = self.carve(49152, [128, 4, DI], BF16)
        o = 81920
        ubuf = [self.carve(o + i * 2064, [128, 516], F32) for i in range(2)]
        o += 2 * 2064
        acc = [self.carve(o + i * 2048, [128, TB], F32) for i in range(2)]
        o += 2 * 2048
        small = self.carve(o, [128, 4, 8, 64], F32)
        o += 8192
        assert o <= self.ARENA * 4, o
        b_x, b_B, b_C, b_z, b_sm = Buf("xT"), Buf("BT"), Buf("CT"), Buf("zs"), Buf("small")
        if blk == 0:
            S.op("dve", lambda e: e.memset(self.halo[:], 0.0), [], [self.b_halo])
            S.op("dve", lambda e: e.memset(self.Hst[:], 0.0), [], self.b_H)
            S.op("dve", lambda e: e.memset(self.Hb[:], 0.0), [], self.b_Hb)
        ci = [0]
        bcv = [Buf(), Buf()]

        def cconv(m, ps, pb):
            i = ci[0] % 2
            ci[0] += 1
            u, a, b = ubuf[i], acc[i], bcv[i]
            self.cp(u[:, 0:3], self.halo[:, m, :], [self.b_halo], [b])
            self.act(u[:, 3:3 + TB], ps, AF.Copy, [pb], [], acc=[b])
            self.cp(self.halo[:, m, :], u[:, TB:TB + 3], [b], [], eng="pool", acc=[self.b_halo])
            cwv = self.cw[:, li, m, :]
            self.ts(a, u[:, 0:TB], cwv[:, 0:1], self.cb[:, li, m:m + 1], ALU.mult, ALU.add, [b] + C_, [], acc=[b])
            for j in range(1, 4):
                self.stt(a, u[:, j:j + TB], cwv[:, j:j + 1], a, ALU.mult, ALU.add, [b] + C_, [], acc=[b])
            if m < 32:
                self.act(xT[:, m, :], a, AF.Silu, [b], [], acc=[b_x])
            elif m < 40:
                self.act(BT[:, m - 32, :], a, AF.Silu, [b], [], acc=[b_B])
            else:
                self.act(CT[:, m - 40, :], a, AF.Silu, [b], [], acc=[b_C])
        self.linear_fm(w, DI, 6144, KC, lambda k: self.xn[:, k, :], [self.b_xn], cconv)
        for j in range(8):
            def cz(c, ps, pb, j=j):
                self.act(zs[:, c, j * 512:(j + 1) * 512], ps, AF.Silu, [pb], [], acc=[b_z])
            self.linear_tm(w, j * 512, 512, KC, lambda k, c: self.xn[:, k, c * 128:(c + 1) * 128], 4, [self.b_xn], cz)
        def cdt(c, ps, pb):
            dt, dtA, acum, ea, cd, wend, tmp = [small[:, c, i, :] for i in range(7)]
            M = [b_sm]
            self.tt(tmp, ps, self.rp[:, li, 0, :], ALU.add, [pb] + C_, M)
            self.act(tmp, tmp, AF.Exp, M, M)
            self.act(dt, tmp, AF.Ln, M, M, bias=1.0)
            self.tt(dtA, dt, self.aneg[:, li, :], ALU.mult, M + C_, M)
            bk = self.bank()
            self.mm(self.ps[bk][:, 0:64], self.tri, dtA, True, True, M + C_, [self.bps[bk]])
            self.cp(acum, self.ps[bk][:, 0:64], [self.bps[bk]], M)
            self.act(ea, acum, AF.Exp, M, M)
            bk = self.bank()
            self.mm(self.ps[bk][:, 0:64], self.sel127, acum, True, True, M + C_, [self.bps[bk]])
            self.act(cd, self.ps[bk][:, 0:64], AF.Exp, [self.bps[bk]], M)
            self.tt(tmp, self.ps[bk][:, 0:64], acum, ALU.subtract, [self.bps[bk]] + M, M)
            self.act(tmp, tmp, AF.Exp, M, M)
            self.tt(wend, tmp, dt, ALU.mult, M, M)
        self.linear_tm(w, 10240, 64, KC, lambda k, c: self.xn[:, k, c * 128:(c + 1) * 128], 4, [self.b_xn], cdt)
        S.barrier()
        Rg = self.carve(81920, [128, 8, 128], F32)
        dec = self.carve(81920 + 4096, [128, 8, 128], F32)
        o2 = 0
        def xc(shape, dt):
            nonlocal o2
            n = 1
            for s_ in shape[1:]:
                n *= s_
            v = self.carve(o2, shape, dt, base="xn")
            o2 += ((n * (4 if dt == F32 else 2) + 3) // 4) * 4
            return v
        CBm = xc([128, 128], F32)
        MT = xc([128, 8, 128], BF16)
        xtok = xc([128, 512], BF16)
        xw = xc([128, 512], BF16)
        Btok = xc([128, 128], BF16)
        t1 = xc([128, 512], F32)
        t2 = xc([128, 512], F32)
        t3 = xc([128, 512], F32)
        yn = xc([128, 512], BF16)
        ssq = xc([128, 4], F32)
        bR, bD, bCB, bM, bX, bY, bS = [Buf() for _ in range(7)]
        v3 = lambda ap: ap.rearrange("p (a b) -> p a b", b=64)
        for c in range(4):
            cs = slice(c * 128, (c + 1) * 128)
            dt, dtA, acum, ea, cd, wend, tmp = [small[:, c, i, :] for i in range(7)]
            for g in range(NG):
                hs_ = slice(g * 8, (g + 1) * 8)
                gs = slice(g * 512, (g + 1) * 512)
                self.tt(Rg, bc(self.tri.unsqueeze(1), [128, 8, 128]), bc(dtA[:, hs_].unsqueeze(2), [128, 8, 128]), ALU.mult,
                        [b_sm] + C_, [bR])
                for hh in range(2):
                    bk = self.bank()
                    self.mm(self.ps[bk][:], self.U, Rg[:, hh * 4:(hh + 1) * 4, :], True, True, [bR] + C_, [self.bps[bk]])
                    self.act(dec[:, hh * 4:(hh + 1) * 4, :], self.ps[bk][:].rearrange("p (a b) -> p a b", b=128), AF.Exp,
                             [self.bps[bk]], [], acc=[bD])
                bk = self.bank()
                self.mm(self.ps[bk][:, 0:128], BT[:, g, cs], CT[:, g, cs], True, True, [b_B, b_C], [self.bps[bk]])
                self.tt(CBm, self.ps[bk][:, 0:128], self.tri, ALU.mult, [self.bps[bk]] + C_, [bCB])
                self.tt(dec, dec, bc(dt[:, hs_].unsqueeze(2), [128, 8, 128]), ALU.mult, [bD, b_sm], [bD], eng="pool")
                self.tt(MT, dec, bc(CBm.unsqueeze(1), [128, 8, 128]), ALU.mult, [bD, bCB], [bM])
                bk = self.bank()
                pb16 = self.ps[bk][:].bitcast(BF16)
                for q in range(4):
                    self.tr(pb16[:, q * 128:(q + 1) * 128], xT[:, g * 4 + q, cs], self.identb, [b_x] + C_, [self.bps[bk]])
                self.cp(xtok, pb16[:, 0:512], [self.bps[bk]], [bX])
                by = self.bank()
                for j in range(8):
                    self.mm(self.ps[by][:, j * 64:(j + 1) * 64], MT[:, j, :], xtok[:, j * 64:(j + 1) * 64], True, True,
                            [bM, bX], [self.bps[by]])
                bo = self.bank()
                self.mm(self.ps[bo][:], CT[:, g, cs], self.Hb[:, gs], True, True, [b_C, self.b_Hb[g]], [self.bps[bo]])
                self.tt(v3(t1), v3(self.ps[bo][:]), bc(ea[:, hs_].unsqueeze(2), [128, 8, 64]), ALU.mult,
                        [self.bps[bo], b_sm], [bY])
                self.tt(t2, self.ps[by][:], t1, ALU.add, [self.bps[by], bY], [bY])
                self.tt(v3(t3), v3(xtok), bc(self.rp[:, li, 2, hs_].unsqueeze(2), [128, 8, 64]), ALU.mult, [bX] + C_, [bS], eng="pool")
                self.tt(t2, t2, t3, ALU.add, [bY, bS], [bY])
                self.tt(t2, t2, zs[:, c, gs], ALU.mult, [bY, b_z], [bY])
                S.op("dve", lambda e: e.memset(ssq[:, 0:1], 0.0), [bY], [bY])
                self.act(t1, t2, AF.Square, [bY], [bY], accum_out=ssq[:, 0:1])
                self.act(ssq[:, 1:2], ssq[:, 0:1], AF.Ln, [bY] + C_, [bY], scale=1.0 / 512, bias=self.epsb[:])
                self.act(ssq[:, 1:2], ssq[:, 1:2], AF.Exp, [bY], [bY], scale=-0.5)
                self.ts(yn, t2, ssq[:, 1:2], None, ALU.mult, None, [bY], [bY])
                bk = self.bank()
                pb16 = self.ps[bk][:].bitcast(BF16)
                self.tr(pb16[:, 0:128], BT[:, g, cs], self.identb, [b_B] + C_, [self.bps[bk]])
                self.cp(Btok, pb16[:, 0:128], [self.bps[bk]], [bS])
                self.tt(v3(xw), v3(xtok), bc(wend[:, hs_].unsqueeze(2), [128, 8, 64]), ALU.mult, [bX, b_sm, bS], [bS], eng="pool")
                bs = self.bank()
                self.mm(self.ps[bs][:], Btok, xw, True, True, [bS], [self.bps[bs]])
                self.tt(v3(self.Hst[:, gs]), v3(self.Hst[:, gs]), bc(cd[:, hs_].unsqueeze(2), [128, 8, 64]), ALU.mult,
                        [self.b_H[g], b_sm], [self.b_H[g]])
                self.tt(self.Hst[:, gs], self.Hst[:, gs], self.ps[bs][:], ALU.add, [self.b_H[g], self.bps[bs]], [self.b_H[g]])
                self.act(self.Hb[:, gs], self.Hst[:, gs], AF.Copy, [self.b_H[g]], [self.b_Hb[g]])
                bk = self.bank()
                pb16 = self.ps[bk][:].bitcast(BF16)
                for q in range(4):
                    self.tr(pb16[:, q * 128:(q + 1) * 128], yn[:, q * 128:(q + 1) * 128], self.identb, [bY] + C_, [self.bps[bk]])
                for q in range(4):
                    self.ts(xT[:, g * 4 + q, cs], pb16[:, q * 128:(q + 1) * 128], self.snw[:, li, g * 4 + q:g * 4 + q + 1], None,
                            ALU.mult, None, [self.bps[bk], bX] + C_, [], acc=[b_x])
        S.barrier()
        self.linear_fm(self.ssd_w_out[li], 0, D, 32, lambda k: xT[:, k, :], [b_x], self.add_to_h)
        S.barrier()

    def rope_chunk(self, ps, pb, scale, cs_t, btab):
        i = self.rsi % 2
        self.rsi += 1
        qs = self.carve(49152 + i * 1024, [128, TB], BF16)
        ta = self.carve(51200 + i * 2048, [128, TB], F32)
        tb_ = self.carve(55296 + i * 2048, [128, TB], F32)
        ob = self.carve(59392 + i * 1024, [128, TB], BF16)
        b = self.rbuf[i]
        self.amul(qs, ps, scale, [pb], [b])
        bk = self.bank()
        self.mm(self.ps[bk][:], self.swapb, qs, True, True, [b, self.b_c], [self.bps[bk]])
        self.tt(ta, qs, cs_t[:, 0, :], ALU.mult, [b, btab], [], eng="pool", acc=[b])
        self.tt(tb_, self.ps[bk][:], cs_t[:, 1, :], ALU.mult, [self.bps[bk], btab], [], acc=[b])
        self.tt(ob, ta, tb_, ALU.add, [b], [], acc=[b])
        return ob, b

    def load_tabs(self, blk):
        cs_t = self.carve(40960, [128, 2, TB], F32)
        btab = Buf()
        self.ld(cs_t[:, 0, :], self.cosd[:, blk * TB:(blk + 1) * TB], [self.b_tab], [], acc=[btab])
        self.ld(cs_t[:, 1, :], self.sind[:, blk * TB:(blk + 1) * TB], [self.b_tab], [], acc=[btab])
        return cs_t, btab

    def kv_block(self, blk):
        self.rms(12)
        self.S.barrier()
        cs_t, btab = self.load_tabs(blk)
        self.rsi = 0
        self.rbuf = [Buf(), Buf()]
        cols = slice(blk * TB, (blk + 1) * TB)

        def ck(m, ps, pb):
            ob, b = self.rope_chunk(ps, pb, 1.0, cs_t, btab)
            self.ld(self.KTs[m, :, cols], ob, [b], [], acc=[self.b_KV])
        self.linear_fm(self.w_kv, 0, 6144, KC, lambda k: self.xn[:, k, :], [self.b_xn], ck)

        def cv(m, ps, pb):
            i = self.rsi % 2
            self.rsi += 1
            ob = self.carve(59392 + i * 1024, [128, TB], BF16)
            b = self.rbuf[i]
            self.act(ob, ps, AF.Copy, [pb], [b])
            self.ld(self.VTs[m, :, cols], ob, [b], [], acc=[self.b_KV])
        self.linear_fm(self.w_kv, 6144, 6144, KC, lambda k: self.xn[:, k, :], [self.b_xn], cv)
        self.S.barrier()

    def q_block(self, li, blk):
        self.rms(li)
        self.S.barrier()
        cs_t, btab = self.load_tabs(blk)
        self.rsi = 0
        self.rbuf = [Buf(), Buf()]
        cols = slice(blk * TB, (blk + 1) * TB)

        def cq(m, ps, pb):
            ob, b = self.rope_chunk(ps, pb, 128 ** -0.5, cs_t, btab)
            self.ld(self.QTs[m, :, cols], ob, [b], [], acc=[self.b_Q])
        self.linear_fm(self.dil_w_q[li - 2], 0, 6144, KC, lambda k: self.xn[:, k, :], [self.b_xn], cq)
        self.S.barrier()

    def dil_attn(self):
        T = self.T
        QS = 2048
        C_ = [self.b_c]
        mask2 = self.carve(0, [128, 2, 128], BF16)
        bmk = Buf()
        self.cp(mask2[:, 0, :], self.cstf[:, 5, :], C_, [], acc=[bmk])
        self.cp(mask2[:, 1, :], self.cstf[:, 1, :], C_, [], acc=[bmk])
        mflat = mask2[:].rearrange("p a b -> p (a b)")
        accb = self.carve(1024, [128, 2, QS], F32)
        oTb = self.carve(17408, [128, QS], BF16)
        rden = self.carve(21504, [128, QS], F32)
        Pm = [self.carve(29696 + i * 512, [128, 256], BF16) for i in range(2)]
        Pf = [self.carve(30720 + i * 1024, [128, 256], F32) for i in range(2)]
        o = 32768
        KT = self.carve(o, [128, 4096], BF16); o += 8192
        VT = self.carve(o, [128, 4096], BF16); o += 8192
        qT = self.carve(o, [128, QS], BF16); o += 4096
        Vtok = self.carve(o, [128, 32, 128], BF16); o += 8192
        assert o <= self.ARENA * 4
        bL = Buf("kvq")
        bV = Buf("vtok")
        bacc = Buf("acc")
        bP = [Buf(), Buf()]
        pcount = 0
        for hh in range(16):
            for sg in range(T // QS):
                q0 = sg * QS
                for g, d in enumerate(DILS):
                    head = g * 16 + hh
                    halo = 128 * d
                    w0 = max(0, q0 - halo)
                    wl = q0 + QS - w0
                    self.ld(KT[:, 0:wl], self.KTs[head, :, w0:q0 + QS], [self.b_KV], [bL])
                    self.ld(VT[:, 0:wl], self.VTs[head, :, w0:q0 + QS], [self.b_KV], [], acc=[bL])
                    self.ld(qT[:, :], self.QTs[head, :, q0:q0 + QS], [self.b_Q], [], acc=[bL])
                    nbi = QS // (128 * d)
                    bi0 = q0 // (128 * d)
                    kb_lo = max(0, bi0 - 1)

                    def kcols(r, bi, d=d, w0=w0):
                        s = r + d * 128 * bi - w0
                        return slice(s, s + 127 * d + 1, d)
                    vidx = {}
                    n = 0
                    first = True
                    for r in range(d):
                        for bi in range(kb_lo, bi0 + nbi):
                            vidx[(r, bi)] = n
                            bk = self.bank()
                            pb16 = self.ps[bk][:].bitcast(BF16)
                            self.tr(pb16[:, 0:128], VT[:, kcols(r, bi)], self.identb, [bL] + C_, [self.bps[bk]])
                            W_, Acc = ([bV], []) if first else ([], [bV])
                            first = False
                            if n % 2 == 0:
                                self.cp(Vtok[:, n, :], pb16[:, 0:128], [self.bps[bk]], W_, acc=Acc)
                            else:
                                self.act(Vtok[:, n, :], pb16[:, 0:128], AF.Copy, [self.bps[bk]], W_, acc=Acc)
                            n += 1
                    for r in range(d):
                        for bi in range(bi0, bi0 + nbi):
                            qa = r + d * 128 * bi - q0
                            qsl = slice(qa, qa + 127 * d + 1, d)
                            has_prev = bi >= 1
                            bk = self.bank()
                            if has_prev:
                                self.mm(self.ps[bk][:, 0:128], KT[:, kcols(r, bi - 1)], qT[:, qsl], True, True, [bL], [self.bps[bk]])
                            self.mm(self.ps[bk][:, 128:256], KT[:, kcols(r, bi)], qT[:, qsl], True, True, [bL], [self.bps[bk]])
                            lo = 0 if has_prev else 128
                            pf, pm, bp = Pf[pcount % 2], Pm[pcount % 2], bP[pcount % 2]
                            pcount += 1
                            self.act(pf[:, lo:256], self.ps[bk][:, lo:256], AF.Exp, [self.bps[bk]], [bp])
                            self.tt(pm[:, lo:256], pf[:, lo:256], mflat[:, lo:256], ALU.mult,
                                    [bmk], [bp], eng=("dve" if pcount % 2 else "pool"))
                            bo = self.bank()
                            if has_prev:
                                self.mm(self.ps[bo][:, 0:128], Vtok[:, vidx[(r, bi - 1)], :], pm[:, 0:128], True, False, [bp, bV], [self.bps[bo]])
                            self.mm(self.ps[bo][:, 0:128], Vtok[:, vidx[(r, bi)], :], pm[:, 128:256], not has_prev, True, [bp, bV], [self.bps[bo]])
                            if has_prev:
                                self.mm(self.ps[bo][:, 128:256], self.onesb[:], pm[:, 0:128], True, False, [bp] + C_, [self.bps[bo]])
                            self.mm(self.ps[bo][:, 128:256], self.onesb[:], pm[:, 128:256], not has_prev, True, [bp] + C_, [self.bps[bo]])
                            src = self.ps[bo][:, 0:256].rearrange("p (a b) -> p a b", b=128)
                            dst = accb[:, :, qsl]
                            if g == 0:
                                self.cp(dst, src, [self.bps[bo]], [bacc])
                            else:
                                self.tt(dst, src, dst, ALU.add, [self.bps[bo], bacc], [bacc])
                self.S.op("dve", lambda e: e.reciprocal(out=rden, in_=accb[:, 1, :]), [bacc], [bacc])
                self.tt(oTb, accb[:, 0, :], rden, ALU.mult, [bacc], [bacc])
                self.ld(self.OTs[hh, :, q0:q0 + QS], oTb, [bacc], [], acc=[self.b_O])
        self.S.barrier()

    def final_block(self, blk):
        xf = self.carve(16384, [128, KC, TB], F32)
        yt = self.carve(49152, [128, 4, D], F32)
        bxf, byt = Buf(), Buf()
        self.rms(14, out=xf, ob=bxf)
        for c in range(4):
            for k in range(KC):
                bk = self.bank()
                self.tr(self.ps[bk][:, 0:128], xf[:, k, c * 128:(c + 1) * 128], self.ident, [bxf, self.b_c], [self.bps[bk]])
                if k % 2 == 0:
                    self.cp(yt[:, c, k * 128:(k + 1) * 128], self.ps[bk][:, 0:128], [self.bps[bk]], [], acc=[byt])
                else:
                    self.act(yt[:, c, k * 128:(k + 1) * 128], self.ps[bk][:, 0:128], AF.Copy, [self.bps[bk]], [], acc=[byt])
        t = self.ld(self.y[blk * TB:(blk + 1) * TB, :].rearrange("(c p) d -> p c d", p=128), yt, [byt], [], acc=[self.b_y])
        self.out_toks.append(t)
        self.S.barrier()

    def build(self):
        self.out_toks = []
        self.setup()
        self.prepass()
        for stage in self.plan:
            kind = stage[0]
            if kind == "ssd":
                li = stage[1]
                self.mem_kv(li)
                for blk in range(self.NB):
                    self.load_h(blk)
                    self.ssd_block(li, blk)
                    if "nomem" not in stage:
                        self.mem_attn(li)
                    if "noffn" not in stage:
                        self.ffn(li)
                    self.store_h(blk)
            elif kind == "kv":
                for blk in range(self.NB):
                    self.load_h(blk)
                    self.kv_block(blk)
            elif kind == "dil":
                li = stage[1]
                self.mem_kv(li)
                for blk in range(self.NB):
                    self.load_h(blk)
                    self.q_block(li, blk)
                self.dil_attn()
                for blk in range(self.NB):
                    self.load_h(blk)
                    self.ld(self.xn[:], self.OTs[:, :, blk * TB:(blk + 1) * TB].rearrange("k p t -> p k t"),
                            [self.b_O], [self.b_xn])
                    self.linear_fm(self.dil_w_o[li - 2], 0, D, KC, lambda k: self.xn[:, k, :], [self.b_xn], self.add_to_h)
                    self.S.barrier()
                    self.mem_attn(li)
                    self.ffn(li)
                    self.store_h(blk)
            elif kind == "ffn":
                li = stage[1]
                for blk in range(self.NB):
                    self.load_h(blk)
                    self.ffn(li)
                    self.store_h(blk)
            elif kind == "mem":
                li = stage[1]
                self.mem_kv(li)
                for blk in range(self.NB):
                    self.load_h(blk)
                    self.mem_attn(li)
                    self.store_h(blk)
        for blk in range(self.NB):
            self.load_h(blk)
            self.final_block(blk)
        self.S.wait_all("sp", self.out_toks)


FULL_PLAN = [("ssd", 0), ("ssd", 1), ("kv",), ("dil", 2), ("dil", 3)]


def host_params(inp):
    f = np.float32
    fm = lambda v: np.ascontiguousarray(np.asarray(v, f).reshape(KC, 128).T)
    gl = [fm(inp["norm_mix"][i]) for i in range(4)] + [fm(inp["norm_mem"][i]) for i in range(4)] + \
         [fm(inp["norm_ffn"][i]) for i in range(4)] + [fm(inp["kv_norm"]), fm(inp["mem_src_norm"]), fm(inp["norm_final"])]
    gains = np.ascontiguousarray(np.stack(gl, axis=1))
    cw = np.asarray(inp["ssd_conv_w"], f)
    convw = np.ascontiguousarray(cw.reshape(2, 4, 48, 128).transpose(3, 0, 2, 1))
    convb = np.ascontiguousarray(np.asarray(inp["ssd_conv_b"], f).reshape(2, 48, 128).transpose(2, 0, 1))
    rowp = np.ascontiguousarray(np.stack([inp["ssd_dt_bias"], inp["ssd_a_log"], inp["ssd_d"]], axis=1).astype(f))
    ssdnw = np.ascontiguousarray(np.asarray(inp["ssd_norm"], f).reshape(2, 32, 128).transpose(2, 0, 1))
    i = np.arange(128)
    s, t = i[:, None], i[None, :]
    cst = np.stack([np.eye(128), (s <= t), (s > t), (s == 127) * np.ones((128, 128)),
                    (s == (t + 64) % 128), (s >= t)], axis=1).astype(f)
    half = 64
    invf = (10000.0 ** (-np.arange(half, dtype=np.float32) / half)).astype(f)
    invf2 = np.stack([np.concatenate([invf, invf]), np.concatenate([-np.ones(half, f), np.ones(half, f)])], axis=1)
    return dict(gains=gains, convw=convw, convb=convb, rowp=rowp, ssdnw=ssdnw, cst=np.ascontiguousarray(cst),
                invf=np.ascontiguousarray(invf2.astype(f)))


WNAMES = {"ssd_w_in": "ssd_w_in", "ssd_w_out": "ssd_w_out", "w_kv_shared": "w_kv_shared", "dil_w_q": "dil_w_q",
          "dil_w_o": "dil_w_o", "mem_w_q": "mem_w_q", "mem_w_kv": "mem_w_kv", "mem_w_o": "mem_w_o",
          "ffn_w_in": "ffn_w_in", "ffn_w_out": "ffn_w_out"}

_CACHE = {}


def run(inp, T, plan, batches, n_cores):
    key = (T, tuple(plan))
    if key not in _CACHE:
        _CACHE[key] = Prog(T, plan)
    prog = _CACHE[key]
    hp = host_params(inp)
    ws = {n: np.ascontiguousarray(np.asarray(inp[n], np.float32)) for n in WNAMES}
    maps = []
    for c in range(n_cores):
        b = batches[c]
        m = dict(hp)
        m["x"] = np.ascontiguousarray(np.asarray(inp["x"][b, :T], np.float32))
        m["mem"] = np.ascontiguousarray(np.asarray(inp["mem"][b], np.float32))
        m["pos"] = np.ascontiguousarray(np.asarray(inp["positions"][b, :T], np.int32))
        m.update(ws)
        maps.append(m)
    res = run_bass_kernel_spmd(prog.nc, maps, core_ids=list(range(n_cores)))
    return [r["y"] for r in res.results]


def kernel(**inputs):
    T = 8192
    ys = run(inputs, T, FULL_PLAN, [0, 1], 2)
    return np.stack([ys[0], ys[1]], axis=0).astype(np.float32)
```

```python
import math
from contextlib import ExitStack
import numpy as np
import concourse.bass as bass
import concourse.mybir as mybir
from concourse.bass_utils import run_bass_kernel_spmd

F32 = mybir.dt.float32
BF16 = mybir.dt.bfloat16
I32 = mybir.dt.int32
AF = mybir.ActivationFunctionType
ALU = mybir.AluOpType

ENG = ("pe", "act", "dve", "pool", "sp")

D = 2048
KC = 16
DI = 4096
NH = 64
NG = 8
DFF = 5632
EPS = 1e-6
DILS = (1, 4, 16)
TB = 512
PK = 4


class Buf:
    __slots__ = ("name", "w", "r")

    def __init__(self, name=""):
        self.name = name
        self.w = {}
        self.r = []


class Sched:
    NDMA = 8

    def __init__(self, nc, stack):
        self.nc = nc
        self.ops = {e: [] for e in ENG}
        self.cnt = {e: 0 for e in ENG}
        self.sem = {e: stack.enter_context(nc.semaphore("s_" + e)) for e in ENG}
        self.dsem = {e: [stack.enter_context(nc.semaphore("d_%s%d" % (e, i))) for i in range(self.NDMA)]
                     for e in ("sp", "pool", "act")}
        self.dcnt = {e: [0] * self.NDMA for e in self.dsem}
        self.drr = {e: 0 for e in self.dsem}
        self.seen = {e: {} for e in ENG}
        self.semobj = {}
        for e in ENG:
            self.semobj[("e", e)] = self.sem[e]
        for e in self.dsem:
            for i, s in enumerate(self.dsem[e]):
                self.semobj[("d", e, i)] = s

    def _deps(self, eng, reads, writes, acc=()):
        deps = {}

        def add(tok):
            k, v = tok
            if deps.get(k, 0) < v:
                deps[k] = v
        for b in reads:
            for t in b.w.items():
                add(t)
        for b in writes:
            for t in b.w.items():
                add(t)
            for t in b.r:
                add(t)
        for b in acc:
            for t in b.r:
                add(t)
        waits = []
        seen = self.seen[eng]
        for k, v in deps.items():
            if eng == "pe" and k == ("e", "pe"):
                continue
            if seen.get(k, 0) >= v:
                continue
            seen[k] = v
            waits.append((k, v))
        return waits

    def _mark(self, tok, reads, writes, acc=()):
        for b in reads:
            b.r = [t for t in b.r if t[0] != tok[0]]
            b.r.append(tok)
        for b in writes:
            b.w = {tok[0]: tok[1]}
            b.r = []
        for b in acc:
            b.w[tok[0]] = tok[1]
            b.r = []

    def op(self, eng, fn, reads=(), writes=(), acc=()):
        waits = self._deps(eng, reads, writes, acc)
        self.cnt[eng] += 1
        tok = (("e", eng), self.cnt[eng])
        self.ops[eng].append((waits, fn, (("e", eng), 1)))
        self._mark(tok, reads, writes, acc)
        return tok

    def dma(self, q, fn, reads=(), writes=(), acc=()):
        i = self.drr[q]
        self.drr[q] = (i + 1) % self.NDMA
        key = ("d", q, i)
        waits = self._deps(q, reads, writes, acc)
        prev = self.dcnt[q][i]
        if prev and self.seen[q].get(key, 0) < prev:
            self.seen[q][key] = prev
            waits.append((key, prev))
        self.dcnt[q][i] = prev + 16
        tok = (key, prev + 16)
        self.ops[q].append((waits, fn, (key, 16)))
        self._mark(tok, reads, writes, acc)
        return tok

    def barrier(self):
        toks = [(("e", e), self.cnt[e]) for e in ENG if self.cnt[e]]
        for q in self.dsem:
            for i in range(self.NDMA):
                if self.dcnt[q][i]:
                    toks.append((("d", q, i), self.dcnt[q][i]))
        for e in ENG:
            if e != "pool":
                self.wait_all(e, toks)

    def wait_all(self, eng, toks):
        waits = []
        for k, v in toks:
            if eng == "pe" and k == ("e", "pe"):
                continue
            if self.seen[eng].get(k, 0) < v:
                self.seen[eng][k] = v
                waits.append((k, v))
        if waits:
            self.ops[eng].append((waits, None, None))

    def emit(self, block):
        engs = {"pe": block.tensor, "act": block.scalar, "dve": block.vector, "pool": block.gpsimd,
                "sp": block.sync}
        for e in ENG:
            ops = self.ops[e]
            if not ops:
                continue

            def body(engine, ops=ops):
                for waits, fn, inc in ops:
                    for k, v in waits:
                        engine.wait_ge(self.semobj[k], v)
                    if fn is not None:
                        ins = fn(engine)
                        ins.then_inc(self.semobj[inc[0]], inc[1])
            engs[e](body)


def bc(ap, shape):
    return ap.broadcast_to(list(shape))


class Prog:
    ARENA = 25620

    def __init__(self, T, plan):
        self.T = T
        self.NB = T // TB
        self.plan = plan
        nc = self.nc = bass.Bass("TRN2", target_bir_lowering=False)
        st = self.st = ExitStack()
        din = lambda n, s, dt=F32: nc.dram_tensor(n, list(s), dt, kind="ExternalInput").ap()
        dint = lambda n, s, dt=F32: nc.dram_tensor(n, list(s), dt, kind="Internal").ap()
        self.x = din("x", [T, D])
        self.mem = din("mem", [256, D])
        self.pos = din("pos", [T], I32)
        self.gains = din("gains", [128, 15, KC])
        self.convw = din("convw", [128, 2, 48, 4])
        self.convb = din("convb", [128, 2, 48])
        self.rowp = din("rowp", [2, 3, 64])
        self.ssdnw = din("ssdnw", [128, 2, 32])
        self.cst = din("cst", [128, 6, 128])
        self.invf = din("invf", [128, 2])
        self.ssd_w_in = din("ssd_w_in", [2, D, 10304])
        self.ssd_w_out = din("ssd_w_out", [2, DI, D])
        self.w_kv = din("w_kv_shared", [D, 12288])
        self.dil_w_q = din("dil_w_q", [2, D, 6144])
        self.dil_w_o = din("dil_w_o", [2, D, D])
        self.mem_w_q = din("mem_w_q", [4, D, 512])
        self.mem_w_kv = din("mem_w_kv", [4, D, 1024])
        self.mem_w_o = din("mem_w_o", [4, 512, D])
        self.ffn_w_in = din("ffn_w_in", [4, D, 2 * DFF])
        self.ffn_w_out = din("ffn_w_out", [4, DFF, D])
        self.y = nc.dram_tensor("y", [T, D], F32, kind="ExternalOutput").ap()
        self.hs = dint("hs", [KC, 128, T])
        self.cosd = dint("cosd", [128, T])
        self.sind = dint("sind", [128, T])
        self.memTd = dint("memTd", [128, KC, 256], BF16)
        self.KTs = dint("KTs", [48, 128, T], BF16)
        self.VTs = dint("VTs", [48, 128, T], BF16)
        self.QTs = dint("QTs", [48, 128, T], BF16)
        self.OTs = dint("OTs", [KC, 128, T], BF16)
        self.b_hs = [Buf() for _ in range(self.NB)]
        self.b_tab = Buf()
        self.b_memTd = Buf()
        self.b_KV = Buf()
        self.b_Q = Buf()
        self.b_O = Buf()
        self.b_y = Buf()

        S = self.S = Sched(nc, st)
        sb = self.sb = lambda n, s, dt: st.enter_context(nc.sbuf_tensor(n, list(s), dt))
        self.ps = [st.enter_context(nc.psum_tensor("ps%d" % i, [128, 512], F32)) for i in range(8)]
        self.bps = [Buf("ps%d" % i) for i in range(8)]
        self.pi = 0
        self.cstf = sb("cstf", [128, 6, 128], F32)
        self.cstb = sb("cstb", [128, 6, 128], BF16)
        self.onesb = sb("onesb", [128, 128], BF16)
        self.gn = sb("gn", [128, 15, KC], F32)
        self.cw = sb("cw", [128, 2, 48, 4], F32)
        self.cb = sb("cb", [128, 2, 48], F32)
        self.rp = sb("rp", [128, 2, 3, 64], F32)
        self.aneg = sb("aneg", [128, 2, 64], F32)
        self.snw = sb("snw", [128, 2, 32], F32)
        self.ivf = sb("ivf", [128, 2], F32)
        self.epsb = sb("epsb", [128, 1], F32)
        self.b_c = Buf("consts")
        self.hT = sb("hT", [128, KC, TB], F32)
        self.b_hT = Buf("hT")
        self.xn = sb("xn", [128, KC, TB], BF16)
        self.b_xn = Buf("xn")
        self.rstd = sb("rstd", [128, TB], F32)
        self.b_rstd = Buf("rstd")
        self.panels = [sb("pan%d" % i, [128, PK, 512], BF16) for i in range(4)]
        self.bpan = [Buf("pan%d" % i) for i in range(4)]
        self.pani = 0
        self.KmT = sb("KmT", [128, 4, 256], BF16)
        self.Vm = sb("Vm", [128, 2, 512], BF16)
        self.b_kvm = Buf()
        self.Hst = sb("Hst", [128, DI], F32)
        self.Hb = sb("Hb", [128, DI], BF16)
        self.b_H = [Buf() for _ in range(NG)]
        self.b_Hb = [Buf() for _ in range(NG)]
        self.halo = sb("halo", [128, 48, 3], F32)
        self.b_halo = Buf()
        self.arena = sb("arena", [128, self.ARENA], F32)
        self.b_ar = Buf("arena")

        self.build()
        with nc.Block() as block:
            S.emit(block)
        st.close()

    def bank(self):
        i = self.pi % 8
        self.pi += 1
        return i

    def carve(self, off_bytes, shape, dt, base="arena"):
        esz = 4 if dt in (F32, I32) else 2
        n = 1
        for s in shape[1:]:
            n *= s
        nw = (n * esz + 3) // 4
        assert off_bytes % 4 == 0
        if base == "arena":
            assert off_bytes + n * esz <= self.ARENA * 4, (off_bytes, shape)
            flat = self.arena[:, off_bytes // 4: off_bytes // 4 + nw]
        else:
            assert off_bytes + n * esz <= KC * TB * 2, (off_bytes, shape)
            flat = self.xn[:].rearrange("p a b -> p (a b)").bitcast(F32)[:, off_bytes // 4: off_bytes // 4 + nw]
        if dt != F32:
            flat = flat.bitcast(dt)
        flat = flat[:, 0:n]
        if len(shape) == 2:
            return flat
        if len(shape) == 3:
            return flat.rearrange("p (a b) -> p a b", b=shape[2])
        if len(shape) == 4:
            return flat.rearrange("p (a b c) -> p a b c", b=shape[2], c=shape[3])
        raise ValueError

    def mm(self, out, lhsT, rhs, start, stop, R, W):
        return self.S.op("pe", lambda e: e.matmul(out, lhsT=lhsT, rhs=rhs, start=start, stop=stop), reads=R, writes=W)

    def tr(self, out, in_, ident, R, W):
        return self.S.op("pe", lambda e: e.transpose(out, in_, ident), reads=R, writes=W)

    def act(self, out, in_, func, R, W, acc=(), **kw):
        return self.S.op("act", lambda e: e.activation(out=out, in_=in_, func=func, **kw), reads=R, writes=W, acc=acc)

    def amul(self, out, in_, mul, R, W, acc=()):
        return self.S.op("act", lambda e: e.mul(out=out, in_=in_, mul=mul), reads=R, writes=W, acc=acc)

    def tt(self, out, in0, in1, op, R, W, eng="dve", acc=()):
        return self.S.op(eng, lambda e: e.tensor_tensor(out=out, in0=in0, in1=in1, op=op), reads=R, writes=W, acc=acc)

    def ts(self, out, in0, s1, s2, op0, op1, R, W, eng="dve", acc=()):
        if s2 is None:
            return self.S.op(eng, lambda e: e.tensor_scalar(out=out, in0=in0, scalar1=s1, scalar2=None, op0=op0),
                             reads=R, writes=W, acc=acc)
        return self.S.op(eng, lambda e: e.tensor_scalar(out=out, in0=in0, scalar1=s1, scalar2=s2, op0=op0, op1=op1),
                         reads=R, writes=W, acc=acc)

    def stt(self, out, in0, scalar, in1, op0, op1, R, W, eng="dve", acc=()):
        return self.S.op(eng, lambda e: e.scalar_tensor_tensor(out=out, in0=in0, scalar=scalar, in1=in1, op0=op0, op1=op1),
                         reads=R, writes=W, acc=acc)

    def cp(self, out, in_, R, W, eng="dve", acc=()):
        return self.S.op(eng, lambda e: e.tensor_copy(out=out, in_=in_), reads=R, writes=W, acc=acc)

    def ld(self, out, in_, R, W, q="sp", acc=(), **kw):
        return self.S.dma(q, lambda e: e.dma_start(out=out, in_=in_, **kw), reads=R, writes=W, acc=acc)

    def panel(self, w2d, k0, nk, c0, ncols):
        i = self.pani % len(self.panels)
        self.pani += 1
        pan, b = self.panels[i], self.bpan[i]
        step = 4
        for kk in range(0, nk, step):
            n = min(step, nk - kk)
            src = w2d[(k0 + kk) * 128:(k0 + kk + n) * 128, c0:c0 + ncols].rearrange("(k p) n -> p k n", p=128)
            self.ld(pan[:, kk:kk + n, 0:ncols], src, [], [], q="pool", acc=[b])
        return pan, b

    def load_h(self, blk):
        self.ld(self.hT[:], self.hs[:, :, blk * TB:(blk + 1) * TB].rearrange("k p t -> p k t"),
                [self.b_hs[blk]], [self.b_hT])

    def store_h(self, blk):
        self.ld(self.hs[:, :, blk * TB:(blk + 1) * TB].rearrange("k p t -> p k t"), self.hT[:],
                [self.b_hT], [self.b_hs[blk]])

    def rms(self, gi, out=None, ob=None):
        out = self.xn if out is None else out
        ob = self.b_xn if ob is None else ob
        sq = self.carve(0, [128, KC, TB], BF16)
        bsq = Buf()
        for k in range(KC):
            self.act(sq[:, k, :], self.hT[:, k, :], AF.Square, [self.b_hT], [], acc=[bsq, self.b_ar])
        bk = self.bank()
        for k in range(KC):
            self.mm(self.ps[bk][:], self.onesb[:], sq[:, k, :], k == 0, k == KC - 1, [bsq, self.b_c], [self.bps[bk]])
        self.act(self.rstd[:], self.ps[bk][:], AF.Ln, [self.bps[bk], self.b_c], [self.b_rstd], scale=1.0 / D, bias=self.epsb[:])
        self.act(self.rstd[:], self.rstd[:], AF.Exp, [self.b_rstd], [self.b_rstd], scale=-0.5)
        for k in range(KC):
            self.stt(out[:, k, :], self.hT[:, k, :], self.gn[:, gi, k:k + 1], self.rstd[:], ALU.mult, ALU.mult,
                     [self.b_hT, self.b_rstd, self.b_c], [], acc=[ob])

    def linear_fm(self, w2d, c0, ncols, nk, rhs_of_k, R, consume, xcols=TB):
        for cb in range(0, ncols, 512):
            nc_ = min(512, ncols - cb)
            nm = nc_ // 128
            banks = [self.bank() for _ in range(nm)]
            for k0 in range(0, nk, PK):
                n = min(PK, nk - k0)
                pan, pb = self.panel(w2d, k0, n, c0 + cb, nc_)
                for m in range(nm):
                    for k in range(n):
                        self.mm(self.ps[banks[m]][:, 0:xcols], pan[:, k, m * 128:(m + 1) * 128], rhs_of_k(k0 + k),
                                (k0 + k) == 0, (k0 + k) == nk - 1, [pb] + R, [self.bps[banks[m]]])
            for m in range(nm):
                consume((cb // 128) + m, self.ps[banks[m]][:, 0:xcols], self.bps[banks[m]])

    def linear_tm(self, w2d, c0, ncols, nk, lhs_of, nch, R, consume):
        banks = [self.bank() for _ in range(nch)]
        for k0 in range(0, nk, PK):
            n = min(PK, nk - k0)
            pan, pb = self.panel(w2d, k0, n, c0, ncols)
            for c in range(nch):
                for k in range(n):
                    self.mm(self.ps[banks[c]][:, 0:ncols], lhs_of(k0 + k, c), pan[:, k, 0:ncols],
                            (k0 + k) == 0, (k0 + k) == nk - 1, [pb] + R, [self.bps[banks[c]]])
        for c in range(nch):
            consume(c, self.ps[banks[c]][:, 0:ncols], self.bps[banks[c]])

    def add_to_h(self, m, ps, pb):
        self.tt(self.hT[:, m, :], ps, self.hT[:, m, :], ALU.add, [pb, self.b_hT], [], acc=[self.b_hT])

    def setup(self):
        S = self.S
        self.ld(self.cstf[:], self.cst, [], [], acc=[self.b_c])
        self.ld(self.gn[:], self.gains, [], [], acc=[self.b_c])
        self.ld(self.cw[:], self.convw, [], [], acc=[self.b_c])
        self.ld(self.cb[:], self.convb, [], [], acc=[self.b_c])
        self.ld(self.snw[:], self.ssdnw, [], [], acc=[self.b_c])
        self.ld(self.ivf[:], self.invf, [], [], acc=[self.b_c])
        self.ld(self.rp[:].rearrange("p a b c -> p (a b c)"),
                self.rowp.rearrange("a b c -> (a b c)").partition_broadcast(128), [], [], acc=[self.b_c])
        S.barrier()
        self.cp(self.cstb[:], self.cstf[:], [self.b_c], [], acc=[self.b_c])
        S.op("dve", lambda e: e.memset(self.onesb[:], 1.0), [], [], acc=[self.b_c])
        S.op("dve", lambda e: e.memset(self.epsb[:], EPS), [], [], acc=[self.b_c])
        self.act(self.aneg[:], self.rp[:, :, 1, :], AF.Exp, [self.b_c], [], acc=[self.b_c])
        S.barrier()
        self.ts(self.aneg[:], self.aneg[:], -1.0, None, ALU.mult, None, [self.b_c], [], acc=[self.b_c])
        self.ident = self.cstf[:, 0, :]
        self.identb = self.cstb[:, 0, :]
        self.tri = self.cstf[:, 1, :]
        self.U = self.cstf[:, 2, :]
        self.sel127 = self.cstf[:, 3, :]
        self.swapb = self.cstb[:, 4, :]
        S.barrier()

    def prepass(self):
        NB = self.NB
        xt = self.carve(0, [128, 4, D], F32)
        A = [self.b_ar]
        for blk in range(NB):
            self.ld(xt, self.x[blk * TB:(blk + 1) * TB, :].rearrange("(c p) d -> p c d", p=128), [], A)
            for c in range(4):
                for k in range(KC):
                    bk = self.bank()
                    self.tr(self.ps[bk][:, 0:128], xt[:, c, k * 128:(k + 1) * 128], self.ident, A + [self.b_c], [self.bps[bk]])
                    if (c * KC + k) % 2 == 0:
                        self.act(self.hT[:, k, c * 128:(c + 1) * 128], self.ps[bk][:, 0:128], AF.Copy, [self.bps[bk]], [], acc=[self.b_hT])
                    else:
                        self.cp(self.hT[:, k, c * 128:(c + 1) * 128], self.ps[bk][:, 0:128], [self.bps[bk]], [], acc=[self.b_hT])
            self.store_h(blk)
            pi_ = self.carve(40960, [128, TB], I32)
            ang = self.carve(43008, [128, TB], F32)
            kf = self.carve(45056, [128, TB], F32)
            r = self.carve(47104, [128, TB], F32)
            sc = self.carve(49152, [128, 2, TB], F32)
            B = [Buf("rope")]
            self.ld(pi_, self.pos[blk * TB:(blk + 1) * TB].partition_broadcast(128), [], B)
            self.cp(ang, pi_, B, B)
            self.ts(ang, ang, self.ivf[:, 0:1], None, ALU.mult, None, B + [self.b_c], B)
            MAGIC = 12582912.0
            HI = 6.28125
            LO = float(2 * np.pi - 6.28125)
            for j, shift in enumerate((0.5 * np.pi, 0.0)):
                self.ts(kf, ang, float(shift), float(1 / (2 * np.pi)), ALU.add, ALU.mult, B, B)
                self.ts(kf, kf, MAGIC, None, ALU.add, None, B, B)
                self.ts(kf, kf, MAGIC, None, ALU.subtract, None, B, B)
                self.ts(r, ang, float(shift), None, ALU.add, None, B, B)
                self.stt(r, kf, -HI, r, ALU.mult, ALU.add, B, B)
                self.stt(r, kf, -LO, r, ALU.mult, ALU.add, B, B)
                self.ts(r, r, float(-np.pi), float(np.pi), ALU.max, ALU.min, B, B)
                self.act(sc[:, j, :], r, AF.Sin, B, B)
            self.ts(sc[:, 1, :], sc[:, 1, :], self.ivf[:, 1:2], None, ALU.mult, None, B + [self.b_c], B)
            self.ld(self.cosd[:, blk * TB:(blk + 1) * TB], sc[:, 0, :], B, [], acc=[self.b_tab])
            self.ld(self.sind[:, blk * TB:(blk + 1) * TB], sc[:, 1, :], B, [], acc=[self.b_tab])
            self.S.barrier()
        mt = self.carve(0, [128, 2, D], F32)
        junk = self.carve(16384, [128, D], F32)
        ssq = self.carve(24576, [128, 4], F32)
        memT = self.carve(32768, [128, KC, 256], BF16)
        self.ld(mt, self.mem.rearrange("(c p) d -> p c d", p=128), [], A)
        for c in range(2):
            self.S.op("dve", lambda e, c=c: e.memset(ssq[:, c:c + 1], 0.0), A, A)
            self.act(junk, mt[:, c, :], AF.Square, A, A, accum_out=ssq[:, c:c + 1])
        self.act(ssq[:, 2:4], ssq[:, 0:2], AF.Ln, A + [self.b_c], A, scale=1.0 / D, bias=self.epsb[:])
        self.act(ssq[:, 2:4], ssq[:, 2:4], AF.Exp, A, A, scale=-0.5)
        bm = Buf()
        for c in range(2):
            self.ts(mt[:, c, :], mt[:, c, :], ssq[:, 2 + c:3 + c], None, ALU.mult, None, A, A)
            for k in range(KC):
                bk = self.bank()
                self.tr(self.ps[bk][:, 0:128], mt[:, c, k * 128:(k + 1) * 128], self.ident, A + [self.b_c], [self.bps[bk]])
                self.ts(memT[:, k, c * 128:(c + 1) * 128], self.ps[bk][:, 0:128], self.gn[:, 13, k:k + 1], None,
                        ALU.mult, None, [self.bps[bk], self.b_c], [], acc=[bm])
        self.ld(self.memTd, memT, [bm], [self.b_memTd])
        self.S.barrier()

    def mem_kv(self, li):
        w = self.mem_w_kv[li]
        memT = self.carve(32768, [128, KC, 256], BF16)
        bm = Buf()
        self.ld(memT, self.memTd, [self.b_memTd], [bm])

        def consume(m, ps, pb):
            self.cp(self.KmT[:, m, :], ps, [pb], [], acc=[self.b_kvm])
        self.linear_fm(w, 0, 512, KC, lambda k: memT[:, k, :], [bm], consume, xcols=256)

        def cv(c, ps, pb):
            self.cp(self.Vm[:, c, :], ps, [pb], [], acc=[self.b_kvm])
        self.linear_tm(w, 512, 512, KC, lambda k, c: memT[:, k, c * 128:(c + 1) * 128], 2, [bm], cv)
        self.S.barrier()

    def mem_attn(self, li):
        self.rms(4 + li)
        self.S.barrier()
        qT = self.carve(0, [128, 4, TB], BF16)
        oT = self.carve(4096, [128, 4, TB], BF16)
        pT = [self.carve(8192 + i * 2048, [128, 2, TB], BF16) for i in range(2)]
        rden = [self.carve(12288 + i * 2048, [128, TB], F32) for i in range(2)]
        bq, bo_ = Buf(), Buf()
        bpp = [Buf(), Buf()]
        brr = [Buf(), Buf()]
        sc = 128 ** -0.5

        def cq(m, ps, pb):
            self.amul(qT[:, m, :], ps, sc, [pb], [], acc=[bq])
        self.linear_fm(self.mem_w_q[li], 0, 512, KC, lambda k: self.xn[:, k, :], [self.b_xn], cq)
        for hd in range(4):
            p_ = pT[hd % 2]
            bp = bpp[hd % 2]
            first = True
            for mc in range(2):
                bk = self.bank()
                self.mm(self.ps[bk][:], self.KmT[:, hd, mc * 128:(mc + 1) * 128], qT[:, hd, :], True, True,
                        [bq, self.b_kvm], [self.bps[bk]])
                self.act(p_[:, mc, :], self.ps[bk][:], AF.Exp, [self.bps[bk]], [bp] if mc == 0 else [], acc=[] if mc == 0 else [bp])
            bo, bd = self.bank(), self.bank()
            for mc in range(2):
                self.mm(self.ps[bo][:], self.Vm[:, mc, hd * 128:(hd + 1) * 128], p_[:, mc, :], mc == 0, mc == 1,
                        [bp, self.b_kvm], [self.bps[bo]])
            for mc in range(2):
                self.mm(self.ps[bd][:], self.onesb[:], p_[:, mc, :], mc == 0, mc == 1, [bp, self.b_c], [self.bps[bd]])
            rd = rden[hd % 2]
            brd = brr[hd % 2]
            self.S.op("dve", lambda e, bd=bd, rd=rd: e.reciprocal(out=rd, in_=self.ps[bd][:]), [self.bps[bd]], [brd])
            self.tt(oT[:, hd, :], self.ps[bo][:], rd, ALU.mult, [self.bps[bo], brd], [], acc=[bo_])
        self.linear_fm(self.mem_w_o[li], 0, D, 4, lambda k: oT[:, k, :], [bo_], self.add_to_h)
        self.S.barrier()

    def ffn(self, li):
        self.rms(8 + li)
        self.S.barrier()
        hid = self.carve(0, [128, 44, TB], BF16)
        sg = self.carve(45056, [128, 8, TB], BF16)
        bhid = Buf()
        bsg8 = [Buf() for _ in range(8)]
        w = self.ffn_w_in[li]
        for j in range(11):
            par = (j % 2) * 4
            bsg = bsg8[par:par + 4]

            def cg(m, ps, pb, bsg=bsg, par=par):
                self.act(sg[:, par + m % 4, :], ps, AF.Silu, [pb], [bsg[m % 4]])
            self.linear_fm(w, j * 512, 512, KC, lambda k: self.xn[:, k, :], [self.b_xn], cg)

            def cu(m, ps, pb, bsg=bsg, par=par, j=j):
                self.tt(hid[:, j * 4 + m % 4, :], ps, sg[:, par + m % 4, :], ALU.mult, [pb, bsg[m % 4]], [], acc=[bhid])
            self.linear_fm(w, DFF + j * 512, 512, KC, lambda k: self.xn[:, k, :], [self.b_xn], cu)
        self.linear_fm(self.ffn_w_out[li], 0, D, 44, lambda k: hid[:, k, :], [bhid], self.add_to_h)
        self.S.barrier()

    def ssd_block(self, li, blk):
        S = self.S
        C_ = [self.b_c]
        w = self.ssd_w_in[li]
        self.rms(li)
        S.barrier()
        xT = self.carve(0, [128, 32, TB], BF16)
        BT = self.carve(32768, [128, NG, TB], BF16)
        CT = self.carve(40960, [128, NG, TB], BF16)
        zs = self.carve(49152, [128, 4, DI], BF16)
        o = 81920
        ubuf = [self.carve(o + i * 2064, [128, 516], F32) for i in range(3)]
        o += 3 * 2064
        acc = [self.carve(o + i * 2048, [128, TB], F32) for i in range(3)]
        o += 3 * 2048
        small = self.carve(o, [128, 4, 8, 64], F32)
        o += 8192
        assert o <= self.ARENA * 4, o
        b_x, b_B, b_C, b_z, b_sm = Buf("xT"), Buf("BT"), Buf("CT"), Buf("zs"), Buf("small")
        if blk == 0:
            S.op("dve", lambda e: e.memset(self.halo[:], 0.0), [], [self.b_halo])
            S.op("dve", lambda e: e.memset(self.Hst[:], 0.0), [], self.b_H)
            S.op("dve", lambda e: e.memset(self.Hb[:], 0.0), [], self.b_Hb)
        ci = [0]
        bcv = [Buf(), Buf(), Buf()]

        def cconv(m, ps, pb):
            i = ci[0] % 3
            ci[0] += 1
            u, a, b = ubuf[i], acc[i], bcv[i]
            cwv = self.cw[:, li, m, :]
            self.cp(u[:, 0:3], self.halo[:, m, :], [self.b_halo], [b])
            self.act(u[:, 3:3 + TB], ps, AF.Copy, [pb], [], acc=[b])
            self.act(a, ps, AF.Identity, [pb] + C_, [], acc=[b], scale=cwv[:, 3:4], bias=self.cb[:, li, m:m + 1])
            self.cp(self.halo[:, m, :], u[:, TB:TB + 3], [b], [], acc=[self.b_halo])
            for j in range(3):
                self.stt(a, u[:, j:j + TB], cwv[:, j:j + 1], a, ALU.mult, ALU.add, [b] + C_, [], acc=[b])
            if m < 32:
                self.act(xT[:, m, :], a, AF.Silu, [b], [], acc=[b_x])
            elif m < 40:
                self.act(BT[:, m - 32, :], a, AF.Silu, [b], [], acc=[b_B])
            else:
                self.act(CT[:, m - 40, :], a, AF.Silu, [b], [], acc=[b_C])
        self.linear_fm(w, DI, 6144, KC, lambda k: self.xn[:, k, :], [self.b_xn], cconv)
        for j in range(8):
            def cz(c, ps, pb, j=j):
                self.act(zs[:, c, j * 512:(j + 1) * 512], ps, AF.Silu, [pb], [], acc=[b_z])
            self.linear_tm(w, j * 512, 512, KC, lambda k, c: self.xn[:, k, c * 128:(c + 1) * 128], 4, [self.b_xn], cz)
        def cdt(c, ps, pb):
            dt, dtA, acum, ea, cd, wend, tmp = [small[:, c, i, :] for i in range(7)]
            M = [b_sm]
            self.tt(tmp, ps, self.rp[:, li, 0, :], ALU.add, [pb] + C_, M)
            self.act(tmp, tmp, AF.Exp, M, M)
            self.act(dt, tmp, AF.Ln, M, M, bias=1.0)
            self.tt(dtA, dt, self.aneg[:, li, :], ALU.mult, M + C_, M)
            bk = self.bank()
            self.mm(self.ps[bk][:, 0:64], self.tri, dtA, True, True, M + C_, [self.bps[bk]])
            self.cp(acum, self.ps[bk][:, 0:64], [self.bps[bk]], M)
            self.act(ea, acum, AF.Exp, M, M)
            bk = self.bank()
            self.mm(self.ps[bk][:, 0:64], self.sel127, acum, True, True, M + C_, [self.bps[bk]])
            self.act(cd, self.ps[bk][:, 0:64], AF.Exp, [self.bps[bk]], M)
            self.tt(tmp, self.ps[bk][:, 0:64], acum, ALU.subtract, [self.bps[bk]] + M, M)
            self.act(tmp, tmp, AF.Exp, M, M)
            self.tt(wend, tmp, dt, ALU.mult, M, M)
        self.linear_tm(w, 10240, 64, KC, lambda k, c: self.xn[:, k, c * 128:(c + 1) * 128], 4, [self.b_xn], cdt)
        S.barrier()
        Rg = self.carve(81920, [128, 8, 128], F32)
        dec = self.carve(81920 + 4096, [128, 8, 128], F32)
        o2 = 0
        def xc(shape, dt):
            nonlocal o2
            n = 1
            for s_ in shape[1:]:
                n *= s_
            v = self.carve(o2, shape, dt, base="xn")
            o2 += ((n * (4 if dt == F32 else 2) + 3) // 4) * 4
            return v
        CBm = xc([128, 128], F32)
        MT2 = [xc([128, 8, 128], BF16) for _ in range(2)]
        xtok2 = [xc([128, 512], BF16) for _ in range(2)]
        xw = xc([128, 512], BF16)
        Btok = xc([128, 128], BF16)
        t1 = xc([128, 512], F32)
        t2 = xc([128, 512], F32)
        t3 = xc([128, 512], F32)
        yn = xc([128, 512], BF16)
        ssq = xc([128, 4], F32)
        bR, bD, bCB, bY, bS = [Buf() for _ in range(5)]
        bM2 = [Buf(), Buf()]
        bX2 = [Buf(), Buf()]
        v3 = lambda ap: ap.rearrange("p (a b) -> p a b", b=64)

        def stageA(c, g, par):
            cs = slice(c * 128, (c + 1) * 128)
            dt, dtA = small[:, c, 0, :], small[:, c, 1, :]
            hs_ = slice(g * 8, (g + 1) * 8)
            MT, xtok, bM, bX = MT2[par], xtok2[par], bM2[par], bX2[par]
            self.tt(Rg, bc(self.tri.unsqueeze(1), [128, 8, 128]), bc(dtA[:, hs_].unsqueeze(2), [128, 8, 128]), ALU.mult,
                    [b_sm] + C_, [bR])
            for hh in range(2):
                bk = self.bank()
                self.mm(self.ps[bk][:], self.U, Rg[:, hh * 4:(hh + 1) * 4, :], True, True, [bR] + C_, [self.bps[bk]])
                if hh == 0:
                    self.act(dec[:, 0:4, :], self.ps[bk][:].rearrange("p (a b) -> p a b", b=128), AF.Exp,
                             [self.bps[bk]], [bD])
                else:
                    self.act(dec[:, 4:8, :], self.ps[bk][:].rearrange("p (a b) -> p a b", b=128), AF.Exp,
                             [self.bps[bk]], [], acc=[bD])
            bk = self.bank()
            self.mm(self.ps[bk][:, 0:128], BT[:, g, cs], CT[:, g, cs], True, True, [b_B, b_C], [self.bps[bk]])
            self.tt(CBm, self.ps[bk][:, 0:128], self.tri, ALU.mult, [self.bps[bk]] + C_, [bCB])
            self.tt(dec, dec, bc(dt[:, hs_].unsqueeze(2), [128, 8, 128]), ALU.mult, [bD, b_sm], [bD], eng="pool")
            self.tt(MT, dec, bc(CBm.unsqueeze(1), [128, 8, 128]), ALU.mult, [bD, bCB], [bM])
            bk = self.bank()
            pb16 = self.ps[bk][:].bitcast(BF16)
            for q in range(4):
                self.tr(pb16[:, q * 128:(q + 1) * 128], xT[:, g * 4 + q, cs], self.identb, [b_x] + C_, [self.bps[bk]])
            self.act(xtok, pb16[:, 0:512], AF.Copy, [self.bps[bk]], [bX])

        def stageB(c, g, par):
            cs = slice(c * 128, (c + 1) * 128)
            dt, dtA, acum, ea, cd, wend, tmp = [small[:, c, i, :] for i in range(7)]
            hs_ = slice(g * 8, (g + 1) * 8)
            gs = slice(g * 512, (g + 1) * 512)
            MT, xtok, bM, bX = MT2[par], xtok2[par], bM2[par], bX2[par]
            by = self.bank()
            for j in range(8):
                self.mm(self.ps[by][:, j * 64:(j + 1) * 64], MT[:, j, :], xtok[:, j * 64:(j + 1) * 64], True, True,
                        [bM, bX], [self.bps[by]])
            bo = self.bank()
            self.mm(self.ps[bo][:], CT[:, g, cs], self.Hb[:, gs], True, True, [b_C, self.b_Hb[g]], [self.bps[bo]])
            bk = self.bank()
            pb16 = self.ps[bk][:].bitcast(BF16)
            self.tr(pb16[:, 0:128], BT[:, g, cs], self.identb, [b_B] + C_, [self.bps[bk]])
            self.act(Btok, pb16[:, 0:128], AF.Copy, [self.bps[bk]], [bS])
            self.tt(v3(xw), v3(xtok), bc(wend[:, hs_].unsqueeze(2), [128, 8, 64]), ALU.mult, [bX, b_sm], [], eng="pool", acc=[bS])
            bs = self.bank()
            self.mm(self.ps[bs][:], Btok, xw, True, True, [bS], [self.bps[bs]])
            self.tt(v3(self.Hst[:, gs]), v3(self.Hst[:, gs]), bc(cd[:, hs_].unsqueeze(2), [128, 8, 64]), ALU.mult,
                    [self.b_H[g], b_sm], [self.b_H[g]], eng="pool")
            self.tt(v3(t1), v3(self.ps[bo][:]), bc(ea[:, hs_].unsqueeze(2), [128, 8, 64]), ALU.mult,
                    [self.bps[bo], b_sm], [bY])
            self.tt(t2, self.ps[by][:], t1, ALU.add, [self.bps[by], bY], [bY])
            self.tt(v3(t3), v3(xtok), bc(self.rp[:, li, 2, hs_].unsqueeze(2), [128, 8, 64]), ALU.mult, [bX] + C_, [bS], eng="pool")
            self.tt(t2, t2, t3, ALU.add, [bY, bS], [bY])
            self.tt(t2, t2, zs[:, c, gs], ALU.mult, [bY, b_z], [bY])
            S.op("dve", lambda e: e.memset(ssq[:, 0:1], 0.0), [bY], [bY])
            self.act(t1, t2, AF.Square, [bY], [bY], accum_out=ssq[:, 0:1])
            self.act(ssq[:, 1:2], ssq[:, 0:1], AF.Ln, [bY] + C_, [bY], scale=1.0 / 512, bias=self.epsb[:])
            self.act(ssq[:, 1:2], ssq[:, 1:2], AF.Exp, [bY], [bY], scale=-0.5)
            self.ts(yn, t2, ssq[:, 1:2], None, ALU.mult, None, [bY], [bY])
            self.tt(self.Hst[:, gs], self.Hst[:, gs], self.ps[bs][:], ALU.add, [self.b_H[g], self.bps[bs]], [self.b_H[g]])
            self.act(self.Hb[:, gs], self.Hst[:, gs], AF.Copy, [self.b_H[g]], [self.b_Hb[g]])
            bk = self.bank()
            pb16 = self.ps[bk][:].bitcast(BF16)
            for q in range(4):
                self.tr(pb16[:, q * 128:(q + 1) * 128], yn[:, q * 128:(q + 1) * 128], self.identb, [bY] + C_, [self.bps[bk]])
            for q in range(4):
                self.ts(xT[:, g * 4 + q, cs], pb16[:, q * 128:(q + 1) * 128], self.snw[:, li, g * 4 + q:g * 4 + q + 1], None,
                        ALU.mult, None, [self.bps[bk], bX] + C_, [], acc=[b_x])

        items = [(c, g) for c in range(4) for g in range(NG)]
        stageA(items[0][0], items[0][1], 0)
        for i, (c, g) in enumerate(items):
            if i + 1 < len(items):
                stageA(items[i + 1][0], items[i + 1][1], (i + 1) % 2)
            stageB(c, g, i % 2)
        S.barrier()
        self.linear_fm(self.ssd_w_out[li], 0, D, 32, lambda k: xT[:, k, :], [b_x], self.add_to_h)
        S.barrier()

    def rope_chunk(self, ps, pb, scale, cs_t, btab):
        i = self.rsi % 3
        self.rsi += 1
        qs = self.carve(49152 + i * 1024, [128, TB], BF16)
        ta = self.carve(52224 + i * 2048, [128, TB], F32)
        tb_ = self.carve(58368 + i * 2048, [128, TB], F32)
        ob = self.carve(64512 + i * 1024, [128, TB], BF16)
        b = self.rbuf[i]
        self.amul(qs, ps, scale, [pb], [b])
        bk = self.bank()
        self.mm(self.ps[bk][:], self.swapb, qs, True, True, [b, self.b_c], [self.bps[bk]])
        self.tt(ta, qs, cs_t[:, 0, :], ALU.mult, [b, btab], [], acc=[b])
        self.tt(tb_, self.ps[bk][:], cs_t[:, 1, :], ALU.mult, [self.bps[bk], btab], [], acc=[b])
        self.tt(ob, ta, tb_, ALU.add, [b], [], acc=[b])
        return ob, b

    def load_tabs(self, blk):
        cs_t = self.carve(40960, [128, 2, TB], F32)
        btab = Buf()
        self.ld(cs_t[:, 0, :], self.cosd[:, blk * TB:(blk + 1) * TB], [self.b_tab], [], acc=[btab])
        self.ld(cs_t[:, 1, :], self.sind[:, blk * TB:(blk + 1) * TB], [self.b_tab], [], acc=[btab])
        return cs_t, btab

    def kv_block(self, blk):
        self.rms(12)
        self.S.barrier()
        cs_t, btab = self.load_tabs(blk)
        self.rsi = 0
        self.rbuf = [Buf(), Buf(), Buf()]
        cols = slice(blk * TB, (blk + 1) * TB)

        def ck(m, ps, pb):
            ob, b = self.rope_chunk(ps, pb, 1.0, cs_t, btab)
            self.ld(self.KTs[m, :, cols], ob, [b], [], acc=[self.b_KV])
        self.linear_fm(self.w_kv, 0, 6144, KC, lambda k: self.xn[:, k, :], [self.b_xn], ck)

        def cv(m, ps, pb):
            i = self.rsi % 3
            self.rsi += 1
            ob = self.carve(64512 + i * 1024, [128, TB], BF16)
            b = self.rbuf[i]
            self.act(ob, ps, AF.Copy, [pb], [b])
            self.ld(self.VTs[m, :, cols], ob, [b], [], acc=[self.b_KV])
        self.linear_fm(self.w_kv, 6144, 6144, KC, lambda k: self.xn[:, k, :], [self.b_xn], cv)
        self.S.barrier()

    def q_block(self, li, blk):
        self.rms(li)
        self.S.barrier()
        cs_t, btab = self.load_tabs(blk)
        self.rsi = 0
        self.rbuf = [Buf(), Buf(), Buf()]
        cols = slice(blk * TB, (blk + 1) * TB)

        def cq(m, ps, pb):
            ob, b = self.rope_chunk(ps, pb, 128 ** -0.5, cs_t, btab)
            self.ld(self.QTs[m, :, cols], ob, [b], [], acc=[self.b_Q])
        self.linear_fm(self.dil_w_q[li - 2], 0, 6144, KC, lambda k: self.xn[:, k, :], [self.b_xn], cq)
        self.S.barrier()

    def dil_attn(self):
        T = self.T
        QS = 2048
        C_ = [self.b_c]
        mask2 = self.carve(0, [128, 2, 128], BF16)
        bmk = Buf()
        self.cp(mask2[:, 0, :], self.cstf[:, 5, :], C_, [], acc=[bmk])
        self.cp(mask2[:, 1, :], self.cstf[:, 1, :], C_, [], acc=[bmk])
        mflat = mask2[:].rearrange("p a b -> p (a b)")
        accb = self.carve(1024, [128, 2, QS], F32)
        oTb = self.carve(17408, [128, QS], BF16)
        rden = self.carve(21504, [128, QS], F32)
        Pm = [self.carve(65536 + i * 512, [128, 256], BF16) for i in range(4)]
        Pf = [self.carve(67584 + i * 1024, [128, 256], F32) for i in range(4)]
        o = 32768
        KT = self.carve(o, [128, 4096], BF16); o += 8192
        VT = self.carve(o, [128, 4096], BF16); o += 8192
        qT = self.carve(o, [128, QS], BF16); o += 4096
        Vtok = self.carve(o, [128, 32, 128], BF16); o += 8192
        assert o <= self.ARENA * 4
        bL = Buf("kvq")
        bV = Buf("vtok")
        bacc = Buf("acc")
        bP = [Buf() for _ in range(4)]
        pcount = 0
        for hh in range(16):
            for sg in range(T // QS):
                q0 = sg * QS
                for g, d in enumerate(DILS):
                    head = g * 16 + hh
                    halo = 128 * d
                    w0 = max(0, q0 - halo)
                    wl = q0 + QS - w0
                    self.ld(KT[:, 0:wl], self.KTs[head, :, w0:q0 + QS], [self.b_KV], [bL])
                    self.ld(VT[:, 0:wl], self.VTs[head, :, w0:q0 + QS], [self.b_KV], [], acc=[bL])
                    self.ld(qT[:, :], self.QTs[head, :, q0:q0 + QS], [self.b_Q], [], acc=[bL])
                    nbi = QS // (128 * d)
                    bi0 = q0 // (128 * d)
                    kb_lo = max(0, bi0 - 1)

                    def kcols(r, bi, d=d, w0=w0):
                        s = r + d * 128 * bi - w0
                        return slice(s, s + 127 * d + 1, d)
                    vidx = {}
                    n = 0
                    first = True
                    for r in range(d):
                        for bi in range(kb_lo, bi0 + nbi):
                            vidx[(r, bi)] = n
                            bk = self.bank()
                            pb16 = self.ps[bk][:].bitcast(BF16)
                            self.tr(pb16[:, 0:128], VT[:, kcols(r, bi)], self.identb, [bL] + C_, [self.bps[bk]])
                            W_, Acc = ([bV], []) if first else ([], [bV])
                            first = False
                            if n % 2 == 0:
                                self.cp(Vtok[:, n, :], pb16[:, 0:128], [self.bps[bk]], W_, acc=Acc)
                            else:
                                self.act(Vtok[:, n, :], pb16[:, 0:128], AF.Copy, [self.bps[bk]], W_, acc=Acc)
                            n += 1
                    for r in range(d):
                        for bi in range(bi0, bi0 + nbi):
                            qa = r + d * 128 * bi - q0
                            qsl = slice(qa, qa + 127 * d + 1, d)
                            has_prev = bi >= 1
                            bk = self.bank()
                            if has_prev:
                                self.mm(self.ps[bk][:, 0:128], KT[:, kcols(r, bi - 1)], qT[:, qsl], True, True, [bL], [self.bps[bk]])
                            self.mm(self.ps[bk][:, 128:256], KT[:, kcols(r, bi)], qT[:, qsl], True, True, [bL], [self.bps[bk]])
                            lo = 0 if has_prev else 128
                            pf, pm, bp = Pf[pcount % 4], Pm[pcount % 4], bP[pcount % 4]
                            pcount += 1
                            self.act(pf[:, lo:256], self.ps[bk][:, lo:256], AF.Exp, [self.bps[bk]], [bp])
                            self.tt(pm[:, lo:256], pf[:, lo:256], mflat[:, lo:256], ALU.mult,
                                    [bmk], [bp], eng=("dve" if pcount % 2 else "pool"))
                            bo = self.bank()
                            if has_prev:
                                self.mm(self.ps[bo][:, 0:128], Vtok[:, vidx[(r, bi - 1)], :], pm[:, 0:128], True, False, [bp, bV], [self.bps[bo]])
                            self.mm(self.ps[bo][:, 0:128], Vtok[:, vidx[(r, bi)], :], pm[:, 128:256], not has_prev, True, [bp, bV], [self.bps[bo]])
                            if has_prev:
                                self.mm(self.ps[bo][:, 128:256], self.onesb[:], pm[:, 0:128], True, False, [bp] + C_, [self.bps[bo]])
                            self.mm(self.ps[bo][:, 128:256], self.onesb[:], pm[:, 128:256], not has_prev, True, [bp] + C_, [self.bps[bo]])
                            src = self.ps[bo][:, 0:256].rearrange("p (a b) -> p a b", b=128)
                            dst = accb[:, :, qsl]
                            if g == 0:
                                self.cp(dst, src, [self.bps[bo]], [bacc])
                            else:
                                self.tt(dst, src, dst, ALU.add, [self.bps[bo], bacc], [bacc])
                self.S.op("dve", lambda e: e.reciprocal(out=rden, in_=accb[:, 1, :]), [bacc], [bacc])
                self.tt(oTb, accb[:, 0, :], rden, ALU.mult, [bacc], [bacc])
                self.ld(self.OTs[hh, :, q0:q0 + QS], oTb, [bacc], [], acc=[self.b_O])
        self.S.barrier()

    def final_block(self, blk):
        xf = self.carve(16384, [128, KC, TB], F32)
        yt = self.carve(49152, [128, 4, D], F32)
        bxf, byt = Buf(), Buf()
        self.rms(14, out=xf, ob=bxf)
        for c in range(4):
            for k in range(KC):
                bk = self.bank()
                self.tr(self.ps[bk][:, 0:128], xf[:, k, c * 128:(c + 1) * 128], self.ident, [bxf, self.b_c], [self.bps[bk]])
                if k % 2 == 0:
                    self.cp(yt[:, c, k * 128:(k + 1) * 128], self.ps[bk][:, 0:128], [self.bps[bk]], [], acc=[byt])
                else:
                    self.act(yt[:, c, k * 128:(k + 1) * 128], self.ps[bk][:, 0:128], AF.Copy, [self.bps[bk]], [], acc=[byt])
        t = self.ld(self.y[blk * TB:(blk + 1) * TB, :].rearrange("(c p) d -> p c d", p=128), yt, [byt], [], acc=[self.b_y])
        self.out_toks.append(t)
        self.S.barrier()

    def build(self):
        self.out_toks = []
        self.setup()
        self.prepass()
        for stage in self.plan:
            kind = stage[0]
            if kind == "ssd":
                li = stage[1]
                self.mem_kv(li)
                for blk in range(self.NB):
                    self.load_h(blk)
                    self.ssd_block(li, blk)
                    if "nomem" not in stage:
                        self.mem_attn(li)
                    if "noffn" not in stage:
                        self.ffn(li)
                    self.store_h(blk)
            elif kind == "kv":
                for blk in range(self.NB):
                    self.load_h(blk)
                    self.kv_block(blk)
            elif kind == "dil":
                li = stage[1]
                self.mem_kv(li)
                for blk in range(self.NB):
                    self.load_h(blk)
                    self.q_block(li, blk)
                self.dil_attn()
                for blk in range(self.NB):
                    self.load_h(blk)
                    self.ld(self.xn[:], self.OTs[:, :, blk * TB:(blk + 1) * TB].rearrange("k p t -> p k t"),
                            [self.b_O], [self.b_xn])
                    self.linear_fm(self.dil_w_o[li - 2], 0, D, KC, lambda k: self.xn[:, k, :], [self.b_xn], self.add_to_h)
                    self.S.barrier()
                    self.mem_attn(li)
                    self.ffn(li)
                    self.store_h(blk)
            elif kind == "ffn":
                li = stage[1]
                for blk in range(self.NB):
                    self.load_h(blk)
                    self.ffn(li)
                    self.store_h(blk)
            elif kind == "mem":
                li = stage[1]
                self.mem_kv(li)
                for blk in range(self.NB):
                    self.load_h(blk)
                    self.mem_attn(li)
                    self.store_h(blk)
        for blk in range(self.NB):
            self.load_h(blk)
            self.final_block(blk)
        self.S.wait_all("sp", self.out_toks)


FULL_PLAN = [("ssd", 0), ("ssd", 1), ("kv",), ("dil", 2), ("dil", 3)]


def host_params(inp):
    f = np.float32
    fm = lambda v: np.ascontiguousarray(np.asarray(v, f).reshape(KC, 128).T)
    gl = [fm(inp["norm_mix"][i]) for i in range(4)] + [fm(inp["norm_mem"][i]) for i in range(4)] + \
         [fm(inp["norm_ffn"][i]) for i in range(4)] + [fm(inp["kv_norm"]), fm(inp["mem_src_norm"]), fm(inp["norm_final"])]
    gains = np.ascontiguousarray(np.stack(gl, axis=1))
    cw = np.asarray(inp["ssd_conv_w"], f)
    convw = np.ascontiguousarray(cw.reshape(2, 4, 48, 128).transpose(3, 0, 2, 1))
    convb = np.ascontiguousarray(np.asarray(inp["ssd_conv_b"], f).reshape(2, 48, 128).transpose(2, 0, 1))
    rowp = np.ascontiguousarray(np.stack([inp["ssd_dt_bias"], inp["ssd_a_log"], inp["ssd_d"]], axis=1).astype(f))
    ssdnw = np.ascontiguousarray(np.asarray(inp["ssd_norm"], f).reshape(2, 32, 128).transpose(2, 0, 1))
    i = np.arange(128)
    s, t = i[:, None], i[None, :]
    cst = np.stack([np.eye(128), (s <= t), (s > t), (s == 127) * np.ones((128, 128)),
                    (s == (t + 64) % 128), (s >= t)], axis=1).astype(f)
    half = 64
    invf = (10000.0 ** (-np.arange(half, dtype=np.float32) / half)).astype(f)
    invf2 = np.stack([np.concatenate([invf, invf]), np.concatenate([-np.ones(half, f), np.ones(half, f)])], axis=1)
    return dict(gains=gains, convw=convw, convb=convb, rowp=rowp, ssdnw=ssdnw, cst=np.ascontiguousarray(cst),
                invf=np.ascontiguousarray(invf2.astype(f)))


WNAMES = {"ssd_w_in": "ssd_w_in", "ssd_w_out": "ssd_w_out", "w_kv_shared": "w_kv_shared", "dil_w_q": "dil_w_q",
          "dil_w_o": "dil_w_o", "mem_w_q": "mem_w_q", "mem_w_kv": "mem_w_kv", "mem_w_o": "mem_w_o",
          "ffn_w_in": "ffn_w_in", "ffn_w_out": "ffn_w_out"}

_CACHE = {}


def run(inp, T, plan, batches, n_cores):
    key = (T, tuple(plan))
    if key not in _CACHE:
        _CACHE[key] = Prog(T, plan)
    prog = _CACHE[key]
    hp = host_params(inp)
    ws = {n: np.ascontiguousarray(np.asarray(inp[n], np.float32)) for n in WNAMES}
    maps = []
    for c in range(n_cores):
        b = batches[c]
        m = dict(hp)
        m["x"] = np.ascontiguousarray(np.asarray(inp["x"][b, :T], np.float32))
        m["mem"] = np.ascontiguousarray(np.asarray(inp["mem"][b], np.float32))
        m["pos"] = np.ascontiguousarray(np.asarray(inp["positions"][b, :T], np.int32))
        m.update(ws)
        maps.append(m)
    res = run_bass_kernel_spmd(prog.nc, maps, core_ids=list(range(n_cores)))
    return [r["y"] for r in res.results]


def kernel(**inputs):
    T = 8192
    ys = run(inputs, T, FULL_PLAN, [0, 1], 2)
    return np.stack([ys[0], ys[1]], axis=0).astype(np.float32)
```

```python
import math
from contextlib import ExitStack
import numpy as np
import concourse.bass as bass
import concourse.mybir as mybir
from concourse.bass_utils import run_bass_kernel_spmd

F32 = mybir.dt.float32
BF16 = mybir.dt.bfloat16
I32 = mybir.dt.int32
AF = mybir.ActivationFunctionType
ALU = mybir.AluOpType

ENG = ("pe", "act", "dve", "pool", "sp")

D = 2048
KC = 16
DI = 4096
NH = 64
NG = 8
DFF = 5632
EPS = 1e-6
DILS = (1, 4, 16)
TB = 512
PK = 4


class Buf:
    __slots__ = ("name", "w", "r")

    def __init__(self, name=""):
        self.name = name
        self.w = {}
        self.r = []


class Sched:
    NDMA = 8

    def __init__(self, nc, stack):
        self.nc = nc
        self.ops = {e: [] for e in ENG}
        self.cnt = {e: 0 for e in ENG}
        self.sem = {e: stack.enter_context(nc.semaphore("s_" + e)) for e in ENG}
        self.dsem = {e: [stack.enter_context(nc.semaphore("d_%s%d" % (e, i))) for i in range(self.NDMA)]
                     for e in ("sp", "pool", "act")}
        self.dcnt = {e: [0] * self.NDMA for e in self.dsem}
        self.drr = {e: 0 for e in self.dsem}
        self.seen = {e: {} for e in ENG}
        self.semobj = {}
        for e in ENG:
            self.semobj[("e", e)] = self.sem[e]
        for e in self.dsem:
            for i, s in enumerate(self.dsem[e]):
                self.semobj[("d", e, i)] = s

    def _deps(self, eng, reads, writes, acc=()):
        deps = {}

        def add(tok):
            k, v = tok
            if deps.get(k, 0) < v:
                deps[k] = v
        for b in reads:
            for t in b.w.items():
                add(t)
        for b in writes:
            for t in b.w.items():
                add(t)
            for t in b.r:
                add(t)
        for b in acc:
            for t in b.r:
                add(t)
        waits = []
        seen = self.seen[eng]
        for k, v in deps.items():
            if eng == "pe" and k == ("e", "pe"):
                continue
            if seen.get(k, 0) >= v:
                continue
            seen[k] = v
            waits.append((k, v))
        return waits

    def _mark(self, tok, reads, writes, acc=()):
        for b in reads:
            b.r = [t for t in b.r if t[0] != tok[0]]
            b.r.append(tok)
        for b in writes:
            b.w = {tok[0]: tok[1]}
            b.r = []
        for b in acc:
            b.w[tok[0]] = tok[1]
            b.r = []

    def op(self, eng, fn, reads=(), writes=(), acc=()):
        waits = self._deps(eng, reads, writes, acc)
        self.cnt[eng] += 1
        tok = (("e", eng), self.cnt[eng])
        self.ops[eng].append((waits, fn, (("e", eng), 1)))
        self._mark(tok, reads, writes, acc)
        return tok

    def dma(self, q, fn, reads=(), writes=(), acc=()):
        i = self.drr[q]
        self.drr[q] = (i + 1) % self.NDMA
        key = ("d", q, i)
        waits = self._deps(q, reads, writes, acc)
        prev = self.dcnt[q][i]
        if prev and self.seen[q].get(key, 0) < prev:
            self.seen[q][key] = prev
            waits.append((key, prev))
        self.dcnt[q][i] = prev + 16
        tok = (key, prev + 16)
        self.ops[q].append((waits, fn, (key, 16)))
        self._mark(tok, reads, writes, acc)
        return tok

    def barrier(self):
        toks = [(("e", e), self.cnt[e]) for e in ENG if self.cnt[e]]
        for q in self.dsem:
            for i in range(self.NDMA):
                if self.dcnt[q][i]:
                    toks.append((("d", q, i), self.dcnt[q][i]))
        for e in ENG:
            if e != "pool":
                self.wait_all(e, toks)

    def wait_all(self, eng, toks):
        waits = []
        for k, v in toks:
            if eng == "pe" and k == ("e", "pe"):
                continue
            if self.seen[eng].get(k, 0) < v:
                self.seen[eng][k] = v
                waits.append((k, v))
        if waits:
            self.ops[eng].append((waits, None, None))

    def emit(self, block):
        engs = {"pe": block.tensor, "act": block.scalar, "dve": block.vector, "pool": block.gpsimd,
                "sp": block.sync}
        for e in ENG:
            ops = self.ops[e]
            if not ops:
                continue

            def body(engine, ops=ops):
                for waits, fn, inc in ops:
                    for k, v in waits:
                        engine.wait_ge(self.semobj[k], v)
                    if fn is not None:
                        ins = fn(engine)
                        ins.then_inc(self.semobj[inc[0]], inc[1])
            engs[e](body)


def bc(ap, shape):
    return ap.broadcast_to(list(shape))


class Prog:
    ARENA = 25620

    def __init__(self, T, plan):
        self.T = T
        self.NB = T // TB
        self.plan = plan
        nc = self.nc = bass.Bass("TRN2", target_bir_lowering=False)
        st = self.st = ExitStack()
        din = lambda n, s, dt=F32: nc.dram_tensor(n, list(s), dt, kind="ExternalInput").ap()
        dint = lambda n, s, dt=F32: nc.dram_tensor(n, list(s), dt, kind="Internal").ap()
        self.x = din("x", [T, D])
        self.mem = din("mem", [256, D])
        self.pos = din("pos", [T], I32)
        self.gains = din("gains", [128, 15, KC])
        self.convw = din("convw", [128, 2, 48, 4])
        self.convb = din("convb", [128, 2, 48])
        self.rowp = din("rowp", [2, 3, 64])
        self.ssdnw = din("ssdnw", [128, 2, 32])
        self.cst = din("cst", [128, 6, 128])
        self.invf = din("invf", [128, 2])
        self.ssd_w_in = din("ssd_w_in", [2, D, 10304])
        self.ssd_w_out = din("ssd_w_out", [2, DI, D])
        self.w_kv = din("w_kv_shared", [D, 12288])
        self.dil_w_q = din("dil_w_q", [2, D, 6144])
        self.dil_w_o = din("dil_w_o", [2, D, D])
        self.mem_w_q = din("mem_w_q", [4, D, 512])
        self.mem_w_kv = din("mem_w_kv", [4, D, 1024])
        self.mem_w_o = din("mem_w_o", [4, 512, D])
        self.ffn_w_in = din("ffn_w_in", [4, D, 2 * DFF])
        self.ffn_w_out = din("ffn_w_out", [4, DFF, D])
        self.y = nc.dram_tensor("y", [T, D], F32, kind="ExternalOutput").ap()
        self.hs = dint("hs", [KC, 128, T])
        self.cosd = dint("cosd", [128, T])
        self.sind = dint("sind", [128, T])
        self.memTd = dint("memTd", [128, KC, 256], BF16)
        self.KTs = dint("KTs", [48, 128, T], BF16)
        self.VTs = dint("VTs", [48, 128, T], BF16)
        self.QTs = dint("QTs", [48, 128, T], BF16)
        self.OTs = dint("OTs", [KC, 128, T], BF16)
        self.b_hs = [Buf() for _ in range(self.NB)]
        self.b_tab = Buf()
        self.b_memTd = Buf()
        self.b_KV = Buf()
        self.b_Q = Buf()
        self.b_O = Buf()
        self.b_y = Buf()

        S = self.S = Sched(nc, st)
        sb = self.sb = lambda n, s, dt: st.enter_context(nc.sbuf_tensor(n, list(s), dt))
        self.ps = [st.enter_context(nc.psum_tensor("ps%d" % i, [128, 512], F32)) for i in range(8)]
        self.bps = [Buf("ps%d" % i) for i in range(8)]
        self.pi = 0
        self.cstf = sb("cstf", [128, 6, 128], F32)
        self.cstb = sb("cstb", [128, 6, 128], BF16)
        self.onesb = sb("onesb", [128, 128], BF16)
        self.gn = sb("gn", [128, 15, KC], F32)
        self.cw = sb("cw", [128, 2, 48, 4], F32)
        self.cb = sb("cb", [128, 2, 48], F32)
        self.rp = sb("rp", [128, 2, 3, 64], F32)
        self.aneg = sb("aneg", [128, 2, 64], F32)
        self.snw = sb("snw", [128, 2, 32], F32)
        self.ivf = sb("ivf", [128, 2], F32)
        self.epsb = sb("epsb", [128, 1], F32)
        self.b_c = Buf("consts")
        self.hT = sb("hT", [128, KC, TB], F32)
        self.b_hT = Buf("hT")
        self.xn = sb("xn", [128, KC, TB], BF16)
        self.b_xn = Buf("xn")
        self.rstd = sb("rstd", [128, TB], F32)
        self.b_rstd = Buf("rstd")
        self.panels = [sb("pan%d" % i, [128, PK, 512], BF16) for i in range(4)]
        self.bpan = [Buf("pan%d" % i) for i in range(4)]
        self.pani = 0
        self.KmT = sb("KmT", [128, 4, 256], BF16)
        self.Vm = sb("Vm", [128, 2, 512], BF16)
        self.b_kvm = Buf()
        self.Hst = sb("Hst", [128, DI], F32)
        self.Hb = sb("Hb", [128, DI], BF16)
        self.b_H = [Buf() for _ in range(NG)]
        self.b_Hb = [Buf() for _ in range(NG)]
        self.halo = sb("halo", [128, 48, 3], F32)
        self.b_halo = Buf()
        self.arena = sb("arena", [128, self.ARENA], F32)
        self.b_ar = Buf("arena")

        self.build()
        with nc.Block() as block:
            S.emit(block)
        st.close()

    def bank(self):
        i = self.pi % 8
        self.pi += 1
        return i

    def carve(self, off_bytes, shape, dt, base="arena"):
        esz = 4 if dt in (F32, I32) else 2
        n = 1
        for s in shape[1:]:
            n *= s
        nw = (n * esz + 3) // 4
        assert off_bytes % 4 == 0
        if base == "arena":
            assert off_bytes + n * esz <= self.ARENA * 4, (off_bytes, shape)
            flat = self.arena[:, off_bytes // 4: off_bytes // 4 + nw]
        else:
            assert off_bytes + n * esz <= KC * TB * 2, (off_bytes, shape)
            flat = self.xn[:].rearrange("p a b -> p (a b)").bitcast(F32)[:, off_bytes // 4: off_bytes // 4 + nw]
        if dt != F32:
            flat = flat.bitcast(dt)
        flat = flat[:, 0:n]
        if len(shape) == 2:
            return flat
        if len(shape) == 3:
            return flat.rearrange("p (a b) -> p a b", b=shape[2])
        if len(shape) == 4:
            return flat.rearrange("p (a b c) -> p a b c", b=shape[2], c=shape[3])
        raise ValueError

    def mm(self, out, lhsT, rhs, start, stop, R, W):
        return self.S.op("pe", lambda e: e.matmul(out, lhsT=lhsT, rhs=rhs, start=start, stop=stop), reads=R, writes=W)

    def tr(self, out, in_, ident, R, W):
        return self.S.op("pe", lambda e: e.transpose(out, in_, ident), reads=R, writes=W)

    def act(self, out, in_, func, R, W, acc=(), **kw):
        return self.S.op("act", lambda e: e.activation(out=out, in_=in_, func=func, **kw), reads=R, writes=W, acc=acc)

    def amul(self, out, in_, mul, R, W, acc=()):
        return self.S.op("act", lambda e: e.mul(out=out, in_=in_, mul=mul), reads=R, writes=W, acc=acc)

    def tt(self, out, in0, in1, op, R, W, eng="dve", acc=()):
        return self.S.op(eng, lambda e: e.tensor_tensor(out=out, in0=in0, in1=in1, op=op), reads=R, writes=W, acc=acc)

    def ts(self, out, in0, s1, s2, op0, op1, R, W, eng="dve", acc=()):
        if s2 is None:
            return self.S.op(eng, lambda e: e.tensor_scalar(out=out, in0=in0, scalar1=s1, scalar2=None, op0=op0),
                             reads=R, writes=W, acc=acc)
        return self.S.op(eng, lambda e: e.tensor_scalar(out=out, in0=in0, scalar1=s1, scalar2=s2, op0=op0, op1=op1),
                         reads=R, writes=W, acc=acc)

    def stt(self, out, in0, scalar, in1, op0, op1, R, W, eng="dve", acc=()):
        return self.S.op(eng, lambda e: e.scalar_tensor_tensor(out=out, in0=in0, scalar=scalar, in1=in1, op0=op0, op1=op1),
                         reads=R, writes=W, acc=acc)

    def cp(self, out, in_, R, W, eng="dve", acc=()):
        return self.S.op(eng, lambda e: e.tensor_copy(out=out, in_=in_), reads=R, writes=W, acc=acc)

    def ld(self, out, in_, R, W, q="sp", acc=(), **kw):
        return self.S.dma(q, lambda e: e.dma_start(out=out, in_=in_, **kw), reads=R, writes=W, acc=acc)

    def panel(self, w2d, k0, nk, c0, ncols):
        i = self.pani % len(self.panels)
        self.pani += 1
        pan, b = self.panels[i], self.bpan[i]
        step = 4
        for kk in range(0, nk, step):
            n = min(step, nk - kk)
            src = w2d[(k0 + kk) * 128:(k0 + kk + n) * 128, c0:c0 + ncols].rearrange("(k p) n -> p k n", p=128)
            self.ld(pan[:, kk:kk + n, 0:ncols], src, [], [], q="pool", acc=[b])
        return pan, b

    def load_h(self, blk):
        self.ld(self.hT[:], self.hs[:, :, blk * TB:(blk + 1) * TB].rearrange("k p t -> p k t"),
                [self.b_hs[blk]], [self.b_hT])

    def store_h(self, blk):
        self.ld(self.hs[:, :, blk * TB:(blk + 1) * TB].rearrange("k p t -> p k t"), self.hT[:],
                [self.b_hT], [self.b_hs[blk]])

    def rms(self, gi, out=None, ob=None):
        out = self.xn if out is None else out
        ob = self.b_xn if ob is None else ob
        sq = self.carve(0, [128, KC, TB], BF16)
        bsq = Buf()
        for k in range(KC):
            self.act(sq[:, k, :], self.hT[:, k, :], AF.Square, [self.b_hT], [], acc=[bsq, self.b_ar])
        bk = self.bank()
        for k in range(KC):
            self.mm(self.ps[bk][:], self.onesb[:], sq[:, k, :], k == 0, k == KC - 1, [bsq, self.b_c], [self.bps[bk]])
        self.act(self.rstd[:], self.ps[bk][:], AF.Ln, [self.bps[bk], self.b_c], [self.b_rstd], scale=1.0 / D, bias=self.epsb[:])
        self.act(self.rstd[:], self.rstd[:], AF.Exp, [self.b_rstd], [self.b_rstd], scale=-0.5)
        for k in range(KC):
            self.stt(out[:, k, :], self.hT[:, k, :], self.gn[:, gi, k:k + 1], self.rstd[:], ALU.mult, ALU.mult,
                     [self.b_hT, self.b_rstd, self.b_c], [], acc=[ob])

    def linear_fm(self, w2d, c0, ncols, nk, rhs_of_k, R, consume, xcols=TB, defer=False):
        pending = []
        for cb in range(0, ncols, 512):
            nc_ = min(512, ncols - cb)
            nm = nc_ // 128
            banks = [self.bank() for _ in range(nm)]
            for k0 in range(0, nk, PK):
                n = min(PK, nk - k0)
                pan, pb = self.panel(w2d, k0, n, c0 + cb, nc_)
                for m in range(nm):
                    for k in range(n):
                        self.mm(self.ps[banks[m]][:, 0:xcols], pan[:, k, m * 128:(m + 1) * 128], rhs_of_k(k0 + k),
                                (k0 + k) == 0, (k0 + k) == nk - 1, [pb] + R, [self.bps[banks[m]]])
            for args in pending:
                consume(*args)
            pending = [((cb // 128) + m, self.ps[banks[m]][:, 0:xcols], self.bps[banks[m]]) for m in range(nm)]
            if not defer:
                for args in pending:
                    consume(*args)
                pending = []
        for args in pending:
            consume(*args)

    def linear_tm(self, w2d, c0, ncols, nk, lhs_of, nch, R, consume):
        banks = [self.bank() for _ in range(nch)]
        for k0 in range(0, nk, PK):
            n = min(PK, nk - k0)
            pan, pb = self.panel(w2d, k0, n, c0, ncols)
            for c in range(nch):
                for k in range(n):
                    self.mm(self.ps[banks[c]][:, 0:ncols], lhs_of(k0 + k, c), pan[:, k, 0:ncols],
                            (k0 + k) == 0, (k0 + k) == nk - 1, [pb] + R, [self.bps[banks[c]]])
        for c in range(nch):
            consume(c, self.ps[banks[c]][:, 0:ncols], self.bps[banks[c]])

    def add_to_h(self, m, ps, pb):
        self.tt(self.hT[:, m, :], ps, self.hT[:, m, :], ALU.add, [pb, self.b_hT], [], acc=[self.b_hT])

    def setup(self):
        S = self.S
        self.ld(self.cstf[:], self.cst, [], [], acc=[self.b_c])
        self.ld(self.gn[:], self.gains, [], [], acc=[self.b_c])
        self.ld(self.cw[:], self.convw, [], [], acc=[self.b_c])
        self.ld(self.cb[:], self.convb, [], [], acc=[self.b_c])
        self.ld(self.snw[:], self.ssdnw, [], [], acc=[self.b_c])
        self.ld(self.ivf[:], self.invf, [], [], acc=[self.b_c])
        self.ld(self.rp[:].rearrange("p a b c -> p (a b c)"),
                self.rowp.rearrange("a b c -> (a b c)").partition_broadcast(128), [], [], acc=[self.b_c])
        S.barrier()
        self.cp(self.cstb[:], self.cstf[:], [self.b_c], [], acc=[self.b_c])
        S.op("dve", lambda e: e.memset(self.onesb[:], 1.0), [], [], acc=[self.b_c])
        S.op("dve", lambda e: e.memset(self.epsb[:], EPS), [], [], acc=[self.b_c])
        self.act(self.aneg[:], self.rp[:, :, 1, :], AF.Exp, [self.b_c], [], acc=[self.b_c])
        S.barrier()
        self.ts(self.aneg[:], self.aneg[:], -1.0, None, ALU.mult, None, [self.b_c], [], acc=[self.b_c])
        self.ident = self.cstf[:, 0, :]
        self.identb = self.cstb[:, 0, :]
        self.tri = self.cstf[:, 1, :]
        self.U = self.cstf[:, 2, :]
        self.sel127 = self.cstf[:, 3, :]
        self.swapb = self.cstb[:, 4, :]
        S.barrier()

    def prepass(self):
        NB = self.NB
        xt = self.carve(0, [128, 4, D], F32)
        A = [self.b_ar]
        for blk in range(NB):
            self.ld(xt, self.x[blk * TB:(blk + 1) * TB, :].rearrange("(c p) d -> p c d", p=128), [], A)
            for c in range(4):
                for k in range(KC):
                    bk = self.bank()
                    self.tr(self.ps[bk][:, 0:128], xt[:, c, k * 128:(k + 1) * 128], self.ident, A + [self.b_c], [self.bps[bk]])
                    if (c * KC + k) % 2 == 0:
                        self.act(self.hT[:, k, c * 128:(c + 1) * 128], self.ps[bk][:, 0:128], AF.Copy, [self.bps[bk]], [], acc=[self.b_hT])
                    else:
                        self.cp(self.hT[:, k, c * 128:(c + 1) * 128], self.ps[bk][:, 0:128], [self.bps[bk]], [], acc=[self.b_hT])
            self.store_h(blk)
            pi_ = self.carve(40960, [128, TB], I32)
            ang = self.carve(43008, [128, TB], F32)
            kf = self.carve(45056, [128, TB], F32)
            r = self.carve(47104, [128, TB], F32)
            sc = self.carve(49152, [128, 2, TB], F32)
            B = [Buf("rope")]
            self.ld(pi_, self.pos[blk * TB:(blk + 1) * TB].partition_broadcast(128), [], B)
            self.cp(ang, pi_, B, B)
            self.ts(ang, ang, self.ivf[:, 0:1], None, ALU.mult, None, B + [self.b_c], B)
            MAGIC = 12582912.0
            HI = 6.28125
            LO = float(2 * np.pi - 6.28125)
            for j, shift in enumerate((0.5 * np.pi, 0.0)):
                self.ts(kf, ang, float(shift), float(1 / (2 * np.pi)), ALU.add, ALU.mult, B, B)
                self.ts(kf, kf, MAGIC, None, ALU.add, None, B, B)
                self.ts(kf, kf, MAGIC, None, ALU.subtract, None, B, B)
                self.ts(r, ang, float(shift), None, ALU.add, None, B, B)
                self.stt(r, kf, -HI, r, ALU.mult, ALU.add, B, B)
                self.stt(r, kf, -LO, r, ALU.mult, ALU.add, B, B)
                self.ts(r, r, float(-np.pi), float(np.pi), ALU.max, ALU.min, B, B)
                self.act(sc[:, j, :], r, AF.Sin, B, B)
            self.ts(sc[:, 1, :], sc[:, 1, :], self.ivf[:, 1:2], None, ALU.mult, None, B + [self.b_c], B)
            self.ld(self.cosd[:, blk * TB:(blk + 1) * TB], sc[:, 0, :], B, [], acc=[self.b_tab])
            self.ld(self.sind[:, blk * TB:(blk + 1) * TB], sc[:, 1, :], B, [], acc=[self.b_tab])
            self.S.barrier()
        mt = self.carve(0, [128, 2, D], F32)
        junk = self.carve(16384, [128, D], F32)
        ssq = self.carve(24576, [128, 4], F32)
        memT = self.carve(32768, [128, KC, 256], BF16)
        self.ld(mt, self.mem.rearrange("(c p) d -> p c d", p=128), [], A)
        for c in range(2):
            self.S.op("dve", lambda e, c=c: e.memset(ssq[:, c:c + 1], 0.0), A, A)
            self.act(junk, mt[:, c, :], AF.Square, A, A, accum_out=ssq[:, c:c + 1])
        self.act(ssq[:, 2:4], ssq[:, 0:2], AF.Ln, A + [self.b_c], A, scale=1.0 / D, bias=self.epsb[:])
        self.act(ssq[:, 2:4], ssq[:, 2:4], AF.Exp, A, A, scale=-0.5)
        bm = Buf()
        for c in range(2):
            self.ts(mt[:, c, :], mt[:, c, :], ssq[:, 2 + c:3 + c], None, ALU.mult, None, A, A)
            for k in range(KC):
                bk = self.bank()
                self.tr(self.ps[bk][:, 0:128], mt[:, c, k * 128:(k + 1) * 128], self.ident, A + [self.b_c], [self.bps[bk]])
                self.ts(memT[:, k, c * 128:(c + 1) * 128], self.ps[bk][:, 0:128], self.gn[:, 13, k:k + 1], None,
                        ALU.mult, None, [self.bps[bk], self.b_c], [], acc=[bm])
        self.ld(self.memTd, memT, [bm], [self.b_memTd])
        self.S.barrier()

    def mem_kv(self, li):
        w = self.mem_w_kv[li]
        memT = self.carve(32768, [128, KC, 256], BF16)
        bm = Buf()
        self.ld(memT, self.memTd, [self.b_memTd], [bm])

        def consume(m, ps, pb):
            self.cp(self.KmT[:, m, :], ps, [pb], [], acc=[self.b_kvm])
        self.linear_fm(w, 0, 512, KC, lambda k: memT[:, k, :], [bm], consume, xcols=256)

        def cv(c, ps, pb):
            self.cp(self.Vm[:, c, :], ps, [pb], [], acc=[self.b_kvm])
        self.linear_tm(w, 512, 512, KC, lambda k, c: memT[:, k, c * 128:(c + 1) * 128], 2, [bm], cv)
        self.S.barrier()

    def mem_attn(self, li):
        self.rms(4 + li)
        self.S.barrier()
        qT = self.carve(0, [128, 4, TB], BF16)
        oT = self.carve(4096, [128, 4, TB], BF16)
        pT = [self.carve(8192 + i * 2048, [128, 2, TB], BF16) for i in range(2)]
        rden = [self.carve(12288 + i * 2048, [128, TB], F32) for i in range(2)]
        bq, bo_ = Buf(), Buf()
        bpp = [Buf(), Buf()]
        brr = [Buf(), Buf()]
        sc = 128 ** -0.5

        def cq(m, ps, pb):
            self.amul(qT[:, m, :], ps, sc, [pb], [], acc=[bq])
        self.linear_fm(self.mem_w_q[li], 0, 512, KC, lambda k: self.xn[:, k, :], [self.b_xn], cq)
        for hd in range(4):
            p_ = pT[hd % 2]
            bp = bpp[hd % 2]
            first = True
            for mc in range(2):
                bk = self.bank()
                self.mm(self.ps[bk][:], self.KmT[:, hd, mc * 128:(mc + 1) * 128], qT[:, hd, :], True, True,
                        [bq, self.b_kvm], [self.bps[bk]])
                self.act(p_[:, mc, :], self.ps[bk][:], AF.Exp, [self.bps[bk]], [bp] if mc == 0 else [], acc=[] if mc == 0 else [bp])
            bo, bd = self.bank(), self.bank()
            for mc in range(2):
                self.mm(self.ps[bo][:], self.Vm[:, mc, hd * 128:(hd + 1) * 128], p_[:, mc, :], mc == 0, mc == 1,
                        [bp, self.b_kvm], [self.bps[bo]])
            for mc in range(2):
                self.mm(self.ps[bd][:], self.onesb[:], p_[:, mc, :], mc == 0, mc == 1, [bp, self.b_c], [self.bps[bd]])
            rd = rden[hd % 2]
            brd = brr[hd % 2]
            self.S.op("dve", lambda e, bd=bd, rd=rd: e.reciprocal(out=rd, in_=self.ps[bd][:]), [self.bps[bd]], [brd])
            self.tt(oT[:, hd, :], self.ps[bo][:], rd, ALU.mult, [self.bps[bo], brd], [], acc=[bo_])
        self.linear_fm(self.mem_w_o[li], 0, D, 4, lambda k: oT[:, k, :], [bo_], self.add_to_h)
        self.S.barrier()

    def ffn(self, li):
        self.rms(8 + li)
        self.S.barrier()
        hid = self.carve(0, [128, 44, TB], BF16)
        sg = self.carve(45056, [128, 8, TB], BF16)
        bhid = Buf()
        bsg8 = [Buf() for _ in range(8)]
        w = self.ffn_w_in[li]
        for j in range(11):
            par = (j % 2) * 4
            bsg = bsg8[par:par + 4]

            def cg(m, ps, pb, bsg=bsg, par=par):
                self.act(sg[:, par + m % 4, :], ps, AF.Silu, [pb], [bsg[m % 4]])
            self.linear_fm(w, j * 512, 512, KC, lambda k: self.xn[:, k, :], [self.b_xn], cg)

            def cu(m, ps, pb, bsg=bsg, par=par, j=j):
                self.tt(hid[:, j * 4 + m % 4, :], ps, sg[:, par + m % 4, :], ALU.mult, [pb, bsg[m % 4]], [], acc=[bhid])
            self.linear_fm(w, DFF + j * 512, 512, KC, lambda k: self.xn[:, k, :], [self.b_xn], cu)
        self.linear_fm(self.ffn_w_out[li], 0, D, 44, lambda k: hid[:, k, :], [bhid], self.add_to_h)
        self.S.barrier()

    def ssd_block(self, li, blk):
        S = self.S
        C_ = [self.b_c]
        w = self.ssd_w_in[li]
        self.rms(li)
        S.barrier()
        xT = self.carve(0, [128, 32, TB], BF16)
        BT = self.carve(32768, [128, NG, TB], BF16)
        CT = self.carve(40960, [128, NG, TB], BF16)
        zs = self.carve(49152, [128, 4, DI], BF16)
        o = 81920
        ubuf = [self.carve(o + i * 2064, [128, 516], F32) for i in range(3)]
        o += 3 * 2064
        acc = [self.carve(o + i * 2048, [128, TB], F32) for i in range(3)]
        o += 3 * 2048
        small = self.carve(o, [128, 4, 8, 64], F32)
        o += 8192
        assert o <= self.ARENA * 4, o
        b_x, b_B, b_C, b_z, b_sm = Buf("xT"), Buf("BT"), Buf("CT"), Buf("zs"), Buf("small")
        if blk == 0:
            S.op("dve", lambda e: e.memset(self.halo[:], 0.0), [], [self.b_halo])
            S.op("dve", lambda e: e.memset(self.Hst[:], 0.0), [], self.b_H)
            S.op("dve", lambda e: e.memset(self.Hb[:], 0.0), [], self.b_Hb)
        ci = [0]
        bcv = [Buf(), Buf(), Buf()]

        def cconv(m, ps, pb):
            i = ci[0] % 3
            ci[0] += 1
            u, a, b = ubuf[i], acc[i], bcv[i]
            cwv = self.cw[:, li, m, :]
            self.cp(u[:, 0:3], self.halo[:, m, :], [self.b_halo], [b])
            self.act(u[:, 3:3 + TB], ps, AF.Copy, [pb], [], acc=[b])
            self.act(a, ps, AF.Identity, [pb] + C_, [], acc=[b], scale=cwv[:, 3:4], bias=self.cb[:, li, m:m + 1])
            self.cp(self.halo[:, m, :], u[:, TB:TB + 3], [b], [], acc=[self.b_halo])
            for j in range(3):
                self.stt(a, u[:, j:j + TB], cwv[:, j:j + 1], a, ALU.mult, ALU.add, [b] + C_, [], acc=[b])
            if m < 32:
                self.act(xT[:, m, :], a, AF.Silu, [b], [], acc=[b_x])
            elif m < 40:
                self.act(BT[:, m - 32, :], a, AF.Silu, [b], [], acc=[b_B])
            else:
                self.act(CT[:, m - 40, :], a, AF.Silu, [b], [], acc=[b_C])
        self.linear_fm(w, DI, 6144, KC, lambda k: self.xn[:, k, :], [self.b_xn], cconv)
        for j in range(8):
            def cz(c, ps, pb, j=j):
                self.act(zs[:, c, j * 512:(j + 1) * 512], ps, AF.Silu, [pb], [], acc=[b_z])
            self.linear_tm(w, j * 512, 512, KC, lambda k, c: self.xn[:, k, c * 128:(c + 1) * 128], 4, [self.b_xn], cz)
        def cdt(c, ps, pb):
            dt, dtA, acum, ea, cd, wend, tmp = [small[:, c, i, :] for i in range(7)]
            M = [b_sm]
            self.tt(tmp, ps, self.rp[:, li, 0, :], ALU.add, [pb] + C_, M)
            self.act(tmp, tmp, AF.Exp, M, M)
            self.act(dt, tmp, AF.Ln, M, M, bias=1.0)
            self.tt(dtA, dt, self.aneg[:, li, :], ALU.mult, M + C_, M)
            bk = self.bank()
            self.mm(self.ps[bk][:, 0:64], self.tri, dtA, True, True, M + C_, [self.bps[bk]])
            self.cp(acum, self.ps[bk][:, 0:64], [self.bps[bk]], M)
            self.act(ea, acum, AF.Exp, M, M)
            bk = self.bank()
            self.mm(self.ps[bk][:, 0:64], self.sel127, acum, True, True, M + C_, [self.bps[bk]])
            self.act(cd, self.ps[bk][:, 0:64], AF.Exp, [self.bps[bk]], M)
            self.tt(tmp, self.ps[bk][:, 0:64], acum, ALU.subtract, [self.bps[bk]] + M, M)
            self.act(tmp, tmp, AF.Exp, M, M)
            self.tt(wend, tmp, dt, ALU.mult, M, M)
        self.linear_tm(w, 10240, 64, KC, lambda k, c: self.xn[:, k, c * 128:(c + 1) * 128], 4, [self.b_xn], cdt)
        S.barrier()
        Rgb = self.carve(81920, [128, 8, 128], BF16)
        dec2 = [self.carve(83968 + i * 4096, [128, 8, 128], F32) for i in range(2)]
        xD2 = [self.carve(92160 + i * 1024, [128, 512], BF16) for i in range(2)]
        rsb = self.rstd[:].bitcast(BF16)
        xw2 = [rsb[:, i * 512:(i + 1) * 512] for i in range(2)]
        o2 = 0

        def xc(shape, dt):
            nonlocal o2
            n = 1
            for s_ in shape[1:]:
                n *= s_
            v = self.carve(o2, shape, dt, base="xn")
            o2 += ((n * (4 if dt == F32 else 2) + 3) // 4) * 4
            return v
        CBm2 = [xc([128, 128], F32) for _ in range(2)]
        MT2 = [xc([128, 8, 128], BF16) for _ in range(2)]
        xtok2 = [xc([128, 512], BF16) for _ in range(2)]
        xdt2 = [xc([128, 512], BF16) for _ in range(2)]
        Btok2 = [xc([128, 128], BF16) for _ in range(2)]
        t1 = xc([128, 512], F32)
        t2 = xc([128, 512], F32)
        yn2 = [xc([128, 512], BF16) for _ in range(2)]
        ssq = xc([128, 4], F32)
        bR, bY = Buf(), Buf()
        bD2, bCB2, bM2, bX2, bP2, bW2, bN2 = [[Buf(), Buf()] for _ in range(7)]
        Ub = self.cstb[:, 2, :]
        v3 = lambda ap: ap.rearrange("p (a b) -> p a b", b=64)
        items = [(c, g) for c in range(4) for g in range(NG)]
        NI = len(items)

        def A1(i):
            c, g = items[i]
            p = i % 2
            cs = slice(c * 128, (c + 1) * 128)
            dt, dtA, wend = small[:, c, 0, :], small[:, c, 1, :], small[:, c, 5, :]
            hs_ = slice(g * 8, (g + 1) * 8)
            dec, xtok = dec2[p], xtok2[p]
            self.tt(Rgb, bc(self.tri.unsqueeze(1), [128, 8, 128]), bc(dtA[:, hs_].unsqueeze(2), [128, 8, 128]), ALU.mult,
                    [b_sm] + C_, [bR])
            for hh in range(2):
                bk = self.bank()
                self.mm(self.ps[bk][:], Ub, Rgb[:, hh * 4:(hh + 1) * 4, :], True, True, [bR] + C_, [self.bps[bk]])
                self.act(dec[:, hh * 4:(hh + 1) * 4, :], self.ps[bk][:].rearrange("p (a b) -> p a b", b=128), AF.Exp,
                         [self.bps[bk]], [bD2[p]] if hh == 0 else [], acc=[] if hh == 0 else [bD2[p]])
            bk = self.bank()
            self.mm(self.ps[bk][:, 0:128], BT[:, g, cs], CT[:, g, cs], True, True, [b_B, b_C], [self.bps[bk]])
            self.tt(CBm2[p], self.ps[bk][:, 0:128], self.tri, ALU.mult, [self.bps[bk]] + C_, [bCB2[p]])
            bk = self.bank()
            pb16 = self.ps[bk][:].bitcast(BF16)
            for q in range(4):
                self.tr(pb16[:, q * 128:(q + 1) * 128], xT[:, g * 4 + q, cs], self.identb, [b_x] + C_, [self.bps[bk]])
            self.act(xtok, pb16[:, 0:512], AF.Copy, [self.bps[bk]], [bX2[p]])
            bk = self.bank()
            pb16 = self.ps[bk][:].bitcast(BF16)
            self.tr(pb16[:, 0:128], BT[:, g, cs], self.identb, [b_B] + C_, [self.bps[bk]])
            self.act(Btok2[p], pb16[:, 0:128], AF.Copy, [self.bps[bk]], [bW2[p]])
            self.tt(v3(xdt2[p]), v3(xtok), bc(dt[:, hs_].unsqueeze(2), [128, 8, 64]), ALU.mult, [bX2[p], b_sm], [bP2[p]], eng="pool")
            self.tt(v3(xD2[p]), v3(xtok), bc(self.rp[:, li, 2, hs_].unsqueeze(2), [128, 8, 64]), ALU.mult, [bX2[p]] + C_, [], eng="pool", acc=[bP2[p]])
            self.tt(v3(xw2[p]), v3(xtok), bc(wend[:, hs_].unsqueeze(2), [128, 8, 64]), ALU.mult, [bX2[p], b_sm], [], eng="pool", acc=[bW2[p]])

        def A2(i):
            p = i % 2
            self.tt(MT2[p], dec2[p], bc(CBm2[p].unsqueeze(1), [128, 8, 128]), ALU.mult, [bD2[p], bCB2[p]], [bM2[p]])

        st = {}

        def B1(i):
            c, g = items[i]
            p = i % 2
            cs = slice(c * 128, (c + 1) * 128)
            ea, cd = small[:, c, 3, :], small[:, c, 4, :]
            hs_ = slice(g * 8, (g + 1) * 8)
            gs = slice(g * 512, (g + 1) * 512)
            by = self.bank()
            self.mm(self.ps[by][:], self.identb, xD2[p], True, False, [bP2[p]] + C_, [self.bps[by]])
            for j in range(8):
                self.mm(self.ps[by][:, j * 64:(j + 1) * 64], MT2[p][:, j, :], xdt2[p][:, j * 64:(j + 1) * 64], False, j == 7,
                        [bM2[p], bP2[p]], [self.bps[by]])
            bo = self.bank()
            self.mm(self.ps[bo][:], CT[:, g, cs], self.Hb[:, gs], True, True, [b_C, self.b_Hb[g]], [self.bps[bo]])
            bs = self.bank()
            self.mm(self.ps[bs][:], Btok2[p], xw2[p], True, True, [bW2[p]], [self.bps[bs]])
            st[i] = bs
            self.tt(v3(self.Hst[:, gs]), v3(self.Hst[:, gs]), bc(cd[:, hs_].unsqueeze(2), [128, 8, 64]), ALU.mult,
                    [self.b_H[g], b_sm], [self.b_H[g]], eng="pool")
            self.tt(v3(t1), v3(self.ps[bo][:]), bc(ea[:, hs_].unsqueeze(2), [128, 8, 64]), ALU.mult,
                    [self.bps[bo], b_sm], [bY])
            self.tt(t2, self.ps[by][:], t1, ALU.add, [self.bps[by], bY], [bY])
            self.tt(t2, t2, zs[:, c, gs], ALU.mult, [bY, b_z], [bY])
            S.op("dve", lambda e: e.memset(ssq[:, 0:1], 0.0), [bY], [bY])
            self.act(t1, t2, AF.Square, [bY], [bY], accum_out=ssq[:, 0:1])
            self.act(ssq[:, 1:2], ssq[:, 0:1], AF.Ln, [bY] + C_, [bY], scale=1.0 / 512, bias=self.epsb[:])
            self.act(ssq[:, 1:2], ssq[:, 1:2], AF.Exp, [bY], [bY], scale=-0.5)

        def B2(i):
            c, g = items[i]
            p = i % 2
            gs = slice(g * 512, (g + 1) * 512)
            bs = st.pop(i)
            self.ts(yn2[p], t2, ssq[:, 1:2], None, ALU.mult, None, [bY], [bN2[p]])
            self.tt(self.Hst[:, gs], self.Hst[:, gs], self.ps[bs][:], ALU.add, [self.b_H[g], self.bps[bs]], [self.b_H[g]])
            self.act(self.Hb[:, gs], self.Hst[:, gs], AF.Copy, [self.b_H[g]], [self.b_Hb[g]])

        def Cst(i):
            c, g = items[i]
            p = i % 2
            cs = slice(c * 128, (c + 1) * 128)
            bk = self.bank()
            pb16 = self.ps[bk][:].bitcast(BF16)
            for q in range(4):
                self.tr(pb16[:, q * 128:(q + 1) * 128], yn2[p][:, q * 128:(q + 1) * 128], self.identb, [bN2[p]] + C_, [self.bps[bk]])
            for q in range(4):
                sc_ = self.snw[:, li, g * 4 + q:g * 4 + q + 1]
                if q % 2 == 0:
                    self.ts(xT[:, g * 4 + q, cs], pb16[:, q * 128:(q + 1) * 128], sc_, None, ALU.mult, None,
                            [self.bps[bk]] + C_, [], acc=[b_x])
                else:
                    self.amul(xT[:, g * 4 + q, cs], pb16[:, q * 128:(q + 1) * 128], sc_, [self.bps[bk]] + C_, [], acc=[b_x])

        A1(0)
        A1(1)
        A2(0)
        for i in range(NI):
            B1(i)
            if i + 1 < NI:
                A2(i + 1)
            B2(i)
            if i >= 1:
                Cst(i - 1)
            if i + 2 < NI:
                A1(i + 2)
        Cst(NI - 1)
        S.barrier()
        self.linear_fm(self.ssd_w_out[li], 0, D, 32, lambda k: xT[:, k, :], [b_x], self.add_to_h)
        S.barrier()

    def rope_chunk(self, ps, pb, scale, cs_t, btab):
        i = self.rsi % 3
        self.rsi += 1
        qs = self.carve(49152 + i * 1024, [128, TB], BF16)
        ta = self.carve(52224 + i * 2048, [128, TB], F32)
        tb_ = self.carve(58368 + i * 2048, [128, TB], F32)
        ob = self.carve(64512 + i * 1024, [128, TB], BF16)
        b = self.rbuf[i]
        self.amul(qs, ps, scale, [pb], [b])
        self.mm(ps, self.swapb, qs, True, True, [b, self.b_c], [pb])
        self.tt(ta, qs, cs_t[:, 0, :], ALU.mult, [b, btab], [], acc=[b])
        self.tt(tb_, ps, cs_t[:, 1, :], ALU.mult, [pb, btab], [], acc=[b])
        self.tt(ob, ta, tb_, ALU.add, [b], [], acc=[b])
        return ob, b

    def load_tabs(self, blk):
        cs_t = self.carve(40960, [128, 2, TB], F32)
        btab = Buf()
        self.ld(cs_t[:, 0, :], self.cosd[:, blk * TB:(blk + 1) * TB], [self.b_tab], [], acc=[btab])
        self.ld(cs_t[:, 1, :], self.sind[:, blk * TB:(blk + 1) * TB], [self.b_tab], [], acc=[btab])
        return cs_t, btab

    def kv_block(self, blk):
        self.rms(12)
        self.S.barrier()
        cs_t, btab = self.load_tabs(blk)
        self.rsi = 0
        self.rbuf = [Buf(), Buf(), Buf()]
        cols = slice(blk * TB, (blk + 1) * TB)

        def ck(m, ps, pb):
            ob, b = self.rope_chunk(ps, pb, 1.0, cs_t, btab)
            self.ld(self.KTs[m, :, cols], ob, [b], [], acc=[self.b_KV])
        self.linear_fm(self.w_kv, 0, 6144, KC, lambda k: self.xn[:, k, :], [self.b_xn], ck, defer=True)

        def cv(m, ps, pb):
            i = self.rsi % 3
            self.rsi += 1
            ob = self.carve(64512 + i * 1024, [128, TB], BF16)
            b = self.rbuf[i]
            self.act(ob, ps, AF.Copy, [pb], [b])
            self.ld(self.VTs[m, :, cols], ob, [b], [], acc=[self.b_KV])
        self.linear_fm(self.w_kv, 6144, 6144, KC, lambda k: self.xn[:, k, :], [self.b_xn], cv)
        self.S.barrier()

    def q_block(self, li, blk):
        self.rms(li)
        self.S.barrier()
        cs_t, btab = self.load_tabs(blk)
        self.rsi = 0
        self.rbuf = [Buf(), Buf(), Buf()]
        cols = slice(blk * TB, (blk + 1) * TB)

        def cq(m, ps, pb):
            ob, b = self.rope_chunk(ps, pb, 128 ** -0.5, cs_t, btab)
            self.ld(self.QTs[m, :, cols], ob, [b], [], acc=[self.b_Q])
        self.linear_fm(self.dil_w_q[li - 2], 0, 6144, KC, lambda k: self.xn[:, k, :], [self.b_xn], cq, defer=True)
        self.S.barrier()

    def dil_attn(self):
        T = self.T
        QS = 2048
        C_ = [self.b_c]
        mask2 = self.carve(0, [128, 2, 128], BF16)
        bmk = Buf()
        self.cp(mask2[:, 0, :], self.cstf[:, 5, :], C_, [], acc=[bmk])
        self.cp(mask2[:, 1, :], self.cstf[:, 1, :], C_, [], acc=[bmk])
        mflat = mask2[:].rearrange("p a b -> p (a b)")
        accb = self.carve(1024, [128, 2, QS], F32)
        oTb = self.carve(17408, [128, QS], BF16)
        rden = self.carve(21504, [128, QS], F32)
        Pm = [self.carve(65536 + i * 512, [128, 256], BF16) for i in range(4)]
        Pf = [self.carve(67584 + i * 1024, [128, 256], F32) for i in range(4)]
        o = 32768
        KT = self.carve(o, [128, 4096], BF16); o += 8192
        VT = self.carve(o, [128, 4096], BF16); o += 8192
        qT = self.carve(o, [128, QS], BF16); o += 4096
        Vtok = self.carve(o, [128, 32, 128], BF16); o += 8192
        assert o <= self.ARENA * 4
        bL = Buf("kvq")
        bV = Buf("vtok")
        bacc = Buf("acc")
        bP = [Buf() for _ in range(4)]
        pcount = 0
        for hh in range(16):
            for sg in range(T // QS):
                q0 = sg * QS
                for g, d in enumerate(DILS):
                    head = g * 16 + hh
                    halo = 128 * d
                    w0 = max(0, q0 - halo)
                    wl = q0 + QS - w0
                    self.ld(KT[:, 0:wl], self.KTs[head, :, w0:q0 + QS], [self.b_KV], [bL])
                    self.ld(VT[:, 0:wl], self.VTs[head, :, w0:q0 + QS], [self.b_KV], [], acc=[bL])
                    self.ld(qT[:, :], self.QTs[head, :, q0:q0 + QS], [self.b_Q], [], acc=[bL])
                    nbi = QS // (128 * d)
                    bi0 = q0 // (128 * d)
                    kb_lo = max(0, bi0 - 1)

                    def kcols(r, bi, d=d, w0=w0):
                        s = r + d * 128 * bi - w0
                        return slice(s, s + 127 * d + 1, d)
                    vidx = {}
                    n = 0
                    first = True
                    for r in range(d):
                        for bi in range(kb_lo, bi0 + nbi):
                            vidx[(r, bi)] = n
                            bk = self.bank()
                            pb16 = self.ps[bk][:].bitcast(BF16)
                            self.tr(pb16[:, 0:128], VT[:, kcols(r, bi)], self.identb, [bL] + C_, [self.bps[bk]])
                            W_, Acc = ([bV], []) if first else ([], [bV])
                            first = False
                            if n % 2 == 0:
                                self.cp(Vtok[:, n, :], pb16[:, 0:128], [self.bps[bk]], W_, acc=Acc)
                            else:
                                self.act(Vtok[:, n, :], pb16[:, 0:128], AF.Copy, [self.bps[bk]], W_, acc=Acc)
                            n += 1
                    qblocks = [(r, bi) for r in range(d) for bi in range(bi0, bi0 + nbi)]

                    def st1(r, bi, slot, d=d, q0=q0, kcols=kcols):
                        qa = r + d * 128 * bi - q0
                        qsl = slice(qa, qa + 127 * d + 1, d)
                        has_prev = bi >= 1
                        bk = self.bank()
                        if has_prev:
                            self.mm(self.ps[bk][:, 0:128], KT[:, kcols(r, bi - 1)], qT[:, qsl], True, True, [bL], [self.bps[bk]])
                        self.mm(self.ps[bk][:, 128:256], KT[:, kcols(r, bi)], qT[:, qsl], True, True, [bL], [self.bps[bk]])
                        lo = 0 if has_prev else 128
                        pf, pm, bp = Pf[slot], Pm[slot], bP[slot]
                        self.act(pf[:, lo:256], self.ps[bk][:, lo:256], AF.Exp, [self.bps[bk]], [bp])
                        self.tt(pm[:, lo:256], pf[:, lo:256], mflat[:, lo:256], ALU.mult,
                                [bmk], [bp], eng=("dve" if slot % 2 else "pool"))

                    def st2(r, bi, slot, g=g, d=d, q0=q0, vidx=vidx):
                        qa = r + d * 128 * bi - q0
                        qsl = slice(qa, qa + 127 * d + 1, d)
                        has_prev = bi >= 1
                        pm, bp = Pm[slot], bP[slot]
                        bo = self.bank()
                        if has_prev:
                            self.mm(self.ps[bo][:, 0:128], Vtok[:, vidx[(r, bi - 1)], :], pm[:, 0:128], True, False, [bp, bV], [self.bps[bo]])
                        self.mm(self.ps[bo][:, 0:128], Vtok[:, vidx[(r, bi)], :], pm[:, 128:256], not has_prev, True, [bp, bV], [self.bps[bo]])
                        if has_prev:
                            self.mm(self.ps[bo][:, 128:256], self.onesb[:], pm[:, 0:128], True, False, [bp] + C_, [self.bps[bo]])
                        self.mm(self.ps[bo][:, 128:256], self.onesb[:], pm[:, 128:256], not has_prev, True, [bp] + C_, [self.bps[bo]])
                        src = self.ps[bo][:, 0:256].rearrange("p (a b) -> p a b", b=128)
                        dst = accb[:, :, qsl]
                        if g == 0:
                            self.cp(dst, src, [self.bps[bo]], [bacc])
                        else:
                            self.tt(dst, src, dst, ALU.add, [self.bps[bo], bacc], [bacc])

                    LA = 2
                    for i in range(min(LA, len(qblocks))):
                        st1(qblocks[i][0], qblocks[i][1], (pcount + i) % 4)
                    for i, (r, bi) in enumerate(qblocks):
                        if i + LA < len(qblocks):
                            st1(qblocks[i + LA][0], qblocks[i + LA][1], (pcount + i + LA) % 4)
                        st2(r, bi, (pcount + i) % 4)
                    pcount += len(qblocks)
                self.S.op("dve", lambda e: e.reciprocal(out=rden, in_=accb[:, 1, :]), [bacc], [bacc])
                self.tt(oTb, accb[:, 0, :], rden, ALU.mult, [bacc], [bacc])
                self.ld(self.OTs[hh, :, q0:q0 + QS], oTb, [bacc], [], acc=[self.b_O])
        self.S.barrier()

    def final_block(self, blk):
        xf = self.carve(16384, [128, KC, TB], F32)
        yt = self.carve(49152, [128, 4, D], F32)
        bxf, byt = Buf(), Buf()
        self.rms(14, out=xf, ob=bxf)
        for c in range(4):
            for k in range(KC):
                bk = self.bank()
                self.tr(self.ps[bk][:, 0:128], xf[:, k, c * 128:(c + 1) * 128], self.ident, [bxf, self.b_c], [self.bps[bk]])
                if k % 2 == 0:
                    self.cp(yt[:, c, k * 128:(k + 1) * 128], self.ps[bk][:, 0:128], [self.bps[bk]], [], acc=[byt])
                else:
                    self.act(yt[:, c, k * 128:(k + 1) * 128], self.ps[bk][:, 0:128], AF.Copy, [self.bps[bk]], [], acc=[byt])
        t = self.ld(self.y[blk * TB:(blk + 1) * TB, :].rearrange("(c p) d -> p c d", p=128), yt, [byt], [], acc=[self.b_y])
        self.out_toks.append(t)
        self.S.barrier()

    def build(self):
        self.out_toks = []
        self.setup()
        self.prepass()
        for stage in self.plan:
            kind = stage[0]
            if kind == "ssd":
                li = stage[1]
                self.mem_kv(li)
                for blk in range(self.NB):
                    self.load_h(blk)
                    self.ssd_block(li, blk)
                    if "nomem" not in stage:
                        self.mem_attn(li)
                    if "noffn" not in stage:
                        self.ffn(li)
                    self.store_h(blk)
            elif kind == "kv":
                for blk in range(self.NB):
                    self.load_h(blk)
                    self.kv_block(blk)
            elif kind == "dil":
                li = stage[1]
                self.mem_kv(li)
                for blk in range(self.NB):
                    self.load_h(blk)
                    self.q_block(li, blk)
                self.dil_attn()
                for blk in range(self.NB):
                    self.load_h(blk)
                    self.ld(self.xn[:], self.OTs[:, :, blk * TB:(blk + 1) * TB].rearrange("k p t -> p k t"),
                            [self.b_O], [self.b_xn])
                    self.linear_fm(self.dil_w_o[li - 2], 0, D, KC, lambda k: self.xn[:, k, :], [self.b_xn], self.add_to_h)
                    self.S.barrier()
                    self.mem_attn(li)
                    self.ffn(li)
                    self.store_h(blk)
            elif kind == "ffn":
                li = stage[1]
                for blk in range(self.NB):
                    self.load_h(blk)
                    self.ffn(li)
                    self.store_h(blk)
            elif kind == "mem":
                li = stage[1]
                self.mem_kv(li)
                for blk in range(self.NB):
                    self.load_h(blk)
                    self.mem_attn(li)
                    self.store_h(blk)
        for blk in range(self.NB):
            self.load_h(blk)
            self.final_block(blk)
        self.S.wait_all("sp", self.out_toks)


FULL_PLAN = [("ssd", 0), ("ssd", 1), ("kv",), ("dil", 2), ("dil", 3)]


def host_params(inp):
    f = np.float32
    fm = lambda v: np.ascontiguousarray(np.asarray(v, f).reshape(KC, 128).T)
    gl = [fm(inp["norm_mix"][i]) for i in range(4)] + [fm(inp["norm_mem"][i]) for i in range(4)] + \
         [fm(inp["norm_ffn"][i]) for i in range(4)] + [fm(inp["kv_norm"]), fm(inp["mem_src_norm"]), fm(inp["norm_final"])]
    gains = np.ascontiguousarray(np.stack(gl, axis=1))
    cw = np.asarray(inp["ssd_conv_w"], f)
    convw = np.ascontiguousarray(cw.reshape(2, 4, 48, 128).transpose(3, 0, 2, 1))
    convb = np.ascontiguousarray(np.asarray(inp["ssd_conv_b"], f).reshape(2, 48, 128).transpose(2, 0, 1))
    rowp = np.ascontiguousarray(np.stack([inp["ssd_dt_bias"], inp["ssd_a_log"], inp["ssd_d"]], axis=1).astype(f))
    ssdnw = np.ascontiguousarray(np.asarray(inp["ssd_norm"], f).reshape(2, 32, 128).transpose(2, 0, 1))
    i = np.arange(128)
    s, t = i[:, None], i[None, :]
    cst = np.stack([np.eye(128), (s <= t), (s > t), (s == 127) * np.ones((128, 128)),
                    (s == (t + 64) % 128), (s >= t)], axis=1).astype(f)
    half = 64
    invf = (10000.0 ** (-np.arange(half, dtype=np.float32) / half)).astype(f)
    invf2 = np.stack([np.concatenate([invf, invf]), np.concatenate([-np.ones(half, f), np.ones(half, f)])], axis=1)
    return dict(gains=gains, convw=convw, convb=convb, rowp=rowp, ssdnw=ssdnw, cst=np.ascontiguousarray(cst),
                invf=np.ascontiguousarray(invf2.astype(f)))


WNAMES = {"ssd_w_in": "ssd_w_in", "ssd_w_out": "ssd_w_out", "w_kv_shared": "w_kv_shared", "dil_w_q": "dil_w_q",
          "dil_w_o": "dil_w_o", "mem_w_q": "mem_w_q", "mem_w_kv": "mem_w_kv", "mem_w_o": "mem_w_o",
          "ffn_w_in": "ffn_w_in", "ffn_w_out": "ffn_w_out"}

_CACHE = {}


def run(inp, T, plan, batches, n_cores):
    key = (T, tuple(plan))
    if key not in _CACHE:
        _CACHE[key] = Prog(T, plan)
    prog = _CACHE[key]
    hp = host_params(inp)
    ws = {n: np.ascontiguousarray(np.asarray(inp[n], np.float32)) for n in WNAMES}
    maps = []
    for c in range(n_cores):
        b = batches[c]
        m = dict(hp)
        m["x"] = np.ascontiguousarray(np.asarray(inp["x"][b, :T], np.float32))
        m["mem"] = np.ascontiguousarray(np.asarray(inp["mem"][b], np.float32))
        m["pos"] = np.ascontiguousarray(np.asarray(inp["positions"][b, :T], np.int32))
        m.update(ws)
        maps.append(m)
    res = run_bass_kernel_spmd(prog.nc, maps, core_ids=list(range(n_cores)))
    return [r["y"] for r in res.results]


def kernel(**inputs):
    T = 8192
    ys = run(inputs, T, FULL_PLAN, [0, 1], 2)
    return np.stack([ys[0], ys[1]], axis=0).astype(np.float32)
```

```python
import math
from contextlib import ExitStack
import numpy as np
import concourse.bass as bass
import concourse.mybir as mybir
from concourse.bass_utils import run_bass_kernel_spmd

F32 = mybir.dt.float32
BF16 = mybir.dt.bfloat16
I32 = mybir.dt.int32
AF = mybir.ActivationFunctionType
ALU = mybir.AluOpType

ENG = ("pe", "act", "dve", "pool", "sp")

D = 2048
KC = 16
DI = 4096
NH = 64
NG = 8
DFF = 5632
EPS = 1e-6
DILS = (1, 4, 16)
TB = 512
PK = 4


class Buf:
    __slots__ = ("name", "w", "r")

    def __init__(self, name=""):
        self.name = name
        self.w = {}
        self.r = []


class Sched:
    NDMA = 8

    def __init__(self, nc, stack):
        self.nc = nc
        self.ops = {e: [] for e in ENG}
        self.cnt = {e: 0 for e in ENG}
        self.sem = {e: stack.enter_context(nc.semaphore("s_" + e)) for e in ENG}
        self.dsem = {e: [stack.enter_context(nc.semaphore("d_%s%d" % (e, i))) for i in range(self.NDMA)]
                     for e in ("sp", "pool", "act")}
        self.dcnt = {e: [0] * self.NDMA for e in self.dsem}
        self.drr = {e: 0 for e in self.dsem}
        self.seen = {e: {} for e in ENG}
        self.semobj = {}
        for e in ENG:
            self.semobj[("e", e)] = self.sem[e]
        for e in self.dsem:
            for i, s in enumerate(self.dsem[e]):
                self.semobj[("d", e, i)] = s

    def _deps(self, eng, reads, writes, acc=()):
        deps = {}

        def add(tok):
            k, v = tok
            if deps.get(k, 0) < v:
                deps[k] = v
        for b in reads:
            for t in b.w.items():
                add(t)
        for b in writes:
            for t in b.w.items():
                add(t)
            for t in b.r:
                add(t)
        for b in acc:
            for t in b.r:
                add(t)
        waits = []
        seen = self.seen[eng]
        for k, v in deps.items():
            if eng == "pe" and k == ("e", "pe"):
                continue
            if seen.get(k, 0) >= v:
                continue
            seen[k] = v
            waits.append((k, v))
        return waits

    def _mark(self, tok, reads, writes, acc=()):
        for b in reads:
            b.r = [t for t in b.r if t[0] != tok[0]]
            b.r.append(tok)
        for b in writes:
            b.w = {tok[0]: tok[1]}
            b.r = []
        for b in acc:
            b.w[tok[0]] = tok[1]
            b.r = []

    def op(self, eng, fn, reads=(), writes=(), acc=()):
        waits = self._deps(eng, reads, writes, acc)
        self.cnt[eng] += 1
        tok = (("e", eng), self.cnt[eng])
        self.ops[eng].append((waits, fn, (("e", eng), 1)))
        self._mark(tok, reads, writes, acc)
        return tok

    def dma(self, q, fn, reads=(), writes=(), acc=()):
        i = self.drr[q]
        self.drr[q] = (i + 1) % self.NDMA
        key = ("d", q, i)
        waits = self._deps(q, reads, writes, acc)
        prev = self.dcnt[q][i]
        if prev and self.seen[q].get(key, 0) < prev:
            self.seen[q][key] = prev
            waits.append((key, prev))
        self.dcnt[q][i] = prev + 16
        tok = (key, prev + 16)
        self.ops[q].append((waits, fn, (key, 16)))
        self._mark(tok, reads, writes, acc)
        return tok

    def barrier(self):
        toks = [(("e", e), self.cnt[e]) for e in ENG if self.cnt[e]]
        for q in self.dsem:
            for i in range(self.NDMA):
                if self.dcnt[q][i]:
                    toks.append((("d", q, i), self.dcnt[q][i]))
        for e in ENG:
            if e != "pool":
                self.wait_all(e, toks)

    def wait_all(self, eng, toks):
        waits = []
        for k, v in toks:
            if eng == "pe" and k == ("e", "pe"):
                continue
            if self.seen[eng].get(k, 0) < v:
                self.seen[eng][k] = v
                waits.append((k, v))
        if waits:
            self.ops[eng].append((waits, None, None))

    def emit(self, block):
        engs = {"pe": block.tensor, "act": block.scalar, "dve": block.vector, "pool": block.gpsimd,
                "sp": block.sync}
        for e in ENG:
            ops = self.ops[e]
            if not ops:
                continue

            def body(engine, ops=ops):
                for waits, fn, inc in ops:
                    for k, v in waits:
                        engine.wait_ge(self.semobj[k], v)
                    if fn is not None:
                        ins = fn(engine)
                        ins.then_inc(self.semobj[inc[0]], inc[1])
            engs[e](body)


def bc(ap, shape):
    return ap.broadcast_to(list(shape))


class Prog:
    ARENA = 25620

    def __init__(self, T, plan):
        self.T = T
        self.NB = T // TB
        self.plan = plan
        nc = self.nc = bass.Bass("TRN2", target_bir_lowering=False)
        st = self.st = ExitStack()
        din = lambda n, s, dt=F32: nc.dram_tensor(n, list(s), dt, kind="ExternalInput").ap()
        dint = lambda n, s, dt=F32: nc.dram_tensor(n, list(s), dt, kind="Internal").ap()
        self.x = din("x", [T, D])
        self.mem = din("mem", [256, D])
        self.pos = din("pos", [T], I32)
        self.gains = din("gains", [128, 15, KC])
        self.convw = din("convw", [128, 2, 48, 4])
        self.convb = din("convb", [128, 2, 48])
        self.rowp = din("rowp", [2, 3, 64])
        self.ssdnw = din("ssdnw", [128, 2, 32])
        self.cst = din("cst", [128, 6, 128])
        self.invf = din("invf", [128, 2])
        self.ssd_w_in = din("ssd_w_in", [2, D, 10304])
        self.ssd_w_out = din("ssd_w_out", [2, DI, D])
        self.w_kv = din("w_kv_shared", [D, 12288])
        self.dil_w_q = din("dil_w_q", [2, D, 6144])
        self.dil_w_o = din("dil_w_o", [2, D, D])
        self.mem_w_q = din("mem_w_q", [4, D, 512])
        self.mem_w_kv = din("mem_w_kv", [4, D, 1024])
        self.mem_w_o = din("mem_w_o", [4, 512, D])
        self.ffn_w_in = din("ffn_w_in", [4, D, 2 * DFF])
        self.ffn_w_out = din("ffn_w_out", [4, DFF, D])
        self.y = nc.dram_tensor("y", [T, D], F32, kind="ExternalOutput").ap()
        self.hs = dint("hs", [KC, 128, T])
        self.cosd = dint("cosd", [128, T])
        self.sind = dint("sind", [128, T])
        self.memTd = dint("memTd", [128, KC, 256], BF16)
        self.KTs = dint("KTs", [48, 128, T], BF16)
        self.VTs = dint("VTs", [48, 128, T], BF16)
        self.QTs = dint("QTs", [48, 128, T], BF16)
        self.OTs = dint("OTs", [KC, 128, T], BF16)
        self.b_hs = [Buf() for _ in range(self.NB)]
        self.b_tab = Buf()
        self.b_memTd = Buf()
        self.b_KV = Buf()
        self.b_Q = Buf()
        self.b_O = Buf()
        self.b_y = Buf()

        S = self.S = Sched(nc, st)
        sb = self.sb = lambda n, s, dt: st.enter_context(nc.sbuf_tensor(n, list(s), dt))
        self.ps = [st.enter_context(nc.psum_tensor("ps%d" % i, [128, 512], F32)) for i in range(8)]
        self.bps = [Buf("ps%d" % i) for i in range(8)]
        self.pi = 0
        self.cstf = sb("cstf", [128, 6, 128], F32)
        self.cstb = sb("cstb", [128, 6, 128], BF16)
        self.onesb = sb("onesb", [128, 128], BF16)
        self.gn = sb("gn", [128, 15, KC], F32)
        self.cw = sb("cw", [128, 2, 48, 4], F32)
        self.cb = sb("cb", [128, 2, 48], F32)
        self.rp = sb("rp", [128, 2, 3, 64], F32)
        self.aneg = sb("aneg", [128, 2, 64], F32)
        self.snw = sb("snw", [128, 2, 32], F32)
        self.ivf = sb("ivf", [128, 2], F32)
        self.epsb = sb("epsb", [128, 1], F32)
        self.b_c = Buf("consts")
        self.hT = sb("hT", [128, KC, TB], F32)
        self.b_hT = Buf("hT")
        self.xn = sb("xn", [128, KC, TB], BF16)
        self.b_xn = Buf("xn")
        self.rstd = sb("rstd", [128, TB], F32)
        self.b_rstd = Buf("rstd")
        self.panels = [sb("pan%d" % i, [128, PK, 512], BF16) for i in range(4)]
        self.bpan = [Buf("pan%d" % i) for i in range(4)]
        self.pani = 0
        self.KmT = sb("KmT", [128, 4, 256], BF16)
        self.Vm = sb("Vm", [128, 2, 512], BF16)
        self.b_kvm = Buf()
        self.Hst = sb("Hst", [128, DI], F32)
        self.Hb = sb("Hb", [128, DI], BF16)
        self.b_H = [Buf() for _ in range(NG)]
        self.b_Hb = [Buf() for _ in range(NG)]
        self.halo = sb("halo", [128, 48, 3], F32)
        self.b_halo = Buf()
        self.arena = sb("arena", [128, self.ARENA], F32)
        self.b_ar = Buf("arena")

        self.build()
        with nc.Block() as block:
            S.emit(block)
        st.close()

    def bank(self):
        i = self.pi % 8
        self.pi += 1
        return i

    def carve(self, off_bytes, shape, dt, base="arena"):
        esz = 4 if dt in (F32, I32) else 2
        n = 1
        for s in shape[1:]:
            n *= s
        nw = (n * esz + 3) // 4
        assert off_bytes % 4 == 0
        if base == "arena":
            assert off_bytes + n * esz <= self.ARENA * 4, (off_bytes, shape)
            flat = self.arena[:, off_bytes // 4: off_bytes // 4 + nw]
        else:
            assert off_bytes + n * esz <= KC * TB * 2, (off_bytes, shape)
            flat = self.xn[:].rearrange("p a b -> p (a b)").bitcast(F32)[:, off_bytes // 4: off_bytes // 4 + nw]
        if dt != F32:
            flat = flat.bitcast(dt)
        flat = flat[:, 0:n]
        if len(shape) == 2:
            return flat
        if len(shape) == 3:
            return flat.rearrange("p (a b) -> p a b", b=shape[2])
        if len(shape) == 4:
            return flat.rearrange("p (a b c) -> p a b c", b=shape[2], c=shape[3])
        raise ValueError

    def mm(self, out, lhsT, rhs, start, stop, R, W):
        return self.S.op("pe", lambda e: e.matmul(out, lhsT=lhsT, rhs=rhs, start=start, stop=stop), reads=R, writes=W)

    def tr(self, out, in_, ident, R, W):
        return self.S.op("pe", lambda e: e.transpose(out, in_, ident), reads=R, writes=W)

    def act(self, out, in_, func, R, W, acc=(), **kw):
        return self.S.op("act", lambda e: e.activation(out=out, in_=in_, func=func, **kw), reads=R, writes=W, acc=acc)

    def amul(self, out, in_, mul, R, W, acc=()):
        return self.S.op("act", lambda e: e.mul(out=out, in_=in_, mul=mul), reads=R, writes=W, acc=acc)

    def tt(self, out, in0, in1, op, R, W, eng="dve", acc=()):
        return self.S.op(eng, lambda e: e.tensor_tensor(out=out, in0=in0, in1=in1, op=op), reads=R, writes=W, acc=acc)

    def ts(self, out, in0, s1, s2, op0, op1, R, W, eng="dve", acc=()):
        if s2 is None:
            return self.S.op(eng, lambda e: e.tensor_scalar(out=out, in0=in0, scalar1=s1, scalar2=None, op0=op0),
                             reads=R, writes=W, acc=acc)
        return self.S.op(eng, lambda e: e.tensor_scalar(out=out, in0=in0, scalar1=s1, scalar2=s2, op0=op0, op1=op1),
                         reads=R, writes=W, acc=acc)

    def stt(self, out, in0, scalar, in1, op0, op1, R, W, eng="dve", acc=()):
        return self.S.op(eng, lambda e: e.scalar_tensor_tensor(out=out, in0=in0, scalar=scalar, in1=in1, op0=op0, op1=op1),
                         reads=R, writes=W, acc=acc)

    def cp(self, out, in_, R, W, eng="dve", acc=()):
        return self.S.op(eng, lambda e: e.tensor_copy(out=out, in_=in_), reads=R, writes=W, acc=acc)

    def ld(self, out, in_, R, W, q="sp", acc=(), **kw):
        return self.S.dma(q, lambda e: e.dma_start(out=out, in_=in_, **kw), reads=R, writes=W, acc=acc)

    def panel(self, w2d, k0, nk, c0, ncols):
        i = self.pani % len(self.panels)
        self.pani += 1
        pan, b = self.panels[i], self.bpan[i]
        step = 4
        for kk in range(0, nk, step):
            n = min(step, nk - kk)
            src = w2d[(k0 + kk) * 128:(k0 + kk + n) * 128, c0:c0 + ncols].rearrange("(k p) n -> p k n", p=128)
            self.ld(pan[:, kk:kk + n, 0:ncols], src, [], [], q="pool", acc=[b])
        return pan, b

    def load_h(self, blk):
        self.ld(self.hT[:], self.hs[:, :, blk * TB:(blk + 1) * TB].rearrange("k p t -> p k t"),
                [self.b_hs[blk]], [self.b_hT])

    def store_h(self, blk):
        self.ld(self.hs[:, :, blk * TB:(blk + 1) * TB].rearrange("k p t -> p k t"), self.hT[:],
                [self.b_hT], [self.b_hs[blk]])

    def rms(self, gi, out=None, ob=None, sq_off=0):
        out = self.xn if out is None else out
        ob = self.b_xn if ob is None else ob
        sq = self.carve(sq_off, [128, KC, TB], BF16)
        bsq = Buf()
        for k in range(KC):
            self.act(sq[:, k, :], self.hT[:, k, :], AF.Square, [self.b_hT], [], acc=[bsq, self.b_ar])
        bk = self.bank()
        for k in range(KC):
            self.mm(self.ps[bk][:], self.onesb[:], sq[:, k, :], k == 0, k == KC - 1, [bsq, self.b_c], [self.bps[bk]])
        self.act(self.rstd[:], self.ps[bk][:], AF.Ln, [self.bps[bk], self.b_c], [self.b_rstd], scale=1.0 / D, bias=self.epsb[:])
        self.act(self.rstd[:], self.rstd[:], AF.Exp, [self.b_rstd], [self.b_rstd], scale=-0.5)
        for k in range(KC):
            self.stt(out[:, k, :], self.hT[:, k, :], self.gn[:, gi, k:k + 1], self.rstd[:], ALU.mult, ALU.mult,
                     [self.b_hT, self.b_rstd, self.b_c], [], acc=[ob])

    def linear_fm(self, w2d, c0, ncols, nk, rhs_of_k, R, consume, xcols=TB, defer=False):
        pending = []
        for cb in range(0, ncols, 512):
            nc_ = min(512, ncols - cb)
            nm = nc_ // 128
            banks = [self.bank() for _ in range(nm)]
            for k0 in range(0, nk, PK):
                n = min(PK, nk - k0)
                pan, pb = self.panel(w2d, k0, n, c0 + cb, nc_)
                for m in range(nm):
                    for k in range(n):
                        self.mm(self.ps[banks[m]][:, 0:xcols], pan[:, k, m * 128:(m + 1) * 128], rhs_of_k(k0 + k),
                                (k0 + k) == 0, (k0 + k) == nk - 1, [pb] + R, [self.bps[banks[m]]])
            for args in pending:
                consume(*args)
            pending = [((cb // 128) + m, self.ps[banks[m]][:, 0:xcols], self.bps[banks[m]]) for m in range(nm)]
            if not defer:
                for args in pending:
                    consume(*args)
                pending = []
        for args in pending:
            consume(*args)

    def linear_tm(self, w2d, c0, ncols, nk, lhs_of, nch, R, consume):
        banks = [self.bank() for _ in range(nch)]
        for k0 in range(0, nk, PK):
            n = min(PK, nk - k0)
            pan, pb = self.panel(w2d, k0, n, c0, ncols)
            for c in range(nch):
                for k in range(n):
                    self.mm(self.ps[banks[c]][:, 0:ncols], lhs_of(k0 + k, c), pan[:, k, 0:ncols],
                            (k0 + k) == 0, (k0 + k) == nk - 1, [pb] + R, [self.bps[banks[c]]])
        for c in range(nch):
            consume(c, self.ps[banks[c]][:, 0:ncols], self.bps[banks[c]])

    def add_to_h(self, m, ps, pb):
        self.tt(self.hT[:, m, :], ps, self.hT[:, m, :], ALU.add, [pb, self.b_hT], [], acc=[self.b_hT])

    def setup(self):
        S = self.S
        self.ld(self.cstf[:], self.cst, [], [], acc=[self.b_c])
        self.ld(self.gn[:], self.gains, [], [], acc=[self.b_c])
        self.ld(self.cw[:], self.convw, [], [], acc=[self.b_c])
        self.ld(self.cb[:], self.convb, [], [], acc=[self.b_c])
        self.ld(self.snw[:], self.ssdnw, [], [], acc=[self.b_c])
        self.ld(self.ivf[:], self.invf, [], [], acc=[self.b_c])
        self.ld(self.rp[:].rearrange("p a b c -> p (a b c)"),
                self.rowp.rearrange("a b c -> (a b c)").partition_broadcast(128), [], [], acc=[self.b_c])
        S.barrier()
        self.cp(self.cstb[:], self.cstf[:], [self.b_c], [], acc=[self.b_c])
        S.op("dve", lambda e: e.memset(self.onesb[:], 1.0), [], [], acc=[self.b_c])
        S.op("dve", lambda e: e.memset(self.epsb[:], EPS), [], [], acc=[self.b_c])
        self.act(self.aneg[:], self.rp[:, :, 1, :], AF.Exp, [self.b_c], [], acc=[self.b_c])
        S.barrier()
        self.ts(self.aneg[:], self.aneg[:], -1.0, None, ALU.mult, None, [self.b_c], [], acc=[self.b_c])
        self.ident = self.cstf[:, 0, :]
        self.identb = self.cstb[:, 0, :]
        self.tri = self.cstf[:, 1, :]
        self.U = self.cstf[:, 2, :]
        self.sel127 = self.cstf[:, 3, :]
        self.swapb = self.cstb[:, 4, :]
        S.barrier()

    def prepass(self):
        NB = self.NB
        xt = self.carve(0, [128, 4, D], F32)
        A = [self.b_ar]
        for blk in range(NB):
            self.ld(xt, self.x[blk * TB:(blk + 1) * TB, :].rearrange("(c p) d -> p c d", p=128), [], A)
            for c in range(4):
                for k in range(KC):
                    bk = self.bank()
                    self.tr(self.ps[bk][:, 0:128], xt[:, c, k * 128:(k + 1) * 128], self.ident, A + [self.b_c], [self.bps[bk]])
                    if (c * KC + k) % 2 == 0:
                        self.act(self.hT[:, k, c * 128:(c + 1) * 128], self.ps[bk][:, 0:128], AF.Copy, [self.bps[bk]], [], acc=[self.b_hT])
                    else:
                        self.cp(self.hT[:, k, c * 128:(c + 1) * 128], self.ps[bk][:, 0:128], [self.bps[bk]], [], acc=[self.b_hT])
            self.store_h(blk)
            pi_ = self.carve(40960, [128, TB], I32)
            ang = self.carve(43008, [128, TB], F32)
            kf = self.carve(45056, [128, TB], F32)
            r = self.carve(47104, [128, TB], F32)
            sc = self.carve(49152, [128, 2, TB], F32)
            B = [Buf("rope")]
            self.ld(pi_, self.pos[blk * TB:(blk + 1) * TB].partition_broadcast(128), [], B)
            self.cp(ang, pi_, B, B)
            self.ts(ang, ang, self.ivf[:, 0:1], None, ALU.mult, None, B + [self.b_c], B)
            MAGIC = 12582912.0
            HI = 6.28125
            LO = float(2 * np.pi - 6.28125)
            for j, shift in enumerate((0.5 * np.pi, 0.0)):
                self.ts(kf, ang, float(shift), float(1 / (2 * np.pi)), ALU.add, ALU.mult, B, B)
                self.ts(kf, kf, MAGIC, None, ALU.add, None, B, B)
                self.ts(kf, kf, MAGIC, None, ALU.subtract, None, B, B)
                self.ts(r, ang, float(shift), None, ALU.add, None, B, B)
                self.stt(r, kf, -HI, r, ALU.mult, ALU.add, B, B)
                self.stt(r, kf, -LO, r, ALU.mult, ALU.add, B, B)
                self.ts(r, r, float(-np.pi), float(np.pi), ALU.max, ALU.min, B, B)
                self.act(sc[:, j, :], r, AF.Sin, B, B)
            self.ts(sc[:, 1, :], sc[:, 1, :], self.ivf[:, 1:2], None, ALU.mult, None, B + [self.b_c], B)
            self.ld(self.cosd[:, blk * TB:(blk + 1) * TB], sc[:, 0, :], B, [], acc=[self.b_tab])
            self.ld(self.sind[:, blk * TB:(blk + 1) * TB], sc[:, 1, :], B, [], acc=[self.b_tab])
            self.S.barrier()
        mt = self.carve(0, [128, 2, D], F32)
        junk = self.carve(16384, [128, D], F32)
        ssq = self.carve(24576, [128, 4], F32)
        memT = self.carve(32768, [128, KC, 256], BF16)
        self.ld(mt, self.mem.rearrange("(c p) d -> p c d", p=128), [], A)
        for c in range(2):
            self.S.op("dve", lambda e, c=c: e.memset(ssq[:, c:c + 1], 0.0), A, A)
            self.act(junk, mt[:, c, :], AF.Square, A, A, accum_out=ssq[:, c:c + 1])
        self.act(ssq[:, 2:4], ssq[:, 0:2], AF.Ln, A + [self.b_c], A, scale=1.0 / D, bias=self.epsb[:])
        self.act(ssq[:, 2:4], ssq[:, 2:4], AF.Exp, A, A, scale=-0.5)
        bm = Buf()
        for c in range(2):
            self.ts(mt[:, c, :], mt[:, c, :], ssq[:, 2 + c:3 + c], None, ALU.mult, None, A, A)
            for k in range(KC):
                bk = self.bank()
                self.tr(self.ps[bk][:, 0:128], mt[:, c, k * 128:(k + 1) * 128], self.ident, A + [self.b_c], [self.bps[bk]])
                self.ts(memT[:, k, c * 128:(c + 1) * 128], self.ps[bk][:, 0:128], self.gn[:, 13, k:k + 1], None,
                        ALU.mult, None, [self.bps[bk], self.b_c], [], acc=[bm])
        self.ld(self.memTd, memT, [bm], [self.b_memTd])
        self.S.barrier()

    def mem_kv(self, li):
        w = self.mem_w_kv[li]
        memT = self.carve(32768, [128, KC, 256], BF16)
        bm = Buf()
        self.ld(memT, self.memTd, [self.b_memTd], [bm])

        def consume(m, ps, pb):
            self.cp(self.KmT[:, m, :], ps, [pb], [], acc=[self.b_kvm])
        self.linear_fm(w, 0, 512, KC, lambda k: memT[:, k, :], [bm], consume, xcols=256)

        def cv(c, ps, pb):
            self.cp(self.Vm[:, c, :], ps, [pb], [], acc=[self.b_kvm])
        self.linear_tm(w, 512, 512, KC, lambda k, c: memT[:, k, c * 128:(c + 1) * 128], 2, [bm], cv)
        self.S.barrier()

    def mem_attn(self, li):
        self.rms(4 + li, sq_off=81920)
        qT = self.carve(0, [128, 4, TB], BF16)
        oT = self.carve(4096, [128, 4, TB], BF16)
        pT = [self.carve(8192 + i * 2048, [128, 2, TB], BF16) for i in range(4)]
        rden = [self.carve(16384 + i * 2048, [128, TB], F32) for i in range(2)]
        bq, bo_ = Buf(), Buf()
        bpp = [Buf() for _ in range(4)]
        brr = [Buf(), Buf()]
        sc = 128 ** -0.5

        def cq(m, ps, pb):
            self.amul(qT[:, m, :], ps, sc, [pb], [], acc=[bq])
        self.linear_fm(self.mem_w_q[li], 0, 512, KC, lambda k: self.xn[:, k, :], [self.b_xn], cq)

        def st1(hd):
            p_, bp = pT[hd], bpp[hd]
            for mc in range(2):
                bk = self.bank()
                self.mm(self.ps[bk][:], self.KmT[:, hd, mc * 128:(mc + 1) * 128], qT[:, hd, :], True, True,
                        [bq, self.b_kvm], [self.bps[bk]])
                self.act(p_[:, mc, :], self.ps[bk][:], AF.Exp, [self.bps[bk]], [bp] if mc == 0 else [], acc=[] if mc == 0 else [bp])

        def st2(hd):
            p_, bp = pT[hd], bpp[hd]
            bo, bd = self.bank(), self.bank()
            for mc in range(2):
                self.mm(self.ps[bo][:], self.Vm[:, mc, hd * 128:(hd + 1) * 128], p_[:, mc, :], mc == 0, mc == 1,
                        [bp, self.b_kvm], [self.bps[bo]])
            for mc in range(2):
                self.mm(self.ps[bd][:], self.onesb[:], p_[:, mc, :], mc == 0, mc == 1, [bp, self.b_c], [self.bps[bd]])
            rd = rden[hd % 2]
            brd = brr[hd % 2]
            self.S.op("dve", lambda e, bd=bd, rd=rd: e.reciprocal(out=rd, in_=self.ps[bd][:]), [self.bps[bd]], [brd])
            self.tt(oT[:, hd, :], self.ps[bo][:], rd, ALU.mult, [self.bps[bo], brd], [], acc=[bo_])
        st1(0)
        st1(1)
        for hd in range(4):
            if hd + 2 < 4:
                st1(hd + 2)
            st2(hd)
        self.linear_fm(self.mem_w_o[li], 0, D, 4, lambda k: oT[:, k, :], [bo_], self.add_to_h)
        self.S.barrier()

    def ffn(self, li):
        self.rms(8 + li, sq_off=81920)
        hid = self.carve(0, [128, 44, TB], BF16)
        sg = self.carve(45056, [128, 8, TB], BF16)
        bhid = Buf()
        bsg8 = [Buf() for _ in range(8)]
        w = self.ffn_w_in[li]
        for j in range(11):
            par = (j % 2) * 4
            bsg = bsg8[par:par + 4]

            def cg(m, ps, pb, bsg=bsg, par=par):
                self.act(sg[:, par + m % 4, :], ps, AF.Silu, [pb], [bsg[m % 4]])
            self.linear_fm(w, j * 512, 512, KC, lambda k: self.xn[:, k, :], [self.b_xn], cg)

            def cu(m, ps, pb, bsg=bsg, par=par, j=j):
                self.tt(hid[:, j * 4 + m % 4, :], ps, sg[:, par + m % 4, :], ALU.mult, [pb, bsg[m % 4]], [], acc=[bhid])
            self.linear_fm(w, DFF + j * 512, 512, KC, lambda k: self.xn[:, k, :], [self.b_xn], cu)
        self.linear_fm(self.ffn_w_out[li], 0, D, 44, lambda k: hid[:, k, :], [bhid], self.add_to_h)
        self.S.barrier()

    def ssd_block(self, li, blk):
        S = self.S
        C_ = [self.b_c]
        w = self.ssd_w_in[li]
        self.rms(li)
        S.barrier()
        xT = self.carve(0, [128, 32, TB], BF16)
        BT = self.carve(32768, [128, NG, TB], BF16)
        CT = self.carve(40960, [128, NG, TB], BF16)
        zs = self.carve(49152, [128, 4, DI], BF16)
        o = 81920
        ubuf = [self.carve(o + i * 2064, [128, 516], F32) for i in range(3)]
        o += 3 * 2064
        acc = [self.carve(o + i * 2048, [128, TB], F32) for i in range(3)]
        o += 3 * 2048
        small = self.carve(o, [128, 4, 8, 64], F32)
        o += 8192
        assert o <= self.ARENA * 4, o
        b_x, b_B, b_C, b_z, b_sm = Buf("xT"), Buf("BT"), Buf("CT"), Buf("zs"), Buf("small")
        if blk == 0:
            S.op("dve", lambda e: e.memset(self.halo[:], 0.0), [], [self.b_halo])
            S.op("dve", lambda e: e.memset(self.Hst[:], 0.0), [], self.b_H)
            S.op("dve", lambda e: e.memset(self.Hb[:], 0.0), [], self.b_Hb)
        ci = [0]
        bcv = [Buf(), Buf(), Buf()]

        def cconv(m, ps, pb):
            i = ci[0] % 3
            ci[0] += 1
            u, a, b = ubuf[i], acc[i], bcv[i]
            cwv = self.cw[:, li, m, :]
            self.cp(u[:, 0:3], self.halo[:, m, :], [self.b_halo], [b])
            self.act(u[:, 3:3 + TB], ps, AF.Copy, [pb], [], acc=[b])
            self.act(a, ps, AF.Identity, [pb] + C_, [], acc=[b], scale=cwv[:, 3:4], bias=self.cb[:, li, m:m + 1])
            self.cp(self.halo[:, m, :], u[:, TB:TB + 3], [b], [], acc=[self.b_halo])
            for j in range(3):
                self.stt(a, u[:, j:j + TB], cwv[:, j:j + 1], a, ALU.mult, ALU.add, [b] + C_, [], acc=[b])
            if m < 32:
                self.act(xT[:, m, :], a, AF.Silu, [b], [], acc=[b_x])
            elif m < 40:
                self.act(BT[:, m - 32, :], a, AF.Silu, [b], [], acc=[b_B])
            else:
                self.act(CT[:, m - 40, :], a, AF.Silu, [b], [], acc=[b_C])
        self.linear_fm(w, DI, 6144, KC, lambda k: self.xn[:, k, :], [self.b_xn], cconv)
        for j in range(8):
            def cz(c, ps, pb, j=j):
                self.act(zs[:, c, j * 512:(j + 1) * 512], ps, AF.Silu, [pb], [], acc=[b_z])
            self.linear_tm(w, j * 512, 512, KC, lambda k, c: self.xn[:, k, c * 128:(c + 1) * 128], 4, [self.b_xn], cz)
        def cdt(c, ps, pb):
            dt, dtA, acum, ea, cd, wend, tmp = [small[:, c, i, :] for i in range(7)]
            M = [b_sm]
            self.tt(tmp, ps, self.rp[:, li, 0, :], ALU.add, [pb] + C_, M)
            self.act(tmp, tmp, AF.Exp, M, M)
            self.act(dt, tmp, AF.Ln, M, M, bias=1.0)
            self.tt(dtA, dt, self.aneg[:, li, :], ALU.mult, M + C_, M)
            bk = self.bank()
            self.mm(self.ps[bk][:, 0:64], self.tri, dtA, True, True, M + C_, [self.bps[bk]])
            self.cp(acum, self.ps[bk][:, 0:64], [self.bps[bk]], M)
            self.act(ea, acum, AF.Exp, M, M)
            bk = self.bank()
            self.mm(self.ps[bk][:, 0:64], self.sel127, acum, True, True, M + C_, [self.bps[bk]])
            self.act(cd, self.ps[bk][:, 0:64], AF.Exp, [self.bps[bk]], M)
            self.tt(tmp, self.ps[bk][:, 0:64], acum, ALU.subtract, [self.bps[bk]] + M, M)
            self.act(tmp, tmp, AF.Exp, M, M)
            self.tt(wend, tmp, dt, ALU.mult, M, M)
        self.linear_tm(w, 10240, 64, KC, lambda k, c: self.xn[:, k, c * 128:(c + 1) * 128], 4, [self.b_xn], cdt)
        S.barrier()
        Rgb = self.carve(81920, [128, 8, 128], BF16)
        dec2 = [self.carve(83968 + i * 4096, [128, 8, 128], F32) for i in range(2)]
        xD2 = [self.carve(92160 + i * 1024, [128, 512], BF16) for i in range(2)]
        rsb = self.rstd[:].bitcast(BF16)
        xw2 = [rsb[:, i * 512:(i + 1) * 512] for i in range(2)]
        o2 = 0

        def xc(shape, dt):
            nonlocal o2
            n = 1
            for s_ in shape[1:]:
                n *= s_
            v = self.carve(o2, shape, dt, base="xn")
            o2 += ((n * (4 if dt == F32 else 2) + 3) // 4) * 4
            return v
        CBm2 = [xc([128, 128], F32) for _ in range(2)]
        MT2 = [xc([128, 8, 128], BF16) for _ in range(2)]
        xtok2 = [xc([128, 512], BF16) for _ in range(2)]
        xdt2 = [xc([128, 512], BF16) for _ in range(2)]
        Btok2 = [xc([128, 128], BF16) for _ in range(2)]
        t1 = xc([128, 512], F32)
        t2 = xc([128, 512], F32)
        yn2 = [xc([128, 512], BF16) for _ in range(2)]
        ssq = xc([128, 4], F32)
        bR, bY = Buf(), Buf()
        bD2, bCB2, bM2, bX2, bP2, bW2, bN2 = [[Buf(), Buf()] for _ in range(7)]
        Ub = self.cstb[:, 2, :]
        v3 = lambda ap: ap.rearrange("p (a b) -> p a b", b=64)
        items = [(c, g) for c in range(4) for g in range(NG)]
        NI = len(items)

        def A1(i):
            c, g = items[i]
            p = i % 2
            cs = slice(c * 128, (c + 1) * 128)
            dt, dtA, wend = small[:, c, 0, :], small[:, c, 1, :], small[:, c, 5, :]
            hs_ = slice(g * 8, (g + 1) * 8)
            dec, xtok = dec2[p], xtok2[p]
            self.tt(Rgb, bc(self.tri.unsqueeze(1), [128, 8, 128]), bc(dtA[:, hs_].unsqueeze(2), [128, 8, 128]), ALU.mult,
                    [b_sm] + C_, [bR])
            for hh in range(2):
                bk = self.bank()
                self.mm(self.ps[bk][:], Ub, Rgb[:, hh * 4:(hh + 1) * 4, :], True, True, [bR] + C_, [self.bps[bk]])
                self.act(dec[:, hh * 4:(hh + 1) * 4, :], self.ps[bk][:].rearrange("p (a b) -> p a b", b=128), AF.Exp,
                         [self.bps[bk]], [bD2[p]] if hh == 0 else [], acc=[] if hh == 0 else [bD2[p]])
            bk = self.bank()
            self.mm(self.ps[bk][:, 0:128], BT[:, g, cs], CT[:, g, cs], True, True, [b_B, b_C], [self.bps[bk]])
            self.tt(CBm2[p], self.ps[bk][:, 0:128], self.tri, ALU.mult, [self.bps[bk]] + C_, [bCB2[p]])
            bk = self.bank()
            pb16 = self.ps[bk][:].bitcast(BF16)
            for q in range(4):
                self.tr(pb16[:, q * 128:(q + 1) * 128], xT[:, g * 4 + q, cs], self.identb, [b_x] + C_, [self.bps[bk]])
            self.act(xtok, pb16[:, 0:512], AF.Copy, [self.bps[bk]], [bX2[p]])
            bk = self.bank()
            pb16 = self.ps[bk][:].bitcast(BF16)
            self.tr(pb16[:, 0:128], BT[:, g, cs], self.identb, [b_B] + C_, [self.bps[bk]])
            self.act(Btok2[p], pb16[:, 0:128], AF.Copy, [self.bps[bk]], [bW2[p]])
            self.tt(v3(xdt2[p]), v3(xtok), bc(dt[:, hs_].unsqueeze(2), [128, 8, 64]), ALU.mult, [bX2[p], b_sm], [bP2[p]], eng="pool")
            self.tt(v3(xD2[p]), v3(xtok), bc(self.rp[:, li, 2, hs_].unsqueeze(2), [128, 8, 64]), ALU.mult, [bX2[p]] + C_, [], eng="pool", acc=[bP2[p]])
            self.tt(v3(xw2[p]), v3(xtok), bc(wend[:, hs_].unsqueeze(2), [128, 8, 64]), ALU.mult, [bX2[p], b_sm], [], eng="pool", acc=[bW2[p]])

        def A2(i):
            p = i % 2
            self.tt(MT2[p], dec2[p], bc(CBm2[p].unsqueeze(1), [128, 8, 128]), ALU.mult, [bD2[p], bCB2[p]], [bM2[p]])

        st = {}

        def B1(i):
            c, g = items[i]
            p = i % 2
            cs = slice(c * 128, (c + 1) * 128)
            ea, cd = small[:, c, 3, :], small[:, c, 4, :]
            hs_ = slice(g * 8, (g + 1) * 8)
            gs = slice(g * 512, (g + 1) * 512)
            by = self.bank()
            self.mm(self.ps[by][:], self.identb, xD2[p], True, False, [bP2[p]] + C_, [self.bps[by]])
            for j in range(8):
                self.mm(self.ps[by][:, j * 64:(j + 1) * 64], MT2[p][:, j, :], xdt2[p][:, j * 64:(j + 1) * 64], False, j == 7,
                        [bM2[p], bP2[p]], [self.bps[by]])
            bo = self.bank()
            self.mm(self.ps[bo][:], CT[:, g, cs], self.Hb[:, gs], True, True, [b_C, self.b_Hb[g]], [self.bps[bo]])
            bs = self.bank()
            self.mm(self.ps[bs][:], Btok2[p], xw2[p], True, True, [bW2[p]], [self.bps[bs]])
            st[i] = bs
            self.tt(v3(self.Hst[:, gs]), v3(self.Hst[:, gs]), bc(cd[:, hs_].unsqueeze(2), [128, 8, 64]), ALU.mult,
                    [self.b_H[g], b_sm], [self.b_H[g]], eng="pool")
            self.tt(v3(t1), v3(self.ps[bo][:]), bc(ea[:, hs_].unsqueeze(2), [128, 8, 64]), ALU.mult,
                    [self.bps[bo], b_sm], [bY])
            self.tt(t2, self.ps[by][:], t1, ALU.add, [self.bps[by], bY], [bY])
            self.tt(t2, t2, zs[:, c, gs], ALU.mult, [bY, b_z], [bY])
            S.op("dve", lambda e: e.memset(ssq[:, 0:1], 0.0), [bY], [bY])
            self.act(t1, t2, AF.Square, [bY], [bY], accum_out=ssq[:, 0:1])
            self.act(ssq[:, 1:2], ssq[:, 0:1], AF.Ln, [bY] + C_, [bY], scale=1.0 / 512, bias=self.epsb[:])
            self.act(ssq[:, 1:2], ssq[:, 1:2], AF.Exp, [bY], [bY], scale=-0.5)

        def B2(i):
            c, g = items[i]
            p = i % 2
            gs = slice(g * 512, (g + 1) * 512)
            bs = st.pop(i)
            self.ts(yn2[p], t2, ssq[:, 1:2], None, ALU.mult, None, [bY], [bN2[p]])
            self.tt(self.Hst[:, gs], self.Hst[:, gs], self.ps[bs][:], ALU.add, [self.b_H[g], self.bps[bs]], [self.b_H[g]])
            self.act(self.Hb[:, gs], self.Hst[:, gs], AF.Copy, [self.b_H[g]], [self.b_Hb[g]])

        def Cst(i):
            c, g = items[i]
            p = i % 2
            cs = slice(c * 128, (c + 1) * 128)
            bk = self.bank()
            pb16 = self.ps[bk][:].bitcast(BF16)
            for q in range(4):
                self.tr(pb16[:, q * 128:(q + 1) * 128], yn2[p][:, q * 128:(q + 1) * 128], self.identb, [bN2[p]] + C_, [self.bps[bk]])
            for q in range(4):
                sc_ = self.snw[:, li, g * 4 + q:g * 4 + q + 1]
                if q == 0:
                    self.ts(xT[:, g * 4 + q, cs], pb16[:, q * 128:(q + 1) * 128], sc_, None, ALU.mult, None,
                            [self.bps[bk]] + C_, [], acc=[b_x])
                else:
                    self.amul(xT[:, g * 4 + q, cs], pb16[:, q * 128:(q + 1) * 128], sc_, [self.bps[bk]] + C_, [], acc=[b_x])

        A1(0)
        A1(1)
        A2(0)
        for i in range(NI):
            B1(i)
            if i + 1 < NI:
                A2(i + 1)
            B2(i)
            if i >= 1:
                Cst(i - 1)
            if i + 2 < NI:
                A1(i + 2)
        Cst(NI - 1)
        S.barrier()
        self.linear_fm(self.ssd_w_out[li], 0, D, 32, lambda k: xT[:, k, :], [b_x], self.add_to_h)
        S.barrier()

    def linear_rope(self, w2d, c0, ncols, scale, cs_t, btab, store):
        qsb = [self.carve(16384 + i * 1024, [128, TB], BF16) for i in range(6)]
        bq = [Buf() for _ in range(6)]
        rb = [Buf() for _ in range(3)]
        ta_ = [self.carve(52224 + i * 2048, [128, TB], F32) for i in range(3)]
        tb_ = [self.carve(58368 + i * 2048, [128, TB], F32) for i in range(3)]
        ob_ = [self.carve(64512 + i * 1024, [128, TB], BF16) for i in range(3)]
        GW = 384
        ngr = ncols // GW
        cnt = [0]

        def tail(gr):
            for m in range(3):
                q = qsb[(gr % 2) * 3 + m]
                b_q = bq[(gr % 2) * 3 + m]
                i = cnt[0] % 3
                sbk = 6 + cnt[0] % 2
                cnt[0] += 1
                self.mm(self.ps[sbk][:], self.swapb, q, True, True, [b_q, self.b_c], [self.bps[sbk]])
                self.tt(ta_[i], q, cs_t[:, 0, :], ALU.mult, [b_q, btab], [rb[i]])
                self.tt(tb_[i], self.ps[sbk][:], cs_t[:, 1, :], ALU.mult, [self.bps[sbk], btab], [], acc=[rb[i]])
                self.tt(ob_[i], ta_[i], tb_[i], ALU.add, [rb[i]], [], acc=[rb[i]])
                store(gr * 3 + m, ob_[i], rb[i])

        for gr in range(ngr):
            banks = [(gr % 2) * 3 + m for m in range(3)]
            for k0 in range(0, KC, PK):
                pan, pb = self.panel(w2d, k0, PK, c0 + gr * GW, GW)
                for m in range(3):
                    for k in range(PK):
                        self.mm(self.ps[banks[m]][:], pan[:, k, m * 128:(m + 1) * 128], self.xn[:, k0 + k, :],
                                (k0 + k) == 0, (k0 + k) == KC - 1, [pb, self.b_xn], [self.bps[banks[m]]])
            for m in range(3):
                self.amul(qsb[(gr % 2) * 3 + m], self.ps[banks[m]][:], scale, [self.bps[banks[m]]], [bq[(gr % 2) * 3 + m]])
            if gr >= 1:
                tail(gr - 1)
        tail(ngr - 1)

    def load_tabs(self, blk):
        cs_t = self.carve(40960, [128, 2, TB], F32)
        btab = Buf()
        self.ld(cs_t[:, 0, :], self.cosd[:, blk * TB:(blk + 1) * TB], [self.b_tab], [], acc=[btab])
        self.ld(cs_t[:, 1, :], self.sind[:, blk * TB:(blk + 1) * TB], [self.b_tab], [], acc=[btab])
        return cs_t, btab

    def kv_block(self, blk):
        self.rms(12)
        self.S.barrier()
        cs_t, btab = self.load_tabs(blk)
        self.rsi = 0
        self.rbuf = [Buf(), Buf(), Buf()]
        cols = slice(blk * TB, (blk + 1) * TB)

        def stk(m, ob, b):
            self.ld(self.KTs[m, :, cols], ob, [b], [], acc=[self.b_KV])
        self.linear_rope(self.w_kv, 0, 6144, 1.0, cs_t, btab, stk)
        self.S.barrier()

        def cv(m, ps, pb):
            i = self.rsi % 3
            self.rsi += 1
            ob = self.carve(64512 + i * 1024, [128, TB], BF16)
            b = self.rbuf[i]
            self.act(ob, ps, AF.Copy, [pb], [b])
            self.ld(self.VTs[m, :, cols], ob, [b], [], acc=[self.b_KV])
        self.linear_fm(self.w_kv, 6144, 6144, KC, lambda k: self.xn[:, k, :], [self.b_xn], cv)
        self.S.barrier()

    def q_block(self, li, blk):
        self.rms(li)
        self.S.barrier()
        cs_t, btab = self.load_tabs(blk)
        self.rsi = 0
        self.rbuf = [Buf(), Buf(), Buf()]
        cols = slice(blk * TB, (blk + 1) * TB)

        def stq(m, ob, b):
            self.ld(self.QTs[m, :, cols], ob, [b], [], acc=[self.b_Q])
        self.linear_rope(self.dil_w_q[li - 2], 0, 6144, 128 ** -0.5, cs_t, btab, stq)
        self.S.barrier()

    def dil_attn(self):
        T = self.T
        QS = 2048
        C_ = [self.b_c]
        mask2 = self.carve(0, [128, 2, 128], BF16)
        bmk = Buf()
        self.cp(mask2[:, 0, :], self.cstf[:, 5, :], C_, [], acc=[bmk])
        self.cp(mask2[:, 1, :], self.cstf[:, 1, :], C_, [], acc=[bmk])
        mflat = mask2[:].rearrange("p a b -> p (a b)")
        accb = self.carve(1024, [128, 2, QS], F32)
        oTb = self.carve(17408, [128, QS], BF16)
        rden = self.carve(21504, [128, QS], F32)
        Pm = [self.carve(65536 + i * 512, [128, 256], BF16) for i in range(4)]
        Pf = [self.carve(67584 + i * 1024, [128, 256], F32) for i in range(4)]
        o = 32768
        KT = self.carve(o, [128, 4096], BF16); o += 8192
        VT = self.carve(o, [128, 4096], BF16); o += 8192
        qT = self.carve(o, [128, QS], BF16); o += 4096
        Vtok = self.carve(o, [128, 32, 128], BF16); o += 8192
        assert o <= self.ARENA * 4
        bL = Buf("kvq")
        bV = Buf("vtok")
        bacc = Buf("acc")
        bP = [Buf() for _ in range(4)]
        pcount = 0
        for hh in range(16):
            for sg in range(T // QS):
                q0 = sg * QS
                for g, d in enumerate(DILS):
                    head = g * 16 + hh
                    halo = 128 * d
                    w0 = max(0, q0 - halo)
                    wl = q0 + QS - w0
                    self.ld(KT[:, 0:wl], self.KTs[head, :, w0:q0 + QS], [self.b_KV], [bL])
                    self.ld(VT[:, 0:wl], self.VTs[head, :, w0:q0 + QS], [self.b_KV], [], acc=[bL])
                    self.ld(qT[:, :], self.QTs[head, :, q0:q0 + QS], [self.b_Q], [], acc=[bL])
                    nbi = QS // (128 * d)
                    bi0 = q0 // (128 * d)
                    kb_lo = max(0, bi0 - 1)

                    def kcols(r, bi, d=d, w0=w0):
                        s = r + d * 128 * bi - w0
                        return slice(s, s + 127 * d + 1, d)
                    vidx = {}
                    n = 0
                    first = True
                    for r in range(d):
                        for bi in range(kb_lo, bi0 + nbi):
                            vidx[(r, bi)] = n
                            bk = self.bank()
                            pb16 = self.ps[bk][:].bitcast(BF16)
                            self.tr(pb16[:, 0:128], VT[:, kcols(r, bi)], self.identb, [bL] + C_, [self.bps[bk]])
                            W_, Acc = ([bV], []) if first else ([], [bV])
                            first = False
                            if n % 2 == 0:
                                self.cp(Vtok[:, n, :], pb16[:, 0:128], [self.bps[bk]], W_, acc=Acc)
                            else:
                                self.act(Vtok[:, n, :], pb16[:, 0:128], AF.Copy, [self.bps[bk]], W_, acc=Acc)
                            n += 1
                    qblocks = [(r, bi) for r in range(d) for bi in range(bi0, bi0 + nbi)]

                    def st1(r, bi, slot, d=d, q0=q0, kcols=kcols):
                        qa = r + d * 128 * bi - q0
                        qsl = slice(qa, qa + 127 * d + 1, d)
                        has_prev = bi >= 1
                        bk = self.bank()
                        if has_prev:
                            self.mm(self.ps[bk][:, 0:128], KT[:, kcols(r, bi - 1)], qT[:, qsl], True, True, [bL], [self.bps[bk]])
                        self.mm(self.ps[bk][:, 128:256], KT[:, kcols(r, bi)], qT[:, qsl], True, True, [bL], [self.bps[bk]])
                        lo = 0 if has_prev else 128
                        pf, pm, bp = Pf[slot], Pm[slot], bP[slot]
                        self.act(pf[:, lo:256], self.ps[bk][:, lo:256], AF.Exp, [self.bps[bk]], [bp])
                        self.tt(pm[:, lo:256], pf[:, lo:256], mflat[:, lo:256], ALU.mult,
                                [bmk], [bp], eng=("dve" if slot % 2 else "pool"))

                    def st2(r, bi, slot, g=g, d=d, q0=q0, vidx=vidx):
                        qa = r + d * 128 * bi - q0
                        qsl = slice(qa, qa + 127 * d + 1, d)
                        has_prev = bi >= 1
                        pm, bp = Pm[slot], bP[slot]
                        bo = self.bank()
                        if has_prev:
                            self.mm(self.ps[bo][:, 0:128], Vtok[:, vidx[(r, bi - 1)], :], pm[:, 0:128], True, False, [bp, bV], [self.bps[bo]])
                        self.mm(self.ps[bo][:, 0:128], Vtok[:, vidx[(r, bi)], :], pm[:, 128:256], not has_prev, True, [bp, bV], [self.bps[bo]])
                        if has_prev:
                            self.mm(self.ps[bo][:, 128:256], self.onesb[:], pm[:, 0:128], True, False, [bp] + C_, [self.bps[bo]])
                        self.mm(self.ps[bo][:, 128:256], self.onesb[:], pm[:, 128:256], not has_prev, True, [bp] + C_, [self.bps[bo]])
                        src = self.ps[bo][:, 0:256].rearrange("p (a b) -> p a b", b=128)
                        dst = accb[:, :, qsl]
                        if g == 0:
                            self.cp(dst, src, [self.bps[bo]], [bacc])
                        else:
                            self.tt(dst, src, dst, ALU.add, [self.bps[bo], bacc], [bacc])

                    LA = 2
                    for i in range(min(LA, len(qblocks))):
                        st1(qblocks[i][0], qblocks[i][1], (pcount + i) % 4)
                    for i, (r, bi) in enumerate(qblocks):
                        if i + LA < len(qblocks):
                            st1(qblocks[i + LA][0], qblocks[i + LA][1], (pcount + i + LA) % 4)
                        st2(r, bi, (pcount + i) % 4)
                    pcount += len(qblocks)
                self.S.op("dve", lambda e: e.reciprocal(out=rden, in_=accb[:, 1, :]), [bacc], [bacc])
                self.tt(oTb, accb[:, 0, :], rden, ALU.mult, [bacc], [bacc])
                self.ld(self.OTs[hh, :, q0:q0 + QS], oTb, [bacc], [], acc=[self.b_O])
        self.S.barrier()

    def final_block(self, blk):
        xf = self.carve(16384, [128, KC, TB], F32)
        yt = self.carve(49152, [128, 4, D], F32)
        bxf, byt = Buf(), Buf()
        self.rms(14, out=xf, ob=bxf)
        for c in range(4):
            for k in range(KC):
                bk = self.bank()
                self.tr(self.ps[bk][:, 0:128], xf[:, k, c * 128:(c + 1) * 128], self.ident, [bxf, self.b_c], [self.bps[bk]])
                if k % 2 == 0:
                    self.cp(yt[:, c, k * 128:(k + 1) * 128], self.ps[bk][:, 0:128], [self.bps[bk]], [], acc=[byt])
                else:
                    self.act(yt[:, c, k * 128:(k + 1) * 128], self.ps[bk][:, 0:128], AF.Copy, [self.bps[bk]], [], acc=[byt])
        t = self.ld(self.y[blk * TB:(blk + 1) * TB, :].rearrange("(c p) d -> p c d", p=128), yt, [byt], [], acc=[self.b_y])
        self.out_toks.append(t)
        self.S.barrier()

    def build(self):
        self.out_toks = []
        self.setup()
        self.prepass()
        for stage in self.plan:
            kind = stage[0]
            if kind == "ssd":
                li = stage[1]
                self.mem_kv(li)
                for blk in range(self.NB):
                    self.load_h(blk)
                    self.ssd_block(li, blk)
                    if "nomem" not in stage:
                        self.mem_attn(li)
                    if "noffn" not in stage:
                        self.ffn(li)
                    self.store_h(blk)
            elif kind == "kv":
                for blk in range(self.NB):
                    self.load_h(blk)
                    self.kv_block(blk)
            elif kind == "dil":
                li = stage[1]
                self.mem_kv(li)
                for blk in range(self.NB):
                    self.load_h(blk)
                    self.q_block(li, blk)
                self.dil_attn()
                for blk in range(self.NB):
                    self.load_h(blk)
                    self.ld(self.xn[:], self.OTs[:, :, blk * TB:(blk + 1) * TB].rearrange("k p t -> p k t"),
                            [self.b_O], [self.b_xn])
                    self.linear_fm(self.dil_w_o[li - 2], 0, D, KC, lambda k: self.xn[:, k, :], [self.b_xn], self.add_to_h)
                    self.S.barrier()
                    self.mem_attn(li)
                    self.ffn(li)
                    self.store_h(blk)
            elif kind == "ffn":
                li = stage[1]
                for blk in range(self.NB):
                    self.load_h(blk)
                    self.ffn(li)
                    self.store_h(blk)
            elif kind == "mem":
                li = stage[1]
                self.mem_kv(li)
                for blk in range(self.NB):
                    self.load_h(blk)
                    self.mem_attn(li)
                    self.store_h(blk)
        for blk in range(self.NB):
            self.load_h(blk)
            self.final_block(blk)
        self.S.wait_all("sp", self.out_toks)


FULL_PLAN = [("ssd", 0), ("ssd", 1), ("kv",), ("dil", 2), ("dil", 3)]


def host_params(inp):
    f = np.float32
    fm = lambda v: np.ascontiguousarray(np.asarray(v, f).reshape(KC, 128).T)
    gl = [fm(inp["norm_mix"][i]) for i in range(4)] + [fm(inp["norm_mem"][i]) for i in range(4)] + \
         [fm(inp["norm_ffn"][i]) for i in range(4)] + [fm(inp["kv_norm"]), fm(inp["mem_src_norm"]), fm(inp["norm_final"])]
    gains = np.ascontiguousarray(np.stack(gl, axis=1))
    cw = np.asarray(inp["ssd_conv_w"], f)
    convw = np.ascontiguousarray(cw.reshape(2, 4, 48, 128).transpose(3, 0, 2, 1))
    convb = np.ascontiguousarray(np.asarray(inp["ssd_conv_b"], f).reshape(2, 48, 128).transpose(2, 0, 1))
    rowp = np.ascontiguousarray(np.stack([inp["ssd_dt_bias"], inp["ssd_a_log"], inp["ssd_d"]], axis=1).astype(f))
    ssdnw = np.ascontiguousarray(np.asarray(inp["ssd_norm"], f).reshape(2, 32, 128).transpose(2, 0, 1))
    i = np.arange(128)
    s, t = i[:, None], i[None, :]
    cst = np.stack([np.eye(128), (s <= t), (s > t), (s == 127) * np.ones((128, 128)),
                    (s == (t + 64) % 128), (s >= t)], axis=1).astype(f)
    half = 64
    invf = (10000.0 ** (-np.arange(half, dtype=np.float32) / half)).astype(f)
    invf2 = np.stack([np.concatenate([invf, invf]), np.concatenate([-np.ones(half, f), np.ones(half, f)])], axis=1)
    return dict(gains=gains, convw=convw, convb=convb, rowp=rowp, ssdnw=ssdnw, cst=np.ascontiguousarray(cst),
                invf=np.ascontiguousarray(invf2.astype(f)))


WNAMES = {"ssd_w_in": "ssd_w_in", "ssd_w_out": "ssd_w_out", "w_kv_shared": "w_kv_shared", "dil_w_q": "dil_w_q",
          "dil_w_o": "dil_w_o", "mem_w_q": "mem_w_q", "mem_w_kv": "mem_w_kv", "mem_w_o": "mem_w_o",
          "ffn_w_in": "ffn_w_in", "ffn_w_out": "ffn_w_out"}

_CACHE = {}


def run(inp, T, plan, batches, n_cores):
    key = (T, tuple(plan))
    if key not in _CACHE:
        _CACHE[key] = Prog(T, plan)
    prog = _CACHE[key]
    hp = host_params(inp)
    ws = {n: np.ascontiguousarray(np.asarray(inp[n], np.float32)) for n in WNAMES}
    maps = []
    for c in range(n_cores):
        b = batches[c]
        m = dict(hp)
        m["x"] = np.ascontiguousarray(np.asarray(inp["x"][b, :T], np.float32))
        m["mem"] = np.ascontiguousarray(np.asarray(inp["mem"][b], np.float32))
        m["pos"] = np.ascontiguousarray(np.asarray(inp["positions"][b, :T], np.int32))
        m.update(ws)
        maps.append(m)
    res = run_bass_kernel_spmd(prog.nc, maps, core_ids=list(range(n_cores)))
    return [r["y"] for r in res.results]


def kernel(**inputs):
    T = 8192
    ys = run(inputs, T, FULL_PLAN, [0, 1], 2)
    return np.stack([ys[0], ys[1]], axis=0).astype(np.float32)
```
